# Optimizing a Trainium2 kernel written in Bass

```python
import jax
import jax.numpy as jnp
from jax import lax
import numpy as np

D_MODEL = 1024
BATCH = 16
SEQ = 256
DEPTH = 4
DEC_BATCH = 8
DEC_SEQ = 1024
PAST_LEN = 512

GRID_W = 64
Q_BLOCK = 128
ROPE_THETA = 10000.0
NORM_EPS = 1e-6
NEG_INF = -1e30
N_MIXERS = 4
N_MOD = 6
D_FF = 4 * D_MODEL

ATTN_HEADS = 8
ATTN_KV_HEADS = 2
ATTN_HEAD_DIM = D_MODEL // ATTN_HEADS
MLA_HEADS = 16
MLA_Q_LORA = 384
MLA_KV_LORA = 256
MLA_NOPE = 64
MLA_ROPE = 32
MLA_V_DIM = 64
MLA_SCALE = (MLA_NOPE + MLA_ROPE) ** -0.5
SWA_HEADS = 16
SWA_KV_HEADS = 4
SWA_HEAD_DIM = D_MODEL // SWA_HEADS
SWA_WINDOW = 128
NAT_HEADS = 16
NAT_HEAD_DIM = D_MODEL // NAT_HEADS
NAT_WIN_R = 8
NAT_WIN_C = 16

kernel_name = 'hybrid_dit_prefix_context_step'


def rmsnorm(x, g):
    xf = x.astype(jnp.float32)
    y = xf * lax.rsqrt(jnp.mean(xf * xf, axis=-1, keepdims=True) + NORM_EPS)
    return (y * g.astype(jnp.float32)).astype(x.dtype)


def axial_rope(x):
    S, dim = x.shape[1], x.shape[-1]
    quarter = dim // 4
    t = jnp.arange(S)
    pos = jnp.stack([t // GRID_W, t % GRID_W], axis=-1).astype(jnp.float32)
    inv = ROPE_THETA ** (-jnp.arange(quarter, dtype=jnp.float32) / quarter)
    ang = pos[:, :, None] * inv
    bshape = (S,) + (1,) * (x.ndim - 3) + (2, quarter)
    cos = jnp.cos(ang).reshape(bshape).astype(x.dtype)
    sin = jnp.sin(ang).reshape(bshape).astype(x.dtype)
    xr = x.reshape(x.shape[:-1] + (2, 2, quarter))
    x1, x2 = xr[..., 0, :], xr[..., 1, :]
    out = jnp.stack([x1 * cos - x2 * sin, x2 * cos + x1 * sin], axis=-2)
    return out.reshape(x.shape)


def attend(q, segments, scale, sink=None):
    logits = []
    for k, _, bias in segments:
        eq = 'bqhgd,bqkhd->bhgqk' if k.ndim == 5 else 'bqhgd,bkhd->bhgqk'
        s = jnp.einsum(eq, q, k, preferred_element_type=jnp.float32) * scale
        if bias is not None:
            s = s + bias
        logits.append(s)
    sizes = [s.shape[-1] for s in logits]
    if sink is not None:
        B, Q = q.shape[:2]
        logits.append(jnp.broadcast_to(sink.astype(jnp.float32)[None, :, :, None, None],
                                       (B,) + sink.shape + (Q, 1)))
    p = jax.nn.softmax(jnp.concatenate(logits, axis=-1), axis=-1)
    out = None
    off = 0
    for (_, v, _), n in zip(segments, sizes):
        pv = p[..., off:off + n].astype(v.dtype)
        eq = 'bhgqk,bqkhd->bqhgd' if v.ndim == 5 else 'bhgqk,bkhd->bqhgd'
        o = jnp.einsum(eq, pv, v)
        out = o if out is None else out + o
        off += n
    return out


def map_query_blocks(fn, q, *block_inputs):
    B, S = q.shape[:2]
    nb = S // Q_BLOCK
    qb = jnp.moveaxis(q.reshape((B, nb, Q_BLOCK) + q.shape[2:]), 1, 0)
    out = lax.map(lambda xs: fn(*xs), (jnp.arange(nb), qb) + tuple(block_inputs))
    return jnp.moveaxis(out, 0, 1).reshape((B, S) + out.shape[3:])


def gqa_split(z, n_heads, n_kv, hd):
    B, T = z.shape[:2]
    q, k, v = jnp.split(z, [n_heads * hd, (n_heads + n_kv) * hd], axis=-1)
    return (q.reshape(B, T, n_kv, n_heads // n_kv, hd),
            k.reshape(B, T, n_kv, hd), v.reshape(B, T, n_kv, hd))


def attn_context(h, w_qkv, q_norm, k_norm, w_o):
    B, L = h.shape[:2]
    q, k, v = gqa_split(h @ w_qkv, ATTN_HEADS, ATTN_KV_HEADS, ATTN_HEAD_DIM)
    q, k = rmsnorm(q, q_norm), rmsnorm(k, k_norm)
    o = attend(q, [(k, v, None)], ATTN_HEAD_DIM ** -0.5)
    return o.reshape(B, L, -1) @ w_o, (k, v)


def attn_latent(h, k_ctx, v_ctx, w_qkv, q_norm, k_norm, w_o):
    B, S = h.shape[:2]
    q, k, v = gqa_split(h @ w_qkv, ATTN_HEADS, ATTN_KV_HEADS, ATTN_HEAD_DIM)
    q, k = axial_rope(rmsnorm(q, q_norm)), axial_rope(rmsnorm(k, k_norm))

    def block(b, qb):
        return attend(qb, [(k, v, None), (k_ctx, v_ctx, None)], ATTN_HEAD_DIM ** -0.5)

    o = map_query_blocks(block, q)
    return o.reshape(B, S, -1) @ w_o


def mla_project(h, w_in, q_norm, kv_norm, w_uq):
    B, T = h.shape[:2]
    cq, ckv, k_pe = jnp.split(h @ w_in, [MLA_Q_LORA, MLA_Q_LORA + MLA_KV_LORA], axis=-1)
    q = (rmsnorm(cq, q_norm) @ w_uq).reshape(B, T, MLA_HEADS, 1, MLA_NOPE + MLA_ROPE)
    return q, rmsnorm(ckv, kv_norm), k_pe


def mla_expand(ckv, k_pe, w_ukv):
    B, T = ckv.shape[:2]
    kv = (ckv @ w_ukv).reshape(B, T, MLA_HEADS, MLA_NOPE + MLA_V_DIM)
    k_nope, v = jnp.split(kv, [MLA_NOPE], axis=-1)
    k_rope = jnp.broadcast_to(k_pe[:, :, None, :], (B, T, MLA_HEADS, MLA_ROPE))
    return jnp.concatenate([k_nope, k_rope], axis=-1), v


def mla_context(h, w_in, q_norm, kv_norm, w_uq, w_ukv, w_o):
    B, L = h.shape[:2]
    q, ckv, k_pe = mla_project(h, w_in, q_norm, kv_norm, w_uq)
    k, v = mla_expand(ckv, k_pe, w_ukv)
    o = attend(q, [(k, v, None)], MLA_SCALE)
    return o.reshape(B, L, -1) @ w_o, (ckv, k_pe)


def mla_latent(h, ckv_ctx, kpe_ctx, w_in, q_norm, kv_norm, w_uq, w_ukv, w_o):
    B, S = h.shape[:2]
    q, ckv, k_pe = mla_project(h, w_in, q_norm, kv_norm, w_uq)
    q = jnp.concatenate([q[..., :MLA_NOPE], axial_rope(q[..., MLA_NOPE:])], axis=-1)
    k, v = mla_expand(ckv, axial_rope(k_pe[:, :, None, :])[:, :, 0, :], w_ukv)
    k_ctx, v_ctx = mla_expand(ckv_ctx, kpe_ctx, w_ukv)

    def block(b, qb):
        return attend(qb, [(k, v, None), (k_ctx, v_ctx, None)], MLA_SCALE)

    o = map_query_blocks(block, q)
    return o.reshape(B, S, -1) @ w_o


def swa_context(h, w_qkv, sink, w_o):
    B, L = h.shape[:2]
    q, k, v = gqa_split(h @ w_qkv, SWA_HEADS, SWA_KV_HEADS, SWA_HEAD_DIM)
    o = attend(q, [(k, v, None)], SWA_HEAD_DIM ** -0.5, sink.reshape(SWA_KV_HEADS, -1))
    return o.reshape(B, L, -1) @ w_o, (k, v)


def swa_latent(h, k_ctx, v_ctx, w_qkv, sink, w_o):
    B, S = h.shape[:2]
    q, k, v = gqa_split(h @ w_qkv, SWA_HEADS, SWA_KV_HEADS, SWA_HEAD_DIM)
    q, k = axial_rope(q), axial_rope(k)
    pad = ((0, 0), (SWA_WINDOW, SWA_WINDOW), (0, 0), (0, 0))
    kp, vp = jnp.pad(k, pad), jnp.pad(v, pad)
    band = Q_BLOCK + 2 * SWA_WINDOW
    sink_g = sink.reshape(SWA_KV_HEADS, -1)

    def block(b, qb):
        start = b * Q_BLOCK
        kb = lax.dynamic_slice_in_dim(kp, start, band, axis=1)
        vb = lax.dynamic_slice_in_dim(vp, start, band, axis=1)
        qpos = start + jnp.arange(Q_BLOCK)
        kpos = start - SWA_WINDOW + jnp.arange(band)
        valid = (kpos[None, :] >= 0) & (kpos[None, :] < S) & (jnp.abs(qpos[:, None] - kpos[None, :]) <= SWA_WINDOW)
        bias = jnp.where(valid, 0.0, NEG_INF).astype(jnp.float32)
        return attend(qb, [(kb, vb, bias), (k_ctx, v_ctx, None)], SWA_HEAD_DIM ** -0.5, sink_g)

    o = map_query_blocks(block, q)
    return o.reshape(B, S, -1) @ w_o


def neighbourhood_tables(S, rpb):
    rows = S // GRID_W
    wr = min(NAT_WIN_R, rows)
    t = jnp.arange(S)
    r, c = t // GRID_W, t % GRID_W
    r0 = jnp.clip(r - wr // 2, 0, rows - wr)
    c0 = jnp.clip(c - NAT_WIN_C // 2, 0, GRID_W - NAT_WIN_C)
    kr = jnp.broadcast_to(r0[:, None, None] + jnp.arange(wr)[None, :, None], (S, wr, NAT_WIN_C)).reshape(S, -1)
    kc = jnp.broadcast_to(c0[:, None, None] + jnp.arange(NAT_WIN_C)[None, None, :], (S, wr, NAT_WIN_C)).reshape(S, -1)
    idx = kr * GRID_W + kc
    bias = rpb[:, kr - r[:, None] + NAT_WIN_R - 1, kc - c[:, None] + NAT_WIN_C - 1]
    return idx, bias


def nat_context(h, w_qkv, w_o):
    B, L = h.shape[:2]
    q, k, v = gqa_split(h @ w_qkv, NAT_HEADS, NAT_HEADS, NAT_HEAD_DIM)
    o = attend(q, [(k, v, None)], NAT_HEAD_DIM ** -0.5)
    return o.reshape(B, L, -1) @ w_o, (k, v)


def nat_latent(h, k_ctx, v_ctx, w_qkv, rpb, w_o):
    B, S = h.shape[:2]
    q, k, v = gqa_split(h @ w_qkv, NAT_HEADS, NAT_HEADS, NAT_HEAD_DIM)
    idx, bias = neighbourhood_tables(S, rpb)
    nb = S // Q_BLOCK
    idx_blocks = idx.reshape(nb, Q_BLOCK, -1)
    bias_blocks = jnp.moveaxis(bias.reshape(NAT_HEADS, nb, Q_BLOCK, -1), 1, 0)

    def block(b, qb, idx_b, bias_b):
        kg = jnp.take(k, idx_b, axis=1)
        vg = jnp.take(v, idx_b, axis=1)
        return attend(qb, [(kg, vg, bias_b[None, :, None].astype(jnp.float32)), (k_ctx, v_ctx, None)],
                      NAT_HEAD_DIM ** -0.5)

    o = map_query_blocks(block, q, idx_blocks, bias_blocks)
    return o.reshape(B, S, -1) @ w_o


def sandwich_layer(x, cond, ada_w, ada_b, norm_g, w1, w2, mix):
    m = jax.nn.silu(cond) @ ada_w + ada_b
    shift1, scale1, gate1, shift2, scale2, gate2 = jnp.split(m[:, None, :], N_MOD, axis=-1)
    out, extra = mix(rmsnorm(x, norm_g[0]) * (1 + scale1) + shift1)
    x = x + gate1 * rmsnorm(out, norm_g[1])
    h = rmsnorm(x, norm_g[2]) * (1 + scale2) + shift2
    ff = jnp.square(jax.nn.relu(h @ w1)) @ w2
    x = x + gate2 * rmsnorm(ff, norm_g[3])
    return x, extra


def setup_inputs(seed: int = 0) -> dict:
    key = jax.random.key(seed)
    ks = iter(jax.random.split(key, 40))

    def nrm(shape, scale=1.0):
        return scale * jax.random.normal(next(ks), shape, jnp.float32)

    D = D_MODEL
    return {
        'x_prompt': nrm((BATCH, SEQ, D)),
        'x_sample': nrm((DEC_BATCH, DEC_SEQ, D)),
        'cache_l0_k': nrm((DEC_BATCH, PAST_LEN, ATTN_KV_HEADS, ATTN_HEAD_DIM)),
        'cache_l0_v': nrm((DEC_BATCH, PAST_LEN, ATTN_KV_HEADS, ATTN_HEAD_DIM)),
        'cache_l1_ckv': nrm((DEC_BATCH, PAST_LEN, MLA_KV_LORA)),
        'cache_l1_kpe': nrm((DEC_BATCH, PAST_LEN, MLA_ROPE)),
        'cache_l2_k': nrm((DEC_BATCH, PAST_LEN, SWA_KV_HEADS, SWA_HEAD_DIM)),
        'cache_l2_v': nrm((DEC_BATCH, PAST_LEN, SWA_KV_HEADS, SWA_HEAD_DIM)),
        'cache_l3_k': nrm((DEC_BATCH, PAST_LEN, NAT_HEADS, NAT_HEAD_DIM)),
        'cache_l3_v': nrm((DEC_BATCH, PAST_LEN, NAT_HEADS, NAT_HEAD_DIM)),
        'c': nrm((DEC_BATCH, D)),
        'c_ctx': nrm((D,)),
        'ada_w': nrm((DEPTH, D, N_MOD * D), 0.5 * D ** -0.5),
        'ada_b': nrm((DEPTH, N_MOD * D), 0.01),
        'norm_g': 1.0 + nrm((DEPTH, 4, D), 0.05),
        'mlp_w1': nrm((DEPTH, D, D_FF), D ** -0.5),
        'mlp_w2': nrm((DEPTH, D_FF, D), D_FF ** -0.5),
        'attn_w_qkv': nrm((D, (ATTN_HEADS + 2 * ATTN_KV_HEADS) * ATTN_HEAD_DIM), D ** -0.5),
        'attn_q_norm': 1.0 + nrm((ATTN_HEAD_DIM,), 0.05),
        'attn_k_norm': 1.0 + nrm((ATTN_HEAD_DIM,), 0.05),
        'attn_w_o': nrm((ATTN_HEADS * ATTN_HEAD_DIM, D), (ATTN_HEADS * ATTN_HEAD_DIM) ** -0.5),
        'mla_w_in': nrm((D, MLA_Q_LORA + MLA_KV_LORA + MLA_ROPE), D ** -0.5),
        'mla_q_norm': 1.0 + nrm((MLA_Q_LORA,), 0.05),
        'mla_kv_norm': 1.0 + nrm((MLA_KV_LORA,), 0.05),
        'mla_w_uq': nrm((MLA_Q_LORA, MLA_HEADS * (MLA_NOPE + MLA_ROPE)), MLA_Q_LORA ** -0.5),
        'mla_w_ukv': nrm((MLA_KV_LORA, MLA_HEADS * (MLA_NOPE + MLA_V_DIM)), MLA_KV_LORA ** -0.5),
        'mla_w_o': nrm((MLA_HEADS * MLA_V_DIM, D), (MLA_HEADS * MLA_V_DIM) ** -0.5),
        'swa_w_qkv': nrm((D, (SWA_HEADS + 2 * SWA_KV_HEADS) * SWA_HEAD_DIM), D ** -0.5),
        'swa_sink': nrm((SWA_HEADS,), 0.5),
        'swa_w_o': nrm((SWA_HEADS * SWA_HEAD_DIM, D), (SWA_HEADS * SWA_HEAD_DIM) ** -0.5),
        'nat_w_qkv': nrm((D, 3 * NAT_HEADS * NAT_HEAD_DIM), D ** -0.5),
        'nat_rpb': nrm((NAT_HEADS, 2 * NAT_WIN_R - 1, 2 * NAT_WIN_C - 1), 0.1),
        'nat_w_o': nrm((NAT_HEADS * NAT_HEAD_DIM, D), (NAT_HEADS * NAT_HEAD_DIM) ** -0.5),
    }


def reference(x_prompt, x_sample, cache_l0_k, cache_l0_v, cache_l1_ckv, cache_l1_kpe,
              cache_l2_k, cache_l2_v, cache_l3_k, cache_l3_v, c, c_ctx,
              ada_w, ada_b, norm_g, mlp_w1, mlp_w2,
              attn_w_qkv, attn_q_norm, attn_k_norm, attn_w_o,
              mla_w_in, mla_q_norm, mla_kv_norm, mla_w_uq, mla_w_ukv, mla_w_o,
              swa_w_qkv, swa_sink, swa_w_o,
              nat_w_qkv, nat_rpb, nat_w_o):
    context_mixers = [
        lambda h: attn_context(h, attn_w_qkv, attn_q_norm, attn_k_norm, attn_w_o),
        lambda h: mla_context(h, mla_w_in, mla_q_norm, mla_kv_norm, mla_w_uq, mla_w_ukv, mla_w_o),
        lambda h: swa_context(h, swa_w_qkv, swa_sink, swa_w_o),
        lambda h: nat_context(h, nat_w_qkv, nat_w_o),
    ]
    latent_mixers = [
        lambda h, st: attn_latent(h, st[0], st[1], attn_w_qkv, attn_q_norm, attn_k_norm, attn_w_o),
        lambda h, st: mla_latent(h, st[0], st[1], mla_w_in, mla_q_norm, mla_kv_norm, mla_w_uq, mla_w_ukv, mla_w_o),
        lambda h, st: swa_latent(h, st[0], st[1], swa_w_qkv, swa_sink, swa_w_o),
        lambda h, st: nat_latent(h, st[0], st[1], nat_w_qkv, nat_rpb, nat_w_o),
    ]
    cached = [(cache_l0_k, cache_l0_v), (cache_l1_ckv, cache_l1_kpe),
              (cache_l2_k, cache_l2_v), (cache_l3_k, cache_l3_v)]
    xp, xs = x_prompt, x_sample
    new_state = []
    for i in range(DEPTH):
        kind = i % N_MIXERS
        layer_w = (ada_w[i], ada_b[i], norm_g[i], mlp_w1[i], mlp_w2[i])
        xp, st = sandwich_layer(xp, c_ctx[None, :], *layer_w, context_mixers[kind])
        new_state.append(st)
        xs, _ = sandwich_layer(xs, c, *layer_w, lambda h: (latent_mixers[kind](h, cached[i]), None))
    (l0_k, l0_v), (l1_ckv, l1_kpe), (l2_k, l2_v), (l3_k, l3_v) = new_state
    return (xp, xs, l0_k, l0_v, l1_ckv, l1_kpe, l2_k, l2_v, l3_k, l3_v)
```

```python
import numpy as np
import concourse.bass as bass
import concourse.mybir as mybir
from concourse.bass_utils import run_bass_kernel_spmd
from contextlib import ExitStack

F32 = mybir.dt.float32
BF16 = mybir.dt.bfloat16
AF = mybir.ActivationFunctionType
ALU = mybir.AluOpType
ENGS = ("pe", "act", "dve", "pool", "sp")
NL = 4
EPS = 1e-6
NEG = -30000.0


class Buf:
    __slots__ = ("name", "w", "r", "psum")

    def __init__(self, name):
        self.name = name
        self.w = None
        self.r = []
        self.psum = len(name) == 3 and name.startswith("ps") and name[2].isdigit()


class Op:
    __slots__ = ("eng", "fn", "waits", "marked", "key", "val", "clock", "isdma", "count", "ndma")


class Prog:
    def __init__(self):
        self.eng_ops = {e: [] for e in ENGS}
        self.eclk = {e: {} for e in ENGS}
        self.dmaval = {}
        self.dmalast = {}
        self.fence_id = 0
        self.fence_deps = []
        self.fence_gen = {e: 0 for e in ENGS}

    def fence(self):
        deps = []
        for e in ENGS:
            for op in reversed(self.eng_ops[e]):
                if not op.isdma:
                    deps.append(op)
                    break
        deps.extend(self.dmalast.values())
        self.fence_id += 1
        self.fence_deps = deps

    def add(self, eng, fn, reads=(), writes=(), dma=None, ndma=1):
        op = Op()
        op.eng = eng
        op.fn = fn
        op.marked = False
        op.isdma = dma is not None
        op.ndma = ndma
        deps = []
        if self.fence_gen[eng] < self.fence_id:
            self.fence_gen[eng] = self.fence_id
            for d in self.fence_deps:
                deps.append((d, 0))
        for b in reads:
            if b.w is not None:
                deps.append((b.w, 0))
            if b.psum:
                for r in b.r:
                    if r.eng != eng:
                        deps.append((r, 3))
        for b in writes:
            if b.w is not None:
                deps.append((b.w, 1))
            for r in b.r:
                deps.append((r, 2))
        clk = self.eclk[eng]
        waits = []
        for d, kind in deps:
            if (not d.isdma) and d.eng == eng:
                if eng == "pe" or kind == 2:
                    continue
            if clk.get(d.key, 0) >= d.val:
                continue
            waits.append(d)
            d.marked = True
            for k, v in d.clock.items():
                if clk.get(k, 0) < v:
                    clk[k] = v
        op.waits = waits
        if dma is not None:
            op.key = ("d", dma.name)
            op.val = self.dmaval.get(op.key, 0) + ndma
            self.dmaval[op.key] = op.val
            self.dmalast[op.key] = op
            op.marked = True
        else:
            op.key = eng
            op.val = len(self.eng_ops[eng]) + 1
        c = dict(clk)
        c[op.key] = op.val
        op.clock = c
        self.eng_ops[eng].append(op)
        for b in reads:
            b.r.append(op)
        for b in writes:
            b.w = op
            b.r = []
        return op

    def emit(self, nc, es, final_keys):
        sems = {}

        def sem_of(key):
            if key not in sems:
                nm = "s%d" % len(sems)
                sems[key] = es.enter_context(nc.semaphore(nm))
            return sems[key]

        for e in ENGS:
            cnt = 0
            for op in self.eng_ops[e]:
                if op.isdma:
                    op.count = 16 * op.val
                elif op.marked:
                    cnt += 1
                    op.count = cnt
        block = es.enter_context(nc.Block())
        handles = {"pe": block.tensor, "act": block.scalar, "dve": block.vector, "pool": block.gpsimd, "sp": block.sync}
        prog = self

        def make(ename):
            def body(e):
                for op in prog.eng_ops[ename]:
                    for d in op.waits:
                        e.wait_ge(sem_of(d.key), d.count)
                    r = op.fn(e)
                    if op.isdma:
                        s = sem_of(op.key)
                        for ins in r:
                            ins.then_inc(s, 16)
                    elif op.marked:
                        r.then_inc(sem_of(op.key), 1)
                if ename == "sp":
                    for key in final_keys:
                        e.wait_ge(sem_of(key), 16 * prog.dmaval[key])
            return body

        for ename in ENGS:
            handles[ename](make(ename))
        return len(sems)


def _rope_tables(hd_rot, rows, row0):
    S, GW = 1024, 64
    q = hd_rot // 4
    t = np.arange(S)
    pos = np.stack([t // GW, t % GW], axis=-1).astype(np.float32)
    inv = (np.float32(10000.0) ** (-np.arange(q, dtype=np.float32) / np.float32(q))).astype(np.float32)
    ang = (pos[:, :, None] * inv).astype(np.float32)
    cos = np.ones((rows, S), np.float32)
    sin = np.zeros((rows, S), np.float32)
    sp = np.zeros((rows, rows), np.float32)
    for a in range(2):
        for j in range(2):
            for i in range(q):
                d = row0 + a * 2 * q + j * q + i
                cos[d] = np.cos(ang[:, a, i])
                sin[d] = np.sin(ang[:, a, i])
                if j == 0:
                    sp[d + q, d] = -1.0
                else:
                    sp[d - q, d] = 1.0
    return cos, sin, sp


def _consts():
    c = {}
    c["ident"] = np.eye(128, dtype=np.float32)
    ca, sa, pa = _rope_tables(128, 128, 0)
    cb, sb_, pb = _rope_tables(32, 96, 64)
    cc, sc, pc = _rope_tables(64, 64, 0)
    tab = np.zeros((3, 2, 128, 1024), np.float32)
    spm = np.zeros((3, 128, 128), np.float32)
    tab[0, 0], tab[0, 1], spm[0] = ca, sa, pa
    tab[1, 0, :96], tab[1, 1, :96], spm[1, :96, :96] = cb, sb_, pb
    tab[2, 0, :64], tab[2, 1, :64], spm[2, :64, :64] = cc, sc, pc
    c["ropetab"] = tab
    c["ropesp"] = spm
    k = np.arange(128)[:, None]
    q = np.arange(128)[None, :]
    bm = np.zeros((2, 128, 128), np.float32)
    bm[0] = np.where(q <= k, 0.0, NEG)
    bm[1] = np.where(k <= q, 0.0, NEG)
    c["bandmask"] = bm
    sh = np.zeros((128, 64, 64), np.float32)
    for cc_ in range(64):
        for kc_ in range(64):
            i_ = kc_ - cc_ + 47
            if 0 <= i_ < 128:
                sh[i_, cc_, kc_] = 1.0
    sh = sh.reshape(128, 4096)
    c["natsh"] = sh
    cq = np.arange(64)
    c0 = np.clip(cq - 8, 0, 48)
    kc = np.arange(64)[:, None]
    m = np.where((kc >= c0[None, :]) & (kc < c0[None, :] + 16), 0.0, NEG).astype(np.float32)
    c["natmask"] = np.concatenate([m, m], 0)
    sel = np.zeros((128, 96), np.float32)
    for i in range(32):
        sel[i, 64 + i] = 1.0
    c["mlasel"] = sel
    return c


class KB:
    def __init__(self, nl=NL, dbg=None):
        self.nl = nl
        self.dbg = dbg
        self.nc = bass.Bass("TRN2", target_bir_lowering=False)
        self.P = Prog()
        self.es = ExitStack()
        self.bufs = {}
        self.outkeys = []
        self._rr = {}

    def B(self, name):
        b = self.bufs.get(name)
        if b is None:
            b = self.bufs[name] = Buf(name)
        return b

    def dram(self, name, shape, out=False):
        return self.nc.dram_tensor(name, list(shape), F32, kind="ExternalOutput" if out else "ExternalInput").ap()

    def sb(self, name, shape, dt=F32):
        return self.es.enter_context(self.nc.sbuf_tensor(name, list(shape), dt))

    def rot(self, name, n):
        i = self._rr.get(name, 0)
        self._rr[name] = (i + 1) % n
        return i

    def bank(self):
        i = self.rot("psum", 6)
        return self.ps[i], self.B("ps%d" % i)

    def qkbank(self):
        i = self.rot("qkb", 4)
        return self.ps[i], self.B("ps%d" % i)

    def tmp(self):
        i = self.rot("tmp", 2)
        return self.tmp32[i], self.B("tmp%d" % i)

    def evac_eng(self):
        return ("act", "dve")[self.rot("evac", 2)]

    def dma_in(self, dst_ap, dst_buf, src_aps, eng="pool", reads=()):
        def fn(e, pairs=src_aps):
            return [e.dma_start(out=o, in_=i) for (o, i) in pairs]
        self.P.add(eng, fn, reads=list(reads), writes=[dst_buf], dma=dst_buf, ndma=len(src_aps))

    def dma_out(self, pairs, src_buf, key):
        def fn(e, pairs=pairs):
            return [e.dma_start(out=o, in_=i) for (o, i) in pairs]
        kb = self.B("out_" + key)
        self.P.add("sp", fn, reads=[src_buf], writes=[], dma=src_buf, ndma=len(pairs))
        k = ("d", src_buf.name)
        if k not in self.outkeys:
            self.outkeys.append(k)

    def wslot(self):
        i = self.rot("wsl", 2)
        return self.wsl[i][:], self.B("wsl%d" % i)

    def copy(self, eng, out, in_, reads, writes):
        if eng == "act":
            self.P.add("act", lambda e: e.activation(out=out, in_=in_, func=AF.Copy), reads=reads, writes=writes)
        else:
            self.P.add("dve", lambda e: e.tensor_copy(out=out, in_=in_), reads=reads, writes=writes)

    def build(self):
        nc, P = self.nc, self.P
        D = self.dram
        self.xp = D("xp", [512, 1024]); self.xs = D("xs", [1024, 1024])
        self.cache = [
            (D("c0k", [512, 256]), D("c0v", [512, 256])),
            (D("c1ckv", [512, 256]), D("c1kpe", [512, 32])),
            (D("c2k", [512, 256]), D("c2v", [512, 256])),
            (D("c3k", [512, 1024]), D("c3v", [512, 1024])),
        ]
        self.misc_in = D("misc", [128, 128])
        self.ada_w = D("ada_w", [4, 1024, 6144]); self.ada_b = D("ada_b", [192, 128]); self.norm_g = D("norm_g", [128, 128])
        self.w1 = D("mlp_w1", [4, 1024, 4096]); self.w2 = D("mlp_w2", [4, 4096, 1024])
        self.attn_w_qkv = D("attn_w_qkv", [1024, 1536]); self.attn_w_o = D("attn_w_o", [1024, 1024])
        self.mla_w_in = D("mla_w_in", [1024, 672]); self.mla_w_uq = D("mla_w_uq", [384, 1536])
        self.mla_w_ukv = D("mla_w_ukv", [256, 2048]); self.mla_w_o = D("mla_w_o", [1024, 1024])
        self.swa_w_qkv = D("swa_w_qkv", [1024, 1536]); self.swa_w_o = D("swa_w_o", [1024, 1024]); self.swa_sink = D("swa_sink", [1, 16])
        self.nat_w_qkv = D("nat_w_qkv", [1024, 3072]); self.nat_w_o = D("nat_w_o", [1024, 1024]); self.nat_rpb = D("nat_rpb", [240, 31])
        self.c_ident = D("ident", [128, 128]); self.c_ropetab = D("ropetab", [3, 2, 128, 1024]); self.c_ropesp = D("ropesp", [3, 128, 128])
        self.c_band = D("bandmask", [2, 128, 128]); self.c_natsh = D("natsh", [128, 4096]); self.c_natmask = D("natmask", [128, 64])
        self.c_mlasel = D("mlasel", [128, 96])
        self.yp = D("yp", [512, 1024], True); self.ys = D("ys", [1024, 1024], True)
        self.so = [
            (D("o0k", [512, 256], True), D("o0v", [512, 256], True)),
            (D("o1ckv", [512, 256], True), D("o1kpe", [512, 32], True)),
            (D("o2k", [512, 256], True), D("o2v", [512, 256], True)),
            (D("o3k", [512, 1024], True), D("o3v", [512, 1024], True)),
        ]
        with self.es:
            sb = self.sb
            self.xT = sb("xT", [128, 8, 1536])
            self.hT = sb("hT", [128, 8, 1536], BF16)
            self.R = sb("R", [128, 49152], BF16)
            self.wsl = [sb("wsl%d" % i, [128, 4096], BF16) for i in range(2)]
            self.sq = [sb("sq%d" % i, [128, 512], BF16) for i in range(2)]
            self.adabuf = sb("adabuf", [128, 2, 1024], BF16)
            self.tmp32 = [sb("tmp%d" % i, [128, 512]) for i in range(2)]
            self.rs = [sb("rs%d" % i, [128, 512]) for i in range(2)]
            self.rtab = sb("rtab", [128, 2, 1024], BF16)
            self.rsp = sb("rsp", [128, 128], BF16)
            self.identF = sb("identF", [128, 128]); self.identB = sb("identB", [128, 128], BF16); self.onesB = sb("onesB", [128, 128], BF16)
            self.gT = sb("gT", [128, 128]); self.adabT = sb("adabT", [128, 192]); self.miscT = sb("miscT", [128, 32])
            self.siluT = sb("siluT", [128, 8, 2], BF16)
            self.mod = sb("mod", [128, 48, 2]); self.drv = sb("drv", [128, 4, 8, 2])
            self.band = sb("band", [128, 2, 128], BF16); self.sinkE = sb("sinkE", [128, 16])
            self.natmask = sb("natmaskS", [128, 64]); self.mlasel = sb("mlaselS", [128, 96], BF16)
            self.ps = [self.es.enter_context(nc.psum_tensor("ps%d" % i, [128, 512], F32)) for i in range(8)]
            self.Rf = self.R[:].bitcast(F32)
            self.prologue()
            import os
            lys = os.environ.get("LAYERS")
            lyl = [int(c) for c in lys] if lys else list(range(self.nl))
            for i, l in enumerate(lyl):
                self.layer(l, lyl[i + 1] if i + 1 < len(lyl) else None)
            self.epilogue()
            if self.dbg:
                self.P.fence()
                dbg = self.dram("dbg", [128, 2048], True)
                db = self.B("dbgbuf")
                pairs = [(dbg[:, 0:96], self.mod[:].rearrange("p a b -> p (a b)")), (dbg[:, 96:160], self.drv[:].rearrange("p a b c -> p (a b c)")),
                         (dbg[:, 160:288], self.gT[:]), (dbg[:, 288:480], self.adabT[:]), (dbg[:, 480:512], self.miscT[:]),
                         (dbg[:, 512:1024], self.rs[0][:]), (dbg[:, 1024:1536], self.xT[:, 0, 0:512])]
                self.P.add("sp", lambda e: [e.dma_start(out=o, in_=i) for (o, i) in pairs], reads=[], writes=[db], dma=db, ndma=len(pairs))
                self.P.add("pool", lambda e: [e.dma_start(out=dbg[:, 1536:2048], in_=self.hT[:, 0, 0:512])], reads=[], writes=[db], dma=db, ndma=1)
                self.outkeys.append(("d", "dbgbuf"))
            nsem = P.emit(nc, self.es, self.outkeys)
            self.nsem = nsem
        return nc

    def stageF(self, i):
        off = 22528 + i * 1024
        return self.Rf[:, off:off + 1024], self.B("stage%d" % i)

    def prologue(self):
        P = self.P
        B = self.B
        self.dma_in(None, B("identF"), [(self.identF[:], self.c_ident[:, :])], eng="sp")
        self.dma_in(None, B("identB"), [(self.identB[:], self.c_ident[:, :])])
        self.dma_in(None, B("band"), [(self.band[:, i, :], self.c_band[i, :, :]) for i in range(2)])
        self.dma_in(None, B("natmask"), [(self.natmask[:], self.c_natmask[:, :])], eng="sp")
        self.dma_in(None, B("mlasel"), [(self.mlasel[:], self.c_mlasel[:, :])])
        P.add("dve", lambda e: e.memset(self.onesB[:], 1.0), writes=[B("onesB")])
        self.dma_in(None, B("sinkE"), [(self.sinkE[:], self.swa_sink[0:1, :].partition_broadcast(128))], eng="sp")
        P.add("act", lambda e: e.activation(out=self.sinkE[:], in_=self.sinkE[:], func=AF.Exp), reads=[B("sinkE")], writes=[B("sinkE")])
        st, stb = self.stageF(0)
        for (src, rows, dst, dcol, dname) in ((self.norm_g[:, :], 128, self.gT, 0, "gT"), (self.ada_b[0:128, :], 128, self.adabT, 0, "adabT"),
                                               (self.ada_b[128:192, :], 64, self.adabT, 128, "adabT"), (self.misc_in[0:32, :], 32, self.miscT, 0, "miscT")):
            self.dma_in(None, stb, [(st[0:rows, 0:128], src)], eng="sp")
            ps, psb = self.bank()
            P.add("pe", lambda e, ps=ps, rows=rows, st=st: e.transpose(out=ps[:, 0:rows], in_=st[0:rows, 0:128], identity=self.identF[0:rows, 0:rows]),
                  reads=[stb, B("identF")], writes=[psb])
            self.copy("dve", dst[:, dcol:dcol + rows], ps[:, 0:rows], [psb], [B(dname)])
        P.add("act", lambda e: e.activation(out=self.siluT[:, :, 1], in_=self.miscT[:, 0:8], func=AF.Silu), reads=[B("miscT")], writes=[B("siluT")])
        P.add("act", lambda e: e.activation(out=self.siluT[:, :, 0], in_=self.miscT[:, 8:16], func=AF.Silu), reads=[B("miscT")], writes=[B("siluT")])
        for t in range(12):
            src = self.xp[t * 128:(t + 1) * 128, :] if t < 4 else self.xs[(t - 4) * 128:(t - 3) * 128, :]
            st, stb = self.stageF(t % 2)
            self.dma_in(None, stb, [(st, src)], eng="sp")
            g = t // 4
            for half in range(2):
                ps, psb = self.bank()
                for j in range(4):
                    c = half * 4 + j
                    P.add("pe", lambda e, ps=ps, st=st, c=c, j=j: e.transpose(out=ps[:, j * 128:(j + 1) * 128], in_=st[:, c * 128:(c + 1) * 128], identity=self.identF[:]),
                          reads=[stb, B("identF")], writes=[psb])
                self.copy(self.evac_eng(), self.xT[:, half * 4:half * 4 + 4, t * 128:(t + 1) * 128], ps[:].rearrange("p (c t) -> p c t", c=4),
                          [psb], [B("xT%d_%d" % (c, g)) for c in range(half * 4, half * 4 + 4)])
        P.fence()

    def epilogue(self):
        P = self.P
        B = self.B
        P.fence()
        for t in range(12):
            g = t // 4
            st, stb = self.stageF(t % 2)
            for half in range(2):
                ps, psb = self.bank()
                for j in range(4):
                    c = half * 4 + j
                    P.add("pe", lambda e, ps=ps, c=c, j=j, t=t: e.transpose(out=ps[:, j * 128:(j + 1) * 128], in_=self.xT[:, c, t * 128:(t + 1) * 128], identity=self.identF[:]),
                          reads=[B("xT%d_%d" % (c, g)), B("identF")], writes=[psb])
                self.copy(self.evac_eng(), st[:, half * 512:(half + 1) * 512], ps[:], [psb], [stb])
            dst = self.yp[t * 128:(t + 1) * 128, :] if t < 4 else self.ys[(t - 4) * 128:(t - 3) * 128, :]
            self.dma_out([(dst, st)], stb, "y")

    def gcols(self, g):
        return slice(g * 512, (g + 1) * 512)

    def colsum_rstd(self, srcs, nfeat, eps=EPS):
        P, B = self.P, self.B
        psn, psnb = self.ps[7], B("ps7")
        n = len(srcs)
        for i, (ap, buf) in enumerate(srcs):
            rows = ap.shape[0]
            k = self.rot("sq", 2)
            sq, sqb = self.sq[k], B("sq%d" % k)
            P.add("act", lambda e, sq=sq, ap=ap, rows=rows: e.activation(out=sq[0:rows, :], in_=ap, func=AF.Square), reads=[buf], writes=[sqb])
            P.add("pe", lambda e, sq=sq, rows=rows, i=i: e.matmul(psn[:], lhsT=self.onesB[0:rows, :], rhs=sq[0:rows, :], start=(i == 0), stop=(i == n - 1)),
                  reads=[sqb, B("onesB")], writes=[psnb])
        k = self.rot("rs", 2)
        rs, rsb = self.rs[k], B("rs%d" % k)
        P.add("act", lambda e: e.activation(out=rs[:], in_=psn[:], func=AF.Ln, scale=1.0 / nfeat, bias=eps), reads=[psnb], writes=[rsb])
        P.add("act", lambda e: e.activation(out=rs[:], in_=rs[:], func=AF.Exp, scale=-0.5), reads=[rsb], writes=[rsb])
        return rs, rsb

    def norm_mod(self, l, which):
        P, B = self.P, self.B
        ai = 0 if which == 0 else 2
        sh = 0 if which == 0 else 3
        for g in range(3):
            ci = 0 if g == 0 else 1
            cs = self.gcols(g)
            rs, rsb = self.colsum_rstd([(self.xT[:, c, cs], B("xT%d_%d" % (c, g))) for c in range(8)], 1024.0)
            for c in range(8):
                t, tb = self.tmp()
                P.add("dve", lambda e, t=t, c=c, cs=cs, ci=ci, rs=rs: e.scalar_tensor_tensor(out=t[:], in0=self.xT[:, c, cs], scalar=self.drv[:, ai, c, ci:ci + 1], in1=rs[:], op0=ALU.mult, op1=ALU.mult),
                      reads=[B("xT%d_%d" % (c, g)), rsb, B("drv")], writes=[tb])
                P.add("act", lambda e, t=t, c=c, cs=cs, ci=ci: e.activation(out=self.hT[:, c, cs], in_=t[:], func=AF.Identity, bias=self.mod[:, sh * 8 + c, ci:ci + 1], scale=1.0),
                      reads=[tb, B("mod")], writes=[B("hT%d_%d" % (c, g))])

    def post_norm_res(self, l, which, bo_ap, bo_buf):
        P, B = self.P, self.B
        gi = 1 if which == 0 else 3
        for g in range(3):
            ci = 0 if g == 0 else 1
            cs = self.gcols(g)
            rs, rsb = self.colsum_rstd([(bo_ap(c, g), bo_buf(c, g)) for c in range(8)], 1024.0)
            for c in range(8):
                t, tb = self.tmp()
                P.add("dve", lambda e, t=t, c=c, g=g, rs=rs: e.tensor_tensor(out=t[:], in0=bo_ap(c, g), in1=rs[:], op=ALU.mult), reads=[bo_buf(c, g), rsb], writes=[tb])
                xb = B("xT%d_%d" % (c, g))
                P.add("dve", lambda e, t=t, c=c, cs=cs, ci=ci: e.scalar_tensor_tensor(out=self.xT[:, c, cs], in0=t[:], scalar=self.drv[:, gi, c, ci:ci + 1], in1=self.xT[:, c, cs], op0=ALU.mult, op1=ALU.add),
                      reads=[tb, B("drv"), xb], writes=[xb])

    def load_w(self, pairs):
        slot, sbuf = self.wslot()
        self.dma_in(None, sbuf, pairs(slot))
        return slot, sbuf

    def mod_units(self, l):
        P, B = self.P, self.B
        psm, psmb = self.ps[6], B("ps6")
        wv = self.ada_w[l].rearrange("(k p) n -> p k n", p=128)
        bufs = [self.adabuf[:, i, :].rearrange("p (k n) -> p k n", k=8) for i in range(2)]

        def load(i):
            self.dma_in(None, B("ada%d" % (i % 2)), [(bufs[i % 2], wv[:, :, i * 128:(i + 1) * 128])])

        def consume(i):
            bv = bufs[i % 2]
            for k in range(8):
                P.add("pe", lambda e, bv=bv, k=k, i=i: e.matmul(psm[:, 2 * i:2 * i + 2], lhsT=bv[:, k, :], rhs=self.siluT[:, k, :], start=(k == 0), stop=(k == 7)),
                      reads=[B("ada%d" % (i % 2)), B("siluT")], writes=[psmb])

        units = []
        for i in range(50):
            def unit(i=i):
                if 0 <= i - 2 < 48:
                    consume(i - 2)
                if i < 48:
                    load(i)
            units.append(unit)
        return units

    def modulation_finish(self, l):
        P, B = self.P, self.B
        psm, psmb = self.ps[6], B("ps6")
        pv = psm[:, 0:96].rearrange("p (j c) -> p j c", c=2)
        for ci in range(2):
            P.add("dve", lambda e, ci=ci: e.tensor_tensor(out=self.mod[:, :, ci], in0=pv[:, :, ci], in1=self.adabT[:, l * 48:(l + 1) * 48], op=ALU.add),
                  reads=[psmb, B("adabT")], writes=[B("mod")])
        for ci in range(2):
            for (di, mi, gi, plus1) in ((0, 1, 0, True), (1, 2, 1, False), (2, 4, 2, True), (3, 5, 3, False)):
                gcol = self.gT[:, (l * 4 + gi) * 8:(l * 4 + gi) * 8 + 8]
                if plus1:
                    P.add("dve", lambda e, di=di, mi=mi, ci=ci, gcol=gcol: e.scalar_tensor_tensor(out=self.drv[:, di, :, ci], in0=self.mod[:, mi * 8:mi * 8 + 8, ci], scalar=1.0, in1=gcol, op0=ALU.add, op1=ALU.mult),
                          reads=[B("mod"), B("gT")], writes=[B("drv")])
                else:
                    P.add("dve", lambda e, di=di, mi=mi, ci=ci, gcol=gcol: e.tensor_tensor(out=self.drv[:, di, :, ci], in0=self.mod[:, mi * 8:mi * 8 + 8, ci], in1=gcol, op=ALU.mult),
                          reads=[B("mod"), B("gT")], writes=[B("drv")])

    def mlp(self, l, units=()):
        P, B = self.P, self.B
        units = list(units)
        per = (len(units) + 13) // 14 if units else 0

        def interleave():
            for _ in range(per):
                if units:
                    units.pop(0)()
        h1 = self.R[:].rearrange("p (c t) -> p c t", c=32)
        w1v = self.w1[l].rearrange("(k p) n -> p k n", p=128)
        for pc in range(8):
            interleave()
            slot, sbuf = self.load_w(lambda s, pc=pc: [(s.rearrange("p (k n) -> p k n", k=8), w1v[:, :, pc * 512:(pc + 1) * 512])])
            sv = slot.rearrange("p (k n) -> p k n", k=8)
            for j in range(4):
                oc = pc * 4 + j
                banks = [self.bank() for _ in range(3)]
                for k in range(8):
                    for g in range(3):
                        ps, psb = banks[g]
                        P.add("pe", lambda e, ps=ps, sv=sv, j=j, k=k, g=g: e.matmul(ps[:], lhsT=sv[:, k, j * 128:(j + 1) * 128], rhs=self.hT[:, k, self.gcols(g)], start=(k == 0), stop=(k == 7)),
                              reads=[sbuf, B("hT%d_%d" % (k, g))], writes=[psb])
                for g in range(3):
                    ps, psb = banks[g]
                    t, tb = self.tmp()
                    P.add("act", lambda e, ps=ps, t=t: e.activation(out=t[:], in_=ps[:], func=AF.Relu), reads=[psb], writes=[tb])
                    P.add("dve", lambda e, t=t, oc=oc, g=g: e.tensor_tensor(out=h1[:, oc, self.gcols(g)], in0=t[:], in1=t[:], op=ALU.mult), reads=[tb], writes=[B("h1_%d_%d" % (oc, g))])
        w2v = self.w2[l].rearrange("(k p) n -> p k n", p=128)
        for dc in range(8):
            interleave()
            slot, sbuf = self.load_w(lambda s, dc=dc: [(s.rearrange("p (k n) -> p k n", k=32), w2v[:, :, dc * 128:(dc + 1) * 128])])
            sv = slot.rearrange("p (k n) -> p k n", k=32)
            banks = [self.bank() for _ in range(3)]
            for k in range(32):
                for g in range(3):
                    ps, psb = banks[g]
                    P.add("pe", lambda e, ps=ps, sv=sv, k=k, g=g: e.matmul(ps[:], lhsT=sv[:, k, :], rhs=h1[:, k, self.gcols(g)], start=(k == 0), stop=(k == 31)),
                          reads=[sbuf, B("h1_%d_%d" % (k, g))], writes=[psb])
            for g in range(3):
                ps, psb = banks[g]
                self.copy(self.evac_eng(), self.hT[:, dc, self.gcols(g)], ps[:], [psb], [B("hT%d_%d" % (dc, g))])
        while units:
            units.pop(0)()
        self.post_norm_res(l, 1, lambda c, g: self.hT[:, c, self.gcols(g)], lambda c, g: B("hT%d_%d" % (c, g)))

    def layer(self, l, nxt=None):
        P = self.P
        if getattr(self, "mod_ready", None) != l:
            for u in self.mod_units(l):
                u()
        self.modulation_finish(l)
        self.norm_mod(l, 0)
        if self.dbg == "norm0":
            return
        import os
        if "m" not in os.environ.get("NATDBG", ""):
            self.mixer(l)
        P.fence()
        self.norm_mod(l, 1)
        self.mlp(l, self.mod_units(nxt) if nxt is not None else ())
        self.mod_ready = nxt
        P.fence()

    def mixer(self, l):
        from_kernel_mixers(self, l)


def from_kernel_mixers(kb, l):
    Mixer(kb, l).run()


class Mixer:
    def __init__(self, kb, l):
        self.kb = kb
        self.l = l
        self.off = 12288

    def alloc(self, n, name):
        o = self.off
        self.off += n
        assert self.off <= 45056, (name, self.off)
        return self.kb.R[:, o:o + n]

    def run(self):
        kb, l = self.kb, self.l
        P, B = kb.P, kb.B
        kind = l % 4
        self.oT = kb.R[:, 0:12288].rearrange("p (c t) -> p c t", c=8)
        self.pT = [self.alloc(512, "pT") for _ in range(3)]
        if kind != 3:
            self.ra = [self.alloc(512, "ra") for _ in range(2)]
            self.rb = [self.alloc(512, "rb") for _ in range(2)]
        if kind in (0, 1, 2):
            ti = kind
            kb.dma_in(None, B("rtab"), [(kb.rtab[:, i, :], kb.c_ropetab[ti, i, :, :]) for i in range(2)])
            kb.dma_in(None, B("rsp"), [(kb.rsp[:], kb.c_ropesp[ti, :, :])])
        [self.mix_a, self.mix_b, self.mix_c, self.mix_d][kind]()
        P.fence()
        wo = [kb.attn_w_o, kb.mla_w_o, kb.swa_w_o, kb.nat_w_o][kind]
        wov = wo.rearrange("(k p) n -> p k n", p=128)
        bo = kb.R[:, 12288:24576].rearrange("p (c t) -> p c t", c=8)
        for pc in range(2):
            slot, sbuf = kb.load_w(lambda s, pc=pc: [(s.rearrange("p (k n) -> p k n", k=8), wov[:, :, pc * 512:(pc + 1) * 512])])
            sv = slot.rearrange("p (k n) -> p k n", k=8)
            for j in range(4):
                oc = pc * 4 + j
                banks = [kb.bank() for _ in range(3)]
                for k in range(8):
                    for g in range(3):
                        ps, psb = banks[g]
                        P.add("pe", lambda e, ps=ps, sv=sv, j=j, k=k, g=g: e.matmul(ps[:], lhsT=sv[:, k, j * 128:(j + 1) * 128], rhs=self.oT[:, k, kb.gcols(g)], start=(k == 0), stop=(k == 7)),
                              reads=[sbuf, B("oT%d_%d" % (k, g))], writes=[psb])
                for g in range(3):
                    ps, psb = banks[g]
                    kb.copy(kb.evac_eng(), bo[:, oc, kb.gcols(g)], ps[:], [psb], [B("bo%d_%d" % (oc, g))])
        kb.post_norm_res(l, 0, lambda c, g: bo[:, c, kb.gcols(g)], lambda c, g: B("bo%d_%d" % (c, g)))

    def proj_fm(self, lhs, nk, rhs, rows, groups=(0, 1, 2), extra=None):
        kb = self.kb
        P = kb.P
        out = {}
        for g in groups:
            out[g] = kb.bank()
        for k in range(nk):
            la, lb = lhs(k)
            for g in groups:
                ps, psb = out[g]
                ra_, rb_ = rhs(k, g)
                ex = extra(g) if extra else []
                last = (k == nk - 1) and not ex
                P.add("pe", lambda e, ps=ps, la=la, ra_=ra_, k=k, last=last: e.matmul(ps[0:rows, :], lhsT=la, rhs=ra_, start=(k == 0), stop=last),
                      reads=[lb, rb_], writes=[psb])
        if extra:
            for g in groups:
                ps, psb = out[g]
                ex = extra(g)
                for i, (la, lb, ra_, rb_) in enumerate(ex):
                    P.add("pe", lambda e, ps=ps, la=la, ra_=ra_, i=i, n=len(ex): e.matmul(ps[0:rows, :], lhsT=la, rhs=ra_, start=False, stop=(i == n - 1)),
                          reads=[lb, rb_], writes=[psb])
        return out

    def hT_rhs(self, k, g):
        kb = self.kb
        return kb.hT[:, k, kb.gcols(g)], kb.B("hT%d_%d" % (k, g))

    def finish_head(self, pss, rows, dst, dstname, norm_g=None, rope=False, state=None):
        kb = self.kb
        P, B = kb.P, kb.B
        for g, (ps, psb) in pss.items():
            src, srcb = ps[0:rows, :], psb
            if norm_g is not None:
                rs, rsb = kb.colsum_rstd([(ps[0:rows, :], psb)], float(rows))
                t, tb = kb.tmp()
                P.add("dve", lambda e, t=t, ps=ps, rs=rs: e.scalar_tensor_tensor(out=t[0:rows, :], in0=ps[0:rows, :], scalar=norm_g, in1=rs[0:rows, :], op0=ALU.mult, op1=ALU.mult),
                      reads=[psb, rsb, B("miscT")], writes=[tb])
                src, srcb = t[0:rows, :], tb
            if state is not None and g == 0:
                if norm_g is None:
                    t, tb = kb.tmp()
                    kb.copy("dve", t[0:rows, :], ps[0:rows, :], [psb], [tb])
                    src, srcb = t[0:rows, :], tb
                self.state_out_fm(src, srcb, rows, state)
            dcols = kb.gcols(g)
            db = B("%s_%d" % (dstname, g))
            if rope and g >= 1:
                i = kb.rot("rope", 2)
                ra, rb = self.ra[i], self.rb[i]
                rab, rbb = B("ra%d" % i), B("rb%d" % i)
                tc = slice((g - 1) * 512, g * 512)
                P.add("dve", lambda e, ra=ra, src=src, tc=tc: e.tensor_tensor(out=ra[0:rows, :], in0=src, in1=kb.rtab[0:rows, 0, tc], op=ALU.mult), reads=[srcb, B("rtab")], writes=[rab])
                P.add("dve", lambda e, rb=rb, src=src, tc=tc: e.tensor_tensor(out=rb[0:rows, :], in0=src, in1=kb.rtab[0:rows, 1, tc], op=ALU.mult), reads=[srcb, B("rtab")], writes=[rbb])
                pr, prb = kb.bank()
                P.add("pe", lambda e, pr=pr, ra=ra: e.matmul(pr[0:rows, :], lhsT=kb.identB[0:rows, 0:rows], rhs=ra[0:rows, :], start=True, stop=False), reads=[rab, B("identB")], writes=[prb])
                P.add("pe", lambda e, pr=pr, rb=rb: e.matmul(pr[0:rows, :], lhsT=kb.rsp[0:rows, 0:rows], rhs=rb[0:rows, :], start=False, stop=True), reads=[rbb, B("rsp")], writes=[prb])
                kb.copy("act", dst[0:rows, dcols], pr[0:rows, :], [prb], [db])
            else:
                kb.copy("act", dst[0:rows, dcols], src, [srcb], [db])

    def state_out_fm(self, src, srcb, rows, dram_view):
        kb = self.kb
        P, B = kb.P, kb.B
        ps, psb = kb.bank()
        for t in range(4):
            P.add("pe", lambda e, t=t: e.transpose(out=ps[:, t * rows:(t + 1) * rows], in_=src[:, t * 128:(t + 1) * 128], identity=kb.identF[0:rows, 0:rows]),
                  reads=[srcb, B("identF")], writes=[psb])
        st, stb = kb.stageF(kb.rot("stage", 2))
        kb.copy("dve", st[:, 0:4 * rows], ps[:, 0:4 * rows], [psb], [stb])
        kb.dma_out([(dram_view.rearrange("(t p) f -> p t f", p=128), st[:, 0:4 * rows].rearrange("p (t f) -> p t f", t=4))], stb, "state")

    def v_tok(self, dst, dstname, tiles, lhs, nk, wslot, wbuf, ncols, state=None):
        kb = self.kb
        P, B = kb.P, kb.B
        wv = wslot[:, 0:nk * ncols].rearrange("p (k n) -> p k n", k=nk)
        for i, a0 in enumerate(tiles):
            nb = (ncols + 511) // 512
            banks = [kb.bank() for _ in range(nb)]
            for k in range(nk):
                la, lb = lhs(k, a0)
                lb = lb if isinstance(lb, list) else [lb]
                for b in range(nb):
                    ps, psb = banks[b]
                    w = min(512, ncols - b * 512)
                    P.add("pe", lambda e, ps=ps, la=la, k=k, b=b, w=w: e.matmul(ps[:, 0:w], lhsT=la, rhs=wv[:, k, b * 512:b * 512 + w], start=(k == 0), stop=(k == nk - 1)),
                          reads=lb + [wbuf], writes=[psb])
            for b in range(nb):
                ps, psb = banks[b]
                w = min(512, ncols - b * 512)
                kb.copy("act", dst[:, i, b * 512:b * 512 + w], ps[:, 0:w], [psb], [B("%s_%d" % (dstname, i))])
                if state is not None and i < 4:
                    dram, cofs = state
                    st, stb = kb.stageF(kb.rot("stage", 2))
                    kb.copy("dve", st[:, 0:w], ps[:, 0:w], [psb], [stb])
                    kb.dma_out([(dram[i * 128:(i + 1) * 128, cofs + b * 512:cofs + b * 512 + w], st[:, 0:w])], stb, "state")

    def load_ctx_T(self, dram, ncol0, rows, dst, dstbuf):
        kb = self.kb
        P, B = kb.P, kb.B
        st, stb = kb.stageF(kb.rot("stage", 2))
        sv = st.rearrange("p (t f) -> p t f", t=4)
        if rows < 128:
            P.add("dve", lambda e, st=st: e.memset(st, 0.0), writes=[stb])
        kb.dma_in(None, stb, [(sv[:, :, 0:rows], dram[:, ncol0:ncol0 + rows].rearrange("(t p) f -> p t f", p=128))], eng="sp")
        ps, psb = kb.bank()
        for t in range(4):
            P.add("pe", lambda e, t=t: e.transpose(out=ps[:, t * 128:(t + 1) * 128], in_=sv[:, t, 0:128], identity=kb.identF[:]),
                  reads=[stb, B("identF")], writes=[psb])
        kb.copy("dve", dst, ps[0:rows, :], [psb], [dstbuf])

    def attn(self, qT, qbuf, ncols, blocks, dv, out_ap, out_bufs, scale, prow=0, sink=None):
        kb = self.kb
        P, B = kb.P, kb.B
        pso, psob = kb.ps[4], B("ps4")
        pss, pssb = kb.ps[5], B("ps5")
        n = len(blocks)
        qk = [None] * n
        r0, r1 = prow, prow + dv

        def emit_qk(i):
            b = blocks[i]
            ps, psb = kb.qkbank()
            lo, hi = b["lo"], b["hi"]
            ka, kbuf = b["k"]
            kbufs = kbuf if isinstance(kbuf, list) else [kbuf]
            hb = b.get("bias") is not None
            P.add("pe", lambda e, ps=ps, ka=ka, lo=lo, hi=hi, hb=hb: e.matmul(ps[:, lo:hi], lhsT=ka, rhs=qT[:, lo:hi], start=True, stop=not hb), reads=kbufs + [qbuf], writes=[psb])
            if hb:
                ba, bb = b["bias"]
                P.add("pe", lambda e, ps=ps, ba=ba, lo=lo, hi=hi: e.matmul(ps[:, lo:hi], lhsT=kb.identB[:], rhs=ba, start=False, stop=True), reads=[bb, B("identB")], writes=[psb])
            qk[i] = (ps, psb)

        for j in range(min(3, n)):
            emit_qk(j)
        for i in range(n):
            b = blocks[i]
            lo, hi = b["lo"], b["hi"]
            ps, psb = qk[i]
            pi = kb.rot("pT", 3)
            pT, pTb = self.pT[pi], B("pT%d" % pi)
            P.add("act", lambda e, pT=pT, ps=ps, lo=lo, hi=hi: e.activation(out=pT[:, lo:hi], in_=ps[:, lo:hi], func=AF.Exp, scale=scale), reads=[psb], writes=[pTb])
            if i + 3 < n:
                emit_qk(i + 3)
            va, vb = b["v"]
            P.add("pe", lambda e, va=va, pT=pT, lo=lo, hi=hi, i=i: e.matmul(pso[r0:r1, lo:hi], lhsT=va, rhs=pT[:, lo:hi], start=(i == 0), stop=(i == n - 1)), reads=[vb, pTb], writes=[psob])
            P.add("pe", lambda e, pT=pT, lo=lo, hi=hi, i=i: e.matmul(pss[r0:r1, lo:hi], lhsT=kb.onesB[:, 0:dv], rhs=pT[:, lo:hi], start=(i == 0), stop=(i == n - 1)), reads=[pTb, B("onesB")], writes=[pssb])
        k = kb.rot("rs", 2)
        rs, rsb = kb.rs[k], B("rs%d" % k)
        if sink is not None:
            P.add("act", lambda e: e.activation(out=rs[r0:r1, 0:ncols], in_=pss[r0:r1, 0:ncols], func=AF.Ln, bias=sink, scale=1.0), reads=[pssb, B("sinkE")], writes=[rsb])
        else:
            P.add("act", lambda e: e.activation(out=rs[r0:r1, 0:ncols], in_=pss[r0:r1, 0:ncols], func=AF.Ln), reads=[pssb], writes=[rsb])
        P.add("act", lambda e: e.activation(out=rs[r0:r1, 0:ncols], in_=rs[r0:r1, 0:ncols], func=AF.Exp, scale=-1.0), reads=[rsb], writes=[rsb])
        P.add("dve", lambda e: e.tensor_tensor(out=out_ap, in0=pso[r0:r1, 0:ncols], in1=rs[r0:r1, 0:ncols], op=ALU.mult), reads=[psob, rsb], writes=out_bufs)

    def gqa(self, wqkv, nq, nkv, hd, scale, kofs, vofs, ci, q_norm=None, k_norm=None, rope=False, sink=False, local=None):
        kb, l = self.kb, self.l
        P, B = kb.P, kb.B
        wv = wqkv.rearrange("(k p) n -> p k n", p=128)
        grp = nq // nkv
        hpc = 128 // hd
        nvch = nkv * hd // 128
        ck, cv = kb.cache[l]
        so_k, so_v = kb.so[l]
        qTs = [self.alloc(1536, "qT") for _ in range(2)]
        kTs = [self.alloc(1536, "kT") for _ in range(2)]
        nck = 2
        ckTs = [self.alloc(512, "ckT") for _ in range(nck)]
        vts = [self.alloc(12 * 128, "vt").rearrange("p (t f) -> p t f", t=12) for _ in range(2)]
        cvs = [self.alloc(4 * 128, "cv").rearrange("p (t f) -> p t f", t=4) for _ in range(2)]
        vss = [self.alloc(7 * 128, "vs").rearrange("p (t f) -> p t f", t=7) for _ in range(2)] if local == "nat" else None
        tiles = [t * 128 for t in range(12)]
        if hd < 128:
            for i in range(2):
                P.add("dve", lambda e, i=i: e.memset(qTs[i], 0.0), writes=[B("qT%d_%d" % (i, g)) for g in range(3)])
                P.add("dve", lambda e, i=i: e.memset(kTs[i], 0.0), writes=[B("kT%d_%d" % (i, g)) for g in range(3)])
            for i in range(nck):
                P.add("dve", lambda e, i=i: e.memset(ckTs[i], 0.0), writes=[B("ckT%d" % i)])
        cur_v = None
        pending = []
        import os
        nh_dbg = int(os.environ.get("NATH", nkv))
        skip = os.environ.get("NATSKIP", "")
        for kvh in range(int(os.environ.get("NATH0", 0)), min(nkv, nh_dbg)):
            vch = kvh * hd // 128
            if cur_v != vch and "v" in skip:
                cur_v = vch
                vi = vch % 2
            if cur_v != vch:
                cur_v = vch
                vi = vch % 2
                slot, sbuf = kb.load_w(lambda s, vch=vch: [(s[:, 0:1024].rearrange("p (k n) -> p k n", k=8), wv[:, :, vofs + vch * 128:vofs + (vch + 1) * 128])])
                hl = lambda k, a: (kb.hT[:, k, a:a + 128], [B("hT%d_%d" % (k, gg)) for gg in sorted({a // 512, (a + 127) // 512})])
                self.v_tok(vts[vi], "vt%d" % vi, tiles, hl, 8, slot, sbuf, 128, state=(so_v, vch * 128))
                if vss is not None:
                    stiles = [576 + i * 128 for i in range(7)]
                    self.v_tok(vss[vi], "vs%d" % vi, stiles, hl, 8, slot, sbuf, 128)
                kb.dma_in(None, B("cv%d" % vi), [(cvs[vi], cv[:, vch * 128:(vch + 1) * 128].rearrange("(t p) f -> p t f", p=128))])
            vt, cvt = vts[vi], cvs[vi]
            vcol = (kvh * hd) % 128
            ki = kvh % 2
            cki = kvh % nck
            kT, ckT = kTs[ki], ckTs[cki]
            slot, sbuf = kb.load_w(lambda s, kvh=kvh: [(s[:, 0:1024].rearrange("p (k n) -> p k n", k=8), wv[:, :, kofs + kvh * hd:kofs + kvh * hd + 128])])
            sv = slot[:, 0:1024].rearrange("p (k n) -> p k n", k=8)
            if "k" not in skip:
                pss = self.proj_fm(lambda k: (sv[:, k, :], sbuf), 8, self.hT_rhs, 128)
                self.finish_head(pss, hd, kT, "kT%d" % ki, norm_g=k_norm, rope=rope, state=(so_k[:, kvh * hd:(kvh + 1) * hd] if "o" not in skip else None))
            if "x" not in skip:
                self.load_ctx_T(ck, kvh * hd, hd, ckT[0:hd, :], B("ckT%d" % cki))
            for qh in range(kvh * grp, (kvh + 1) * grp):
                if "q" in skip:
                    continue
                qi = qh % 2
                qT = qTs[qi]
                slot, sbuf = kb.load_w(lambda s, qh=qh: [(s[:, 0:1024].rearrange("p (k n) -> p k n", k=8), wv[:, :, qh * hd:qh * hd + 128])])
                sv = slot[:, 0:1024].rearrange("p (k n) -> p k n", k=8)
                pss = self.proj_fm(lambda k: (sv[:, k, :], sbuf), 8, self.hT_rhs, 128)
                self.finish_head(pss, hd, qT, "qT%d" % qi, norm_g=q_norm, rope=rope)
                och, prow = (qh * hd) // 128, (qh * hd) % 128
                sk = kb.sinkE[prow:prow + hd, qh:qh + 1] if sink else None
                def do_attn(qT=qT, kT=kT, ckT=ckT, vt=vt, cvt=cvt, vcol=vcol, qi=qi, ki=ki, vi=vi, cki=cki, qh=qh, och=och, prow=prow, sk=sk):
                    for s in range(2 if "p" not in skip else 0):
                        blocks = []
                        for i in range(2):
                            tl = 2 * s + i
                            blocks.append(dict(k=(kT[:, tl * 128:(tl + 1) * 128], B("kT%d_0" % ki)), v=(vt[:, tl, vcol:vcol + hd], B("vt%d_%d" % (vi, tl))), lo=0, hi=256))
                        self.attn(qT[:, s * 256:(s + 1) * 256], B("qT%d_0" % qi), 256, blocks, hd, self.oT[prow:prow + hd, och, s * 256:(s + 1) * 256], [B("oT%d_0" % och)], scale, prow=prow, sink=sk)
                    for g in ((1, 2) if "s" not in skip else ()):
                        blocks = []
                        for i in range(4):
                            blocks.append(dict(k=(ckT[:, i * 128:(i + 1) * 128], B("ckT%d" % cki)), v=(cvt[:, i, vcol:vcol + hd], B("cv%d" % vi)), lo=0, hi=512))
                        if local is None:
                            for i in range(8):
                                blocks.append(dict(k=(kT[:, 512 + i * 128:512 + (i + 1) * 128], B("kT%d_%d" % (ki, 1 + i // 4))), v=(vt[:, 4 + i, vcol:vcol + hd], B("vt%d_%d" % (vi, 4 + i))), lo=0, hi=512))
                        elif local == "swa":
                            for bl in range(4):
                                b = (g - 1) * 4 + bl
                                for kbk in (b - 1, b, b + 1):
                                    if kbk < 0 or kbk > 7:
                                        continue
                                    bias = None
                                    if kbk == b - 1:
                                        bias = (kb.band[:, 0, :], B("band"))
                                    elif kbk == b + 1:
                                        bias = (kb.band[:, 1, :], B("band"))
                                    blocks.append(dict(k=(kT[:, 512 + kbk * 128:512 + (kbk + 1) * 128], B("kT%d_%d" % (ki, 1 + kbk // 4))), v=(vt[:, 4 + kbk, vcol:vcol + hd], B("vt%d_%d" % (vi, 4 + kbk))), lo=bl * 128, hi=(bl + 1) * 128, bias=bias))
                        else:
                            for rl in range(8):
                                r = (g - 1) * 8 + rl
                                r0 = min(max(r - 4, 0), 8)
                                for j in range(4):
                                    kr = r0 + 2 * j
                                    kc0 = 512 + kr * 64
                                    kbn = [B("kT%d_%d" % (ki, gg)) for gg in sorted({kc0 // 512, (kc0 + 127) // 512})]
                                    if r0 % 2 == 0:
                                        va = (vt[:, 4 + kr // 2, vcol:vcol + hd], B("vt%d_%d" % (vi, 4 + kr // 2)))
                                    else:
                                        va = (vss[vi][:, (kr - 1) // 2, vcol:vcol + hd], B("vs%d_%d" % (vi, (kr - 1) // 2)))
                                    tt = kr - r + 7
                                    blocks.append(dict(k=(kT[:, kc0:kc0 + 128], kbn), v=va, lo=rl * 64, hi=(rl + 1) * 64, bias=(self.BB[:, qh, tt, :], B("BB"))))
                        self.attn(qT[:, kb.gcols(g)], B("qT%d_%d" % (qi, g)), 512, blocks, hd, self.oT[prow:prow + hd, och, kb.gcols(g)], [B("oT%d_%d" % (och, g))], scale, prow=prow, sink=sk)


                pending.append(do_attn)
                if len(pending) > 1:
                    pending.pop(0)()
        while pending:
            pending.pop(0)()
    def mix_a(self):
        kb = self.kb
        self.gqa(kb.attn_w_qkv, 8, 2, 128, 128 ** -0.5, 1024, 1280, 0, q_norm=kb.miscT[:, 16:17], k_norm=kb.miscT[:, 17:18], rope=True)

    def mix_c(self):
        kb = self.kb
        self.gqa(kb.swa_w_qkv, 16, 4, 64, 64 ** -0.5, 1024, 1280, 2, rope=True, sink=True, local="swa")

    def mix_d(self):
        kb = self.kb
        P, B = kb.P, kb.B
        import os
        dbgm = os.environ.get("NATDBG", "")
        if "c" in dbgm:
            self.gqa(kb.nat_w_qkv, 16, 16, 64, 64 ** -0.5, 1024, 2048, 3, local=None)
            return
        self.BB = self.alloc(16 * 14 * 64, "BB").rearrange("p (h t c) -> p h t c", h=16, t=14)
        off_save = self.off
        rext = self.alloc(240, "rext")
        natsh = self.alloc(4096, "natsh").rearrange("p (c k) -> p c k", c=64)
        kb.dma_in(None, B("natsh"), [(natsh, kb.c_natsh[:, :].rearrange("p (c k) -> p c k", c=64))])
        for half2 in range(2):
            st, stb = kb.stageF(kb.rot("stage", 2))
            P.add("dve", lambda e, st=st: e.memset(st[:, 0:128], 0.0), writes=[stb])
            kb.dma_in(None, stb, [(st[0:120, 32:63], kb.nat_rpb[half2 * 120:(half2 + 1) * 120, :])], eng="sp")
            ps, psb = kb.bank()
            P.add("pe", lambda e, ps=ps, st=st: e.transpose(out=ps[:, 0:128], in_=st[:, 0:128], identity=kb.identF[:]), reads=[stb, B("identF")], writes=[psb])
            kb.copy("dve", rext[:, half2 * 120:(half2 + 1) * 120], ps[:, 0:120], [psb], [B("rext")])
        rv = rext[:, :].rearrange("p (h d) -> p h d", h=16)
        import os
        dbgm = os.environ.get("NATDBG", "")
        if "a" in dbgm:
            P.add("dve", lambda e: e.memset(self.BB, 0.0), writes=[B("BB")])
        for c in range(64 if "a" not in dbgm else 0):
            ps, psb = kb.bank()
            pv = ps[:, 0:224].rearrange("p (h t) -> p h t", h=16)
            for half in range(2):
                P.add("pe", lambda e, pv=pv, c=c, half=half: e.matmul(pv[half * 64:(half + 1) * 64, :, :], lhsT=natsh[:, c, :], rhs=rv[:, :, half:half + 14], start=True, stop=True),
                      reads=[B("natsh"), B("rext")], writes=[psb])
            P.add("act", lambda e, pv=pv, c=c: e.activation(out=self.BB[:, :, :, c], in_=pv, func=AF.Identity, scale=8.0, bias=kb.natmask[:, c:c + 1]), reads=[psb, B("natmask")], writes=[B("BB")])
        P.fence()
        self.off = off_save
        self.gqa(kb.nat_w_qkv, 16, 16, 64, 64 ** -0.5, 1024, 2048, 3, local=("nat" if "b" not in dbgm else None))

    def mix_b(self):
        kb, l = self.kb, self.l
        P, B = kb.P, kb.B
        scale = 96 ** -0.5
        win = kb.mla_w_in.rearrange("(k p) n -> p k n", p=128)
        wuq = kb.mla_w_uq.rearrange("(k p) n -> p k n", p=128)
        wukv = kb.mla_w_ukv.rearrange("(k p) (h e) -> p k h e", p=128, e=128)
        ckv_c, kpe_c = kb.cache[l]
        so_ckv, so_kpe = kb.so[l]
        cqn = self.alloc(3 * 1536, "cqn").rearrange("p (c t) -> p c t", c=3)
        ckvn = self.alloc(2 * 1536, "ckvn").rearrange("p (c t) -> p c t", c=2)
        kpe = self.alloc(1536, "kpe")
        cckv = self.alloc(2 * 512, "cckv").rearrange("p (c t) -> p c t", c=2)
        ckpe = self.alloc(512, "ckpe")
        P.add("dve", lambda e: e.memset(kpe[:, :], 0.0), writes=[B("kpe_0"), B("kpe_1"), B("kpe_2")])
        P.add("dve", lambda e: e.memset(ckpe[:, :], 0.0), writes=[B("ckpe")])
        slot, sbuf = kb.load_w(lambda s: [(s[:, 0:8 * 384].rearrange("p (k n) -> p k n", k=8), win[:, :, 0:384])])
        sv = slot[:, 0:8 * 384].rearrange("p (k n) -> p k n", k=8)
        for g in range(3):
            chunks = []
            for c in range(3):
                pss = self.proj_fm(lambda k, c=c: (sv[:, k, c * 128:(c + 1) * 128], sbuf), 8, self.hT_rhs, 128, groups=(g,))
                chunks.append(pss[g])
            rs, rsb = kb.colsum_rstd([(ps[:], psb) for (ps, psb) in chunks], 384.0)
            for c, (ps, psb) in enumerate(chunks):
                P.add("dve", lambda e, ps=ps, c=c, g=g, rs=rs: e.scalar_tensor_tensor(out=cqn[:, c, kb.gcols(g)], in0=ps[:], scalar=kb.miscT[:, 18 + c:19 + c], in1=rs[:], op0=ALU.mult, op1=ALU.mult),
                      reads=[psb, rsb, B("miscT")], writes=[B("cqn%d_%d" % (c, g))])
        slot, sbuf = kb.load_w(lambda s: [(s[:, 0:8 * 256].rearrange("p (k n) -> p k n", k=8), win[:, :, 384:640]), (s[:, 2048:2048 + 8 * 32].rearrange("p (k n) -> p k n", k=8), win[:, :, 640:672])])
        sv = slot[:, 0:8 * 256].rearrange("p (k n) -> p k n", k=8)
        svp = slot[:, 2048:2048 + 8 * 32].rearrange("p (k n) -> p k n", k=8)
        for g in range(3):
            chunks = []
            for c in range(2):
                pss = self.proj_fm(lambda k, c=c: (sv[:, k, c * 128:(c + 1) * 128], sbuf), 8, self.hT_rhs, 128, groups=(g,))
                chunks.append(pss[g])
            rs, rsb = kb.colsum_rstd([(ps[:], psb) for (ps, psb) in chunks], 256.0)
            for c, (ps, psb) in enumerate(chunks):
                t, tb = kb.tmp()
                P.add("dve", lambda e, t=t, ps=ps, c=c, rs=rs: e.scalar_tensor_tensor(out=t[:], in0=ps[:], scalar=kb.miscT[:, 21 + c:22 + c], in1=rs[:], op0=ALU.mult, op1=ALU.mult),
                      reads=[psb, rsb, B("miscT")], writes=[tb])
                kb.copy("act", ckvn[:, c, kb.gcols(g)], t[:], [tb], [B("ckvn%d_%d" % (c, g))])
                if g == 0:
                    self.state_out_fm(t[:], tb, 128, so_ckv[:, c * 128:(c + 1) * 128])
        pss = self.proj_fm(lambda k: (svp[:, k, :], sbuf), 8, self.hT_rhs, 32)
        for g, (ps, psb) in pss.items():
            if g == 0:
                t, tb = kb.tmp()
                kb.copy("dve", t[0:32, :], ps[0:32, :], [psb], [tb])
                self.state_out_fm(t[0:32, :], tb, 32, so_kpe[:, 0:32])
                kb.copy("act", kpe[0:32, kb.gcols(0)], t[0:32, :], [tb], [B("kpe_0")])
            else:
                kb.copy("act", kpe[0:32, kb.gcols(g)], ps[0:32, :], [psb], [B("kpe_%d" % g)])
        for c in range(2):
            self.load_ctx_T(ckv_c, c * 128, 128, cckv[:, c, :], B("cckv%d" % c))
        self.load_ctx_T(kpe_c, 0, 32, ckpe[0:32, :], B("ckpe"))
        Qs = [self.alloc(1536, "Qop") for _ in range(2)]
        Ks = [self.alloc(1536, "Kop") for _ in range(2)]
        cKs = [self.alloc(512, "cKop") for _ in range(2)]
        vts = [self.alloc(12 * 128, "vt").rearrange("p (t f) -> p t f", t=12) for _ in range(2)]
        cvs = [self.alloc(4 * 128, "cv").rearrange("p (t f) -> p t f", t=4) for _ in range(2)]
        wk = [self.alloc(2 * 96, "wk").rearrange("p (k n) -> p k n", k=2) for _ in range(2)]
        for i in range(2):
            P.add("dve", lambda e, i=i: e.memset(wk[i][:, :, :], 0.0), writes=[B("wk%d" % i)])
        tiles = [t * 128 for t in range(12)]
        pending = []
        for h in range(16):
            hi_ = h % 2
            if h % 2 == 0:
                vi = (h // 2) % 2
                slot, sbuf = kb.load_w(lambda s, h=h: [(s[:, 0:256].rearrange("p (k a e) -> p k a e", k=2, a=2)[:, :, a, :], wukv[:, :, h + a, 64:128]) for a in range(2)])
                self.v_tok(vts[vi], "vt%d" % vi, tiles, lambda k, a: (ckvn[:, k, a:a + 128], B("ckvn%d_%d" % (k, a // 512))), 2, slot, sbuf, 128)
                self.v_tok(cvs[vi], "cv%d" % vi, [i * 128 for i in range(4)], lambda k, a: (cckv[:, k, a:a + 128], B("cckv%d" % k)), 2, slot, sbuf, 128)
            vt, cvt = vts[vi], cvs[vi]
            vcol = (h % 2) * 64
            kb.dma_in(None, B("wk%d" % hi_), [(wk[hi_][:, :, 0:64], wukv[:, :, h, 0:64])])
            K, cK, Q = Ks[hi_], cKs[hi_], Qs[hi_]
            pss = self.proj_fm(lambda k: (wk[hi_][:, k, :], B("wk%d" % hi_)), 2, lambda k, g: (ckvn[:, k, kb.gcols(g)], B("ckvn%d_%d" % (k, g))), 96,
                               extra=lambda g: [(kb.mlasel[:, :], B("mlasel"), kpe[:, kb.gcols(g)], B("kpe_%d" % g))])
            self.finish_head(pss, 96, K, "Kop%d" % hi_, rope=True)
            pss = self.proj_fm(lambda k: (wk[hi_][:, k, :], B("wk%d" % hi_)), 2, lambda k, g: (cckv[:, k, :], B("cckv%d" % k)), 96, groups=(0,),
                               extra=lambda g: [(kb.mlasel[:, :], B("mlasel"), ckpe[:, :], B("ckpe"))])
            ps, psb = pss[0]
            kb.copy("act", cK[0:96, :], ps[0:96, :], [psb], [B("cKop%d" % hi_)])
            slot, sbuf = kb.load_w(lambda s, h=h: [(s[:, 0:3 * 96].rearrange("p (k n) -> p k n", k=3), wuq[:, :, h * 96:(h + 1) * 96])])
            sv = slot[:, 0:3 * 96].rearrange("p (k n) -> p k n", k=3)
            pss = self.proj_fm(lambda k: (sv[:, k, :], sbuf), 3, lambda k, g: (cqn[:, k, kb.gcols(g)], B("cqn%d_%d" % (k, g))), 96)
            self.finish_head(pss, 96, Q, "Qop%d" % hi_, rope=True)
            def do_attn(Q=Q, K=K, cK=cK, vt=vt, cvt=cvt, vcol=vcol, hi_=hi_, vi=vi, h=h):
                och, prow = h // 2, (h % 2) * 64
                for s in range(2):
                    blocks = []
                    for i in range(2):
                        tl = 2 * s + i
                        blocks.append(dict(k=(K[0:96, tl * 128:(tl + 1) * 128], B("Kop%d_0" % hi_)), v=(vt[:, tl, vcol:vcol + 64], B("vt%d_%d" % (vi, tl))), lo=0, hi=256))
                    self.attn(Q[0:96, s * 256:(s + 1) * 256], B("Qop%d_0" % hi_), 256, blocks, 64, self.oT[prow:prow + 64, och, s * 256:(s + 1) * 256], [B("oT%d_0" % och)], scale, prow=prow)
                for g in (1, 2):
                    blocks = []
                    for i in range(4):
                        blocks.append(dict(k=(cK[0:96, i * 128:(i + 1) * 128], B("cKop%d" % hi_)), v=(cvt[:, i, vcol:vcol + 64], B("cv%d_%d" % (vi, i))), lo=0, hi=512))
                    for i in range(8):
                        blocks.append(dict(k=(K[0:96, 512 + i * 128:512 + (i + 1) * 128], B("Kop%d_%d" % (hi_, 1 + i // 4))), v=(vt[:, 4 + i, vcol:vcol + 64], B("vt%d_%d" % (vi, 4 + i))), lo=0, hi=512))
                    self.attn(Q[0:96, kb.gcols(g)], B("Qop%d_%d" % (hi_, g)), 512, blocks, 64, self.oT[prow:prow + 64, och, kb.gcols(g)], [B("oT%d_%d" % (och, g))], scale, prow=prow)
            pending.append(do_attn)
            if len(pending) > 1:
                pending.pop(0)()
        while pending:
            pending.pop(0)()

_CACHE = {}


def _get_nc(nl=NL):
    if nl not in _CACHE:
        kb = KB(nl)
        _CACHE[nl] = (kb.build(), kb)
    return _CACHE[nl]


def kernel(x_prompt, x_sample, cache_l0_k, cache_l0_v, cache_l1_ckv, cache_l1_kpe, cache_l2_k, cache_l2_v, cache_l3_k, cache_l3_v,
           c, c_ctx, ada_w, ada_b, norm_g, mlp_w1, mlp_w2, attn_w_qkv, attn_q_norm, attn_k_norm, attn_w_o,
           mla_w_in, mla_q_norm, mla_kv_norm, mla_w_uq, mla_w_ukv, mla_w_o, swa_w_qkv, swa_sink, swa_w_o,
           nat_w_qkv, nat_rpb, nat_w_o, _nl=NL):
    f = lambda a: np.ascontiguousarray(np.asarray(a, dtype=np.float32))
    nc, kb = _get_nc(_nl)
    consts = _consts()
    shared = dict(ada_w=f(ada_w), ada_b=f(ada_b).reshape(192, 128), norm_g=f(norm_g).reshape(128, 128), mlp_w1=f(mlp_w1), mlp_w2=f(mlp_w2),
                  attn_w_qkv=f(attn_w_qkv), attn_w_o=f(attn_w_o), mla_w_in=f(mla_w_in), mla_w_uq=f(mla_w_uq), mla_w_ukv=f(mla_w_ukv), mla_w_o=f(mla_w_o),
                  swa_w_qkv=f(swa_w_qkv), swa_w_o=f(swa_w_o), swa_sink=f(swa_sink).reshape(1, 16), nat_w_qkv=f(nat_w_qkv), nat_w_o=f(nat_w_o),
                  nat_rpb=f(nat_rpb).reshape(240, 31))
    shared.update(consts)
    xp = f(x_prompt); xs = f(x_sample); cc = f(c)
    in_maps = []
    for i in range(8):
        misc = np.zeros((128, 128), np.float32)
        misc[0:8] = cc[i].reshape(8, 128)
        misc[8:16] = f(c_ctx).reshape(8, 128)
        misc[16] = f(attn_q_norm); misc[17] = f(attn_k_norm)
        misc[18:21] = f(mla_q_norm).reshape(3, 128); misc[21:23] = f(mla_kv_norm).reshape(2, 128)
        m = dict(shared)
        m.update(xp=xp[2 * i:2 * i + 2].reshape(512, 1024), xs=xs[i], misc=misc,
                 c0k=f(cache_l0_k)[i].reshape(512, 256), c0v=f(cache_l0_v)[i].reshape(512, 256),
                 c1ckv=f(cache_l1_ckv)[i], c1kpe=f(cache_l1_kpe)[i],
                 c2k=f(cache_l2_k)[i].reshape(512, 256), c2v=f(cache_l2_v)[i].reshape(512, 256),
                 c3k=f(cache_l3_k)[i].reshape(512, 1024), c3v=f(cache_l3_v)[i].reshape(512, 1024))
        in_maps.append(m)
    res = run_bass_kernel_spmd(nc, in_maps, core_ids=list(range(8)))
    r = res.results
    cat = lambda k: np.concatenate([np.asarray(r[i][k], dtype=np.float32) for i in range(8)], axis=0)
    yp = cat("yp").reshape(16, 256, 1024)
    ys = cat("ys").reshape(8, 1024, 1024)
    return (yp, ys,
            cat("o0k").reshape(16, 256, 2, 128), cat("o0v").reshape(16, 256, 2, 128),
            cat("o1ckv").reshape(16, 256, 256), cat("o1kpe").reshape(16, 256, 32),
            cat("o2k").reshape(16, 256, 4, 64), cat("o2v").reshape(16, 256, 4, 64),
            cat("o3k").reshape(16, 256, 16, 64), cat("o3v").reshape(16, 256, 16, 64))
```

```python
import numpy as np
import concourse.bass as bass
import concourse.mybir as mybir
from concourse.bass_utils import run_bass_kernel_spmd
from contextlib import ExitStack

F32 = mybir.dt.float32
BF16 = mybir.dt.bfloat16
AF = mybir.ActivationFunctionType
ALU = mybir.AluOpType
ENGS = ("pe", "act", "dve", "pool", "sp")
NL = 4
EPS = 1e-6
NEG = -30000.0


class Buf:
    __slots__ = ("name", "w", "r", "psum")

    def __init__(self, name):
        self.name = name
        self.w = None
        self.r = []
        self.psum = len(name) == 3 and name.startswith("ps") and name[2].isdigit()


class Op:
    __slots__ = ("eng", "fn", "waits", "marked", "key", "val", "clock", "isdma", "count", "ndma")


class Prog:
    def __init__(self):
        self.eng_ops = {e: [] for e in ENGS}
        self.eclk = {e: {} for e in ENGS}
        self.dmaval = {}
        self.dmalast = {}
        self.fence_id = 0
        self.fence_deps = []
        self.fence_gen = {e: 0 for e in ENGS}

    def fence(self):
        deps = []
        for e in ENGS:
            for op in reversed(self.eng_ops[e]):
                if not op.isdma:
                    deps.append(op)
                    break
        deps.extend(self.dmalast.values())
        self.fence_id += 1
        self.fence_deps = deps

    def add(self, eng, fn, reads=(), writes=(), dma=None, ndma=1):
        op = Op()
        op.eng = eng
        op.fn = fn
        op.marked = False
        op.isdma = dma is not None
        op.ndma = ndma
        deps = []
        if self.fence_gen[eng] < self.fence_id:
            self.fence_gen[eng] = self.fence_id
            for d in self.fence_deps:
                deps.append((d, 0))
        for b in reads:
            if b.w is not None:
                deps.append((b.w, 0))
            if b.psum:
                for r in b.r:
                    if r.eng != eng:
                        deps.append((r, 3))
        for b in writes:
            if b.w is not None:
                deps.append((b.w, 1))
            for r in b.r:
                deps.append((r, 2))
        clk = self.eclk[eng]
        waits = []
        for d, kind in deps:
            if (not d.isdma) and d.eng == eng:
                if eng == "pe" or kind == 2:
                    continue
            if clk.get(d.key, 0) >= d.val:
                continue
            waits.append(d)
            d.marked = True
            for k, v in d.clock.items():
                if clk.get(k, 0) < v:
                    clk[k] = v
        op.waits = waits
        if dma is not None:
            op.key = ("d", dma.name)
            op.val = self.dmaval.get(op.key, 0) + ndma
            self.dmaval[op.key] = op.val
            self.dmalast[op.key] = op
            op.marked = True
        else:
            op.key = eng
            op.val = len(self.eng_ops[eng]) + 1
        c = dict(clk)
        c[op.key] = op.val
        op.clock = c
        self.eng_ops[eng].append(op)
        for b in reads:
            b.r.append(op)
        for b in writes:
            b.w = op
            b.r = []
        return op

    def emit(self, nc, es, final_keys):
        sems = {}

        def sem_of(key):
            if key not in sems:
                nm = "s%d" % len(sems)
                sems[key] = es.enter_context(nc.semaphore(nm))
            return sems[key]

        for e in ENGS:
            cnt = 0
            for op in self.eng_ops[e]:
                if op.isdma:
                    op.count = 16 * op.val
                elif op.marked:
                    cnt += 1
                    op.count = cnt
        block = es.enter_context(nc.Block())
        handles = {"pe": block.tensor, "act": block.scalar, "dve": block.vector, "pool": block.gpsimd, "sp": block.sync}
        prog = self

        def make(ename):
            def body(e):
                for op in prog.eng_ops[ename]:
                    for d in op.waits:
                        e.wait_ge(sem_of(d.key), d.count)
                    r = op.fn(e)
                    if op.isdma:
                        s = sem_of(op.key)
                        for ins in r:
                            ins.then_inc(s, 16)
                    elif op.marked:
                        r.then_inc(sem_of(op.key), 1)
                if ename == "sp":
                    for key in final_keys:
                        e.wait_ge(sem_of(key), 16 * prog.dmaval[key])
            return body

        for ename in ENGS:
            handles[ename](make(ename))
        return len(sems)


def _rope_tables(hd_rot, rows, row0):
    S, GW = 1024, 64
    q = hd_rot // 4
    t = np.arange(S)
    pos = np.stack([t // GW, t % GW], axis=-1).astype(np.float32)
    inv = (np.float32(10000.0) ** (-np.arange(q, dtype=np.float32) / np.float32(q))).astype(np.float32)
    ang = (pos[:, :, None] * inv).astype(np.float32)
    cos = np.ones((rows, S), np.float32)
    sin = np.zeros((rows, S), np.float32)
    sp = np.zeros((rows, rows), np.float32)
    for a in range(2):
        for j in range(2):
            for i in range(q):
                d = row0 + a * 2 * q + j * q + i
                cos[d] = np.cos(ang[:, a, i])
                sin[d] = np.sin(ang[:, a, i])
                if j == 0:
                    sp[d + q, d] = -1.0
                else:
                    sp[d - q, d] = 1.0
    return cos, sin, sp


def _consts():
    c = {}
    c["ident"] = np.eye(128, dtype=np.float32)
    ca, sa, pa = _rope_tables(128, 128, 0)
    cb, sb_, pb = _rope_tables(32, 96, 64)
    cc, sc, pc = _rope_tables(64, 64, 0)
    tab = np.zeros((3, 2, 128, 1024), np.float32)
    spm = np.zeros((3, 128, 128), np.float32)
    tab[0, 0], tab[0, 1], spm[0] = ca, sa, pa
    tab[1, 0, :96], tab[1, 1, :96], spm[1, :96, :96] = cb, sb_, pb
    tab[2, 0, :64], tab[2, 1, :64], spm[2, :64, :64] = cc, sc, pc
    c["ropetab"] = tab
    c["ropesp"] = spm
    k = np.arange(128)[:, None]
    q = np.arange(128)[None, :]
    bm = np.zeros((2, 128, 128), np.float32)
    bm[0] = np.where(q <= k, 0.0, NEG)
    bm[1] = np.where(k <= q, 0.0, NEG)
    c["bandmask"] = bm
    sh = np.zeros((128, 64, 64), np.float32)
    for cc_ in range(64):
        for kc_ in range(64):
            i_ = kc_ - cc_ + 47
            if 0 <= i_ < 128:
                sh[i_, cc_, kc_] = 1.0
    sh = sh.reshape(128, 4096)
    c["natsh"] = sh
    cq = np.arange(64)
    c0 = np.clip(cq - 8, 0, 48)
    kc = np.arange(64)[:, None]
    m = np.where((kc >= c0[None, :]) & (kc < c0[None, :] + 16), 0.0, NEG).astype(np.float32)
    c["natmask"] = np.concatenate([m, m], 0)
    sel = np.zeros((128, 96), np.float32)
    for i in range(32):
        sel[i, 64 + i] = 1.0
    c["mlasel"] = sel
    return c


class KB:
    def __init__(self, nl=NL, dbg=None):
        self.nl = nl
        self.dbg = dbg
        self.nc = bass.Bass("TRN2", target_bir_lowering=False)
        self.P = Prog()
        self.es = ExitStack()
        self.bufs = {}
        self.outkeys = []
        self._rr = {}

    def B(self, name):
        b = self.bufs.get(name)
        if b is None:
            b = self.bufs[name] = Buf(name)
        return b

    def dram(self, name, shape, out=False):
        return self.nc.dram_tensor(name, list(shape), F32, kind="ExternalOutput" if out else "ExternalInput").ap()

    def sb(self, name, shape, dt=F32):
        return self.es.enter_context(self.nc.sbuf_tensor(name, list(shape), dt))

    def rot(self, name, n):
        i = self._rr.get(name, 0)
        self._rr[name] = (i + 1) % n
        return i

    def bank(self):
        i = self.rot("psum", 6)
        return self.ps[i], self.B("ps%d" % i)

    def qkbank(self):
        i = self.rot("qkb", 4)
        return self.ps[i], self.B("ps%d" % i)

    def tmp(self):
        i = self.rot("tmp", 3)
        return self.tmp32[i], self.B("tmp%d" % i)

    def evac_eng(self):
        return ("act", "dve")[self.rot("evac", 2)]

    def dma_in(self, dst_ap, dst_buf, src_aps, eng="pool", reads=()):
        def fn(e, pairs=src_aps):
            return [e.dma_start(out=o, in_=i) for (o, i) in pairs]
        self.P.add(eng, fn, reads=list(reads), writes=[dst_buf], dma=dst_buf, ndma=len(src_aps))

    def dma_out(self, pairs, src_buf, key):
        def fn(e, pairs=pairs):
            return [e.dma_start(out=o, in_=i) for (o, i) in pairs]
        kb = self.B("out_" + key)
        self.P.add("sp", fn, reads=[src_buf], writes=[], dma=src_buf, ndma=len(pairs))
        k = ("d", src_buf.name)
        if k not in self.outkeys:
            self.outkeys.append(k)

    def wslot(self):
        i = self.rot("wsl", 2)
        return self.wsl[i][:], self.B("wsl%d" % i)

    def copy(self, eng, out, in_, reads, writes):
        if eng == "act":
            self.P.add("act", lambda e: e.activation(out=out, in_=in_, func=AF.Copy), reads=reads, writes=writes)
        else:
            self.P.add("dve", lambda e: e.tensor_copy(out=out, in_=in_), reads=reads, writes=writes)

    def build(self):
        nc, P = self.nc, self.P
        D = self.dram
        self.xp = D("xp", [512, 1024]); self.xs = D("xs", [1024, 1024])
        self.cache = [
            (D("c0k", [512, 256]), D("c0v", [512, 256])),
            (D("c1ckv", [512, 256]), D("c1kpe", [512, 32])),
            (D("c2k", [512, 256]), D("c2v", [512, 256])),
            (D("c3k", [512, 1024]), D("c3v", [512, 1024])),
        ]
        self.misc_in = D("misc", [128, 128])
        self.ada_w = D("ada_w", [4, 1024, 6144]); self.ada_b = D("ada_b", [192, 128]); self.norm_g = D("norm_g", [128, 128])
        self.w1 = D("mlp_w1", [4, 1024, 4096]); self.w2 = D("mlp_w2", [4, 4096, 1024])
        self.attn_w_qkv = D("attn_w_qkv", [1024, 1536]); self.attn_w_o = D("attn_w_o", [1024, 1024])
        self.mla_w_in = D("mla_w_in", [1024, 672]); self.mla_w_uq = D("mla_w_uq", [384, 1536])
        self.mla_w_ukv = D("mla_w_ukv", [256, 2048]); self.mla_w_o = D("mla_w_o", [1024, 1024])
        self.swa_w_qkv = D("swa_w_qkv", [1024, 1536]); self.swa_w_o = D("swa_w_o", [1024, 1024]); self.swa_sink = D("swa_sink", [1, 16])
        self.nat_w_qkv = D("nat_w_qkv", [1024, 3072]); self.nat_w_o = D("nat_w_o", [1024, 1024]); self.nat_rpb = D("nat_rpb", [240, 31])
        self.c_ident = D("ident", [128, 128]); self.c_ropetab = D("ropetab", [3, 2, 128, 1024]); self.c_ropesp = D("ropesp", [3, 128, 128])
        self.c_band = D("bandmask", [2, 128, 128]); self.c_natsh = D("natsh", [128, 4096]); self.c_natmask = D("natmask", [128, 64])
        self.c_mlasel = D("mlasel", [128, 96])
        self.yp = D("yp", [512, 1024], True); self.ys = D("ys", [1024, 1024], True)
        self.so = [
            (D("o0k", [512, 256], True), D("o0v", [512, 256], True)),
            (D("o1ckv", [512, 256], True), D("o1kpe", [512, 32], True)),
            (D("o2k", [512, 256], True), D("o2v", [512, 256], True)),
            (D("o3k", [512, 1024], True), D("o3v", [512, 1024], True)),
        ]
        with self.es:
            sb = self.sb
            self.xT = sb("xT", [128, 8, 1536])
            self.hT = sb("hT", [128, 8, 1536], BF16)
            self.R = sb("R", [128, 49152], BF16)
            self.wsl = [sb("wsl%d" % i, [128, 4096], BF16) for i in range(2)]
            self.sq = [sb("sq%d" % i, [128, 512], BF16) for i in range(4)]
            self.tmp32 = [sb("tmp%d" % i, [128, 512]) for i in range(3)]
            self.rs = [sb("rs%d" % i, [128, 512]) for i in range(2)]
            self.rtab = sb("rtab", [128, 2, 1024], BF16)
            self.rsp = sb("rsp", [128, 128], BF16)
            self.identF = sb("identF", [128, 128]); self.identB = sb("identB", [128, 128], BF16); self.onesB = sb("onesB", [128, 128], BF16)
            self.gT = sb("gT", [128, 128]); self.adabT = sb("adabT", [128, 192]); self.miscT = sb("miscT", [128, 32])
            self.siluT = sb("siluT", [128, 8, 2], BF16)
            self.mod = sb("mod", [128, 48, 2]); self.drv = sb("drv", [128, 4, 8, 2])
            self.band = sb("band", [128, 2, 128], BF16); self.sinkE = sb("sinkE", [128, 16])
            self.natmask = sb("natmaskS", [128, 64]); self.mlasel = sb("mlaselS", [128, 96], BF16)
            self.ps = [self.es.enter_context(nc.psum_tensor("ps%d" % i, [128, 512], F32)) for i in range(8)]
            self.Rf = self.R[:].bitcast(F32)
            self.prologue()
            import os
            lys = os.environ.get("LAYERS")
            lyl = [int(c) for c in lys] if lys else list(range(self.nl))
            for i, l in enumerate(lyl):
                self.layer(l, lyl[i + 1] if i + 1 < len(lyl) else None)
            self.epilogue()
            if self.dbg:
                self.P.fence()
                dbg = self.dram("dbg", [128, 2048], True)
                db = self.B("dbgbuf")
                pairs = [(dbg[:, 0:96], self.mod[:].rearrange("p a b -> p (a b)")), (dbg[:, 96:160], self.drv[:].rearrange("p a b c -> p (a b c)")),
                         (dbg[:, 160:288], self.gT[:]), (dbg[:, 288:480], self.adabT[:]), (dbg[:, 480:512], self.miscT[:]),
                         (dbg[:, 512:1024], self.rs[0][:]), (dbg[:, 1024:1536], self.xT[:, 0, 0:512])]
                self.P.add("sp", lambda e: [e.dma_start(out=o, in_=i) for (o, i) in pairs], reads=[], writes=[db], dma=db, ndma=len(pairs))
                self.P.add("pool", lambda e: [e.dma_start(out=dbg[:, 1536:2048], in_=self.hT[:, 0, 0:512])], reads=[], writes=[db], dma=db, ndma=1)
                self.outkeys.append(("d", "dbgbuf"))
            nsem = P.emit(nc, self.es, self.outkeys)
            self.nsem = nsem
        return nc

    def stageF(self, i):
        off = 22528 + i * 1024
        return self.Rf[:, off:off + 1024], self.B("stage%d" % i)

    def prologue(self):
        P = self.P
        B = self.B
        self.dma_in(None, B("identF"), [(self.identF[:], self.c_ident[:, :])], eng="sp")
        self.dma_in(None, B("identB"), [(self.identB[:], self.c_ident[:, :])])
        self.dma_in(None, B("band"), [(self.band[:, i, :], self.c_band[i, :, :]) for i in range(2)])
        self.dma_in(None, B("natmask"), [(self.natmask[:], self.c_natmask[:, :])], eng="sp")
        self.dma_in(None, B("mlasel"), [(self.mlasel[:], self.c_mlasel[:, :])])
        P.add("dve", lambda e: e.memset(self.onesB[:], 1.0), writes=[B("onesB")])
        self.dma_in(None, B("sinkE"), [(self.sinkE[:], self.swa_sink[0:1, :].partition_broadcast(128))], eng="sp")
        P.add("act", lambda e: e.activation(out=self.sinkE[:], in_=self.sinkE[:], func=AF.Exp), reads=[B("sinkE")], writes=[B("sinkE")])
        st, stb = self.stageF(0)
        for (src, rows, dst, dcol, dname) in ((self.norm_g[:, :], 128, self.gT, 0, "gT"), (self.ada_b[0:128, :], 128, self.adabT, 0, "adabT"),
                                               (self.ada_b[128:192, :], 64, self.adabT, 128, "adabT"), (self.misc_in[0:32, :], 32, self.miscT, 0, "miscT")):
            self.dma_in(None, stb, [(st[0:rows, 0:128], src)], eng="sp")
            ps, psb = self.bank()
            P.add("pe", lambda e, ps=ps, rows=rows, st=st: e.transpose(out=ps[:, 0:rows], in_=st[0:rows, 0:128], identity=self.identF[0:rows, 0:rows]),
                  reads=[stb, B("identF")], writes=[psb])
            self.copy("dve", dst[:, dcol:dcol + rows], ps[:, 0:rows], [psb], [B(dname)])
        P.add("act", lambda e: e.activation(out=self.siluT[:, :, 1], in_=self.miscT[:, 0:8], func=AF.Silu), reads=[B("miscT")], writes=[B("siluT")])
        P.add("act", lambda e: e.activation(out=self.siluT[:, :, 0], in_=self.miscT[:, 8:16], func=AF.Silu), reads=[B("miscT")], writes=[B("siluT")])
        for t in range(12):
            src = self.xp[t * 128:(t + 1) * 128, :] if t < 4 else self.xs[(t - 4) * 128:(t - 3) * 128, :]
            st, stb = self.stageF(t % 2)
            self.dma_in(None, stb, [(st, src)], eng="sp")
            g = t // 4
            for half in range(2):
                ps, psb = self.bank()
                for j in range(4):
                    c = half * 4 + j
                    P.add("pe", lambda e, ps=ps, st=st, c=c, j=j: e.transpose(out=ps[:, j * 128:(j + 1) * 128], in_=st[:, c * 128:(c + 1) * 128], identity=self.identF[:]),
                          reads=[stb, B("identF")], writes=[psb])
                self.copy(self.evac_eng(), self.xT[:, half * 4:half * 4 + 4, t * 128:(t + 1) * 128], ps[:].rearrange("p (c t) -> p c t", c=4),
                          [psb], [B("xT%d_%d" % (c, g)) for c in range(half * 4, half * 4 + 4)])
        P.fence()

    def epilogue(self):
        P = self.P
        B = self.B
        P.fence()
        for t in range(12):
            g = t // 4
            st, stb = self.stageF(t % 2)
            for half in range(2):
                ps, psb = self.bank()
                for j in range(4):
                    c = half * 4 + j
                    P.add("pe", lambda e, ps=ps, c=c, j=j, t=t: e.transpose(out=ps[:, j * 128:(j + 1) * 128], in_=self.xT[:, c, t * 128:(t + 1) * 128], identity=self.identF[:]),
                          reads=[B("xT%d_%d" % (c, g)), B("identF")], writes=[psb])
                self.copy(self.evac_eng(), st[:, half * 512:(half + 1) * 512], ps[:], [psb], [stb])
            dst = self.yp[t * 128:(t + 1) * 128, :] if t < 4 else self.ys[(t - 4) * 128:(t - 3) * 128, :]
            self.dma_out([(dst, st)], stb, "y")

    def gcols(self, g):
        return slice(g * 512, (g + 1) * 512)

    def colsum_rstd(self, srcs, nfeat, eps=EPS):
        P, B = self.P, self.B
        psn, psnb = self.ps[7], B("ps7")
        n = len(srcs)
        for i, (ap, buf) in enumerate(srcs):
            rows = ap.shape[0]
            k = self.rot("sq", 4)
            sq, sqb = self.sq[k], B("sq%d" % k)
            P.add("act", lambda e, sq=sq, ap=ap, rows=rows: e.activation(out=sq[0:rows, :], in_=ap, func=AF.Square), reads=[buf], writes=[sqb])
            P.add("pe", lambda e, sq=sq, rows=rows, i=i: e.matmul(psn[:], lhsT=self.onesB[0:rows, :], rhs=sq[0:rows, :], start=(i == 0), stop=(i == n - 1)),
                  reads=[sqb, B("onesB")], writes=[psnb])
        k = self.rot("rs", 2)
        rs, rsb = self.rs[k], B("rs%d" % k)
        P.add("act", lambda e: e.activation(out=rs[:], in_=psn[:], func=AF.Ln, scale=1.0 / nfeat, bias=eps), reads=[psnb], writes=[rsb])
        P.add("act", lambda e: e.activation(out=rs[:], in_=rs[:], func=AF.Exp, scale=-0.5), reads=[rsb], writes=[rsb])
        return rs, rsb

    def norm_mod(self, l, which):
        P, B = self.P, self.B
        ai = 0 if which == 0 else 2
        sh = 0 if which == 0 else 3
        for g in range(3):
            ci = 0 if g == 0 else 1
            cs = self.gcols(g)
            rs, rsb = self.colsum_rstd([(self.xT[:, c, cs], B("xT%d_%d" % (c, g))) for c in range(8)], 1024.0)
            for c in range(8):
                t, tb = self.tmp()
                P.add("dve", lambda e, t=t, c=c, cs=cs, ci=ci, rs=rs: e.scalar_tensor_tensor(out=t[:], in0=self.xT[:, c, cs], scalar=self.drv[:, ai, c, ci:ci + 1], in1=rs[:], op0=ALU.mult, op1=ALU.mult),
                      reads=[B("xT%d_%d" % (c, g)), rsb, B("drv")], writes=[tb])
                P.add("act", lambda e, t=t, c=c, cs=cs, ci=ci: e.activation(out=self.hT[:, c, cs], in_=t[:], func=AF.Identity, bias=self.mod[:, sh * 8 + c, ci:ci + 1], scale=1.0),
                      reads=[tb, B("mod")], writes=[B("hT%d_%d" % (c, g))])

    def post_norm_res(self, l, which, bo_ap, bo_buf):
        P, B = self.P, self.B
        gi = 1 if which == 0 else 3
        for g in range(3):
            ci = 0 if g == 0 else 1
            cs = self.gcols(g)
            rs, rsb = self.colsum_rstd([(bo_ap(c, g), bo_buf(c, g)) for c in range(8)], 1024.0)
            for c in range(8):
                t, tb = self.tmp()
                P.add("dve", lambda e, t=t, c=c, g=g, rs=rs: e.tensor_tensor(out=t[:], in0=bo_ap(c, g), in1=rs[:], op=ALU.mult), reads=[bo_buf(c, g), rsb], writes=[tb])
                xb = B("xT%d_%d" % (c, g))
                P.add("dve", lambda e, t=t, c=c, cs=cs, ci=ci: e.scalar_tensor_tensor(out=self.xT[:, c, cs], in0=t[:], scalar=self.drv[:, gi, c, ci:ci + 1], in1=self.xT[:, c, cs], op0=ALU.mult, op1=ALU.add),
                      reads=[tb, B("drv"), xb], writes=[xb])

    def load_w(self, pairs):
        slot, sbuf = self.wslot()
        self.dma_in(None, sbuf, pairs(slot))
        return slot, sbuf

    def mod_units(self, l, ring=None):
        P, B = self.P, self.B
        psm, psmb = self.ps[6], B("ps6")
        wv = self.ada_w[l].rearrange("(k p) n -> p k n", p=128)
        if ring is None:
            ring = [self.wsl[i][:, j * 1024:(j + 1) * 1024] for i in range(2) for j in range(4)]
            names = ["mring%d" % i for i in range(8)]
        else:
            names = ["ring%d" % i for i in range(len(ring))]
        D = len(ring)
        bufs = [r.rearrange("p (k n) -> p k n", k=8) for r in ring]

        def load(i):
            self.dma_in(None, B(names[i % D]), [(bufs[i % D], wv[:, :, i * 128:(i + 1) * 128])])

        def consume(i):
            bv = bufs[i % D]
            for k in range(8):
                P.add("pe", lambda e, bv=bv, k=k, i=i: e.matmul(psm[:, 2 * i:2 * i + 2], lhsT=bv[:, k, :], rhs=self.siluT[:, k, :], start=(k == 0), stop=(k == 7)),
                      reads=[B(names[i % D]), B("siluT")], writes=[psmb])

        units = []
        for i in range(48 + D):
            def unit(i=i):
                if 0 <= i - D < 48:
                    consume(i - D)
                if i < 48:
                    load(i)
            units.append(unit)
        return units

    def modulation_finish(self, l):
        P, B = self.P, self.B
        psm, psmb = self.ps[6], B("ps6")
        pv = psm[:, 0:96].rearrange("p (j c) -> p j c", c=2)
        for ci in range(2):
            P.add("dve", lambda e, ci=ci: e.tensor_tensor(out=self.mod[:, :, ci], in0=pv[:, :, ci], in1=self.adabT[:, l * 48:(l + 1) * 48], op=ALU.add),
                  reads=[psmb, B("adabT")], writes=[B("mod")])
        for ci in range(2):
            for (di, mi, gi, plus1) in ((0, 1, 0, True), (1, 2, 1, False), (2, 4, 2, True), (3, 5, 3, False)):
                gcol = self.gT[:, (l * 4 + gi) * 8:(l * 4 + gi) * 8 + 8]
                if plus1:
                    P.add("dve", lambda e, di=di, mi=mi, ci=ci, gcol=gcol: e.scalar_tensor_tensor(out=self.drv[:, di, :, ci], in0=self.mod[:, mi * 8:mi * 8 + 8, ci], scalar=1.0, in1=gcol, op0=ALU.add, op1=ALU.mult),
                          reads=[B("mod"), B("gT")], writes=[B("drv")])
                else:
                    P.add("dve", lambda e, di=di, mi=mi, ci=ci, gcol=gcol: e.tensor_tensor(out=self.drv[:, di, :, ci], in0=self.mod[:, mi * 8:mi * 8 + 8, ci], in1=gcol, op=ALU.mult),
                          reads=[B("mod"), B("gT")], writes=[B("drv")])

    def mlp(self, l, units=()):
        P, B = self.P, self.B
        units = list(units)
        per = (len(units) + 13) // 14 if units else 0

        def interleave():
            for _ in range(per):
                if units:
                    units.pop(0)()
        h1 = self.R[:].rearrange("p (c t) -> p c t", c=32)
        w1v = self.w1[l].rearrange("(k p) n -> p k n", p=128)
        for pc in range(8):
            interleave()
            slot, sbuf = self.load_w(lambda s, pc=pc: [(s.rearrange("p (k n) -> p k n", k=8), w1v[:, :, pc * 512:(pc + 1) * 512])])
            sv = slot.rearrange("p (k n) -> p k n", k=8)
            for j in range(4):
                oc = pc * 4 + j
                banks = [self.bank() for _ in range(3)]
                for k in range(8):
                    for g in range(3):
                        ps, psb = banks[g]
                        P.add("pe", lambda e, ps=ps, sv=sv, j=j, k=k, g=g: e.matmul(ps[:], lhsT=sv[:, k, j * 128:(j + 1) * 128], rhs=self.hT[:, k, self.gcols(g)], start=(k == 0), stop=(k == 7)),
                              reads=[sbuf, B("hT%d_%d" % (k, g))], writes=[psb])
                for g in range(3):
                    ps, psb = banks[g]
                    t, tb = self.tmp()
                    P.add("act", lambda e, ps=ps, t=t: e.activation(out=t[:], in_=ps[:], func=AF.Relu), reads=[psb], writes=[tb])
                    P.add("dve", lambda e, t=t, oc=oc, g=g: e.tensor_tensor(out=h1[:, oc, self.gcols(g)], in0=t[:], in1=t[:], op=ALU.mult), reads=[tb], writes=[B("h1_%d_%d" % (oc, g))])
        w2v = self.w2[l].rearrange("(k p) n -> p k n", p=128)
        for dc in range(8):
            interleave()
            slot, sbuf = self.load_w(lambda s, dc=dc: [(s.rearrange("p (k n) -> p k n", k=32), w2v[:, :, dc * 128:(dc + 1) * 128])])
            sv = slot.rearrange("p (k n) -> p k n", k=32)
            banks = [self.bank() for _ in range(3)]
            for k in range(32):
                for g in range(3):
                    ps, psb = banks[g]
                    P.add("pe", lambda e, ps=ps, sv=sv, k=k, g=g: e.matmul(ps[:], lhsT=sv[:, k, :], rhs=h1[:, k, self.gcols(g)], start=(k == 0), stop=(k == 31)),
                          reads=[sbuf, B("h1_%d_%d" % (k, g))], writes=[psb])
            for g in range(3):
                ps, psb = banks[g]
                self.copy(self.evac_eng(), self.hT[:, dc, self.gcols(g)], ps[:], [psb], [B("hT%d_%d" % (dc, g))])
        while units:
            units.pop(0)()
        self.post_norm_res(l, 1, lambda c, g: self.hT[:, c, self.gcols(g)], lambda c, g: B("hT%d_%d" % (c, g)))

    def layer(self, l, nxt=None):
        P = self.P
        if getattr(self, "mod_ready", None) != l:
            for u in self.mod_units(l):
                u()
            P.fence()
        self.modulation_finish(l)
        self.norm_mod(l, 0)
        if self.dbg == "norm0":
            return
        self.mod_ready = None
        self.prefetch_next = nxt if (nxt is not None and l % 4 != 3) else None
        self.mixer(l)
        P.fence()
        self.norm_mod(l, 1)
        self.mlp(l)
        P.fence()

    def mixer(self, l):
        from_kernel_mixers(self, l)


def from_kernel_mixers(kb, l):
    Mixer(kb, l).run()


class Mixer:
    def __init__(self, kb, l):
        self.kb = kb
        self.l = l
        self.off = 12288

    def inter(self):
        for _ in range(self.per):
            if self.units:
                self.units.pop(0)()

    def alloc(self, n, name):
        o = self.off
        self.off += n
        assert self.off <= 45056, (name, self.off)
        return self.kb.R[:, o:o + n]

    def run(self):
        kb, l = self.kb, self.l
        P, B = kb.P, kb.B
        kind = l % 4
        self.oT = kb.R[:, 0:12288].rearrange("p (c t) -> p c t", c=8)
        self.pT = [self.alloc(512, "pT") for _ in range(3)]
        if kind != 3:
            self.ra = [self.alloc(512, "ra") for _ in range(2)]
            self.rb = [self.alloc(512, "rb") for _ in range(2)]
        if kind in (0, 1, 2):
            ti = kind
            kb.dma_in(None, B("rtab"), [(kb.rtab[:, i, :], kb.c_ropetab[ti, i, :, :]) for i in range(2)])
            kb.dma_in(None, B("rsp"), [(kb.rsp[:], kb.c_ropesp[ti, :, :])])
        self.units = []
        self.per = 0
        if kb.prefetch_next is not None:
            ring = [self.alloc(1024, "ring") for _ in range(6)]
            self.units = kb.mod_units(kb.prefetch_next, ring)
            nheads = 8 if kind == 0 else 16
            self.per = (len(self.units) + nheads - 1) // nheads
            kb.mod_ready = kb.prefetch_next
        [self.mix_a, self.mix_b, self.mix_c, self.mix_d][kind]()
        while self.units:
            self.units.pop(0)()
        P.fence()
        wo = [kb.attn_w_o, kb.mla_w_o, kb.swa_w_o, kb.nat_w_o][kind]
        wov = wo.rearrange("(k p) n -> p k n", p=128)
        bo = kb.R[:, 12288:24576].rearrange("p (c t) -> p c t", c=8)
        for pc in range(2):
            slot, sbuf = kb.load_w(lambda s, pc=pc: [(s.rearrange("p (k n) -> p k n", k=8), wov[:, :, pc * 512:(pc + 1) * 512])])
            sv = slot.rearrange("p (k n) -> p k n", k=8)
            for j in range(4):
                oc = pc * 4 + j
                banks = [kb.bank() for _ in range(3)]
                for k in range(8):
                    for g in range(3):
                        ps, psb = banks[g]
                        P.add("pe", lambda e, ps=ps, sv=sv, j=j, k=k, g=g: e.matmul(ps[:], lhsT=sv[:, k, j * 128:(j + 1) * 128], rhs=self.oT[:, k, kb.gcols(g)], start=(k == 0), stop=(k == 7)),
                              reads=[sbuf, B("oT%d_%d" % (k, g))], writes=[psb])
                for g in range(3):
                    ps, psb = banks[g]
                    kb.copy(kb.evac_eng(), bo[:, oc, kb.gcols(g)], ps[:], [psb], [B("bo%d_%d" % (oc, g))])
        kb.post_norm_res(l, 0, lambda c, g: bo[:, c, kb.gcols(g)], lambda c, g: B("bo%d_%d" % (c, g)))

    def proj_fm(self, lhs, nk, rhs, rows, groups=(0, 1, 2), extra=None):
        kb = self.kb
        P = kb.P
        out = {}
        for g in groups:
            out[g] = kb.bank()
        for k in range(nk):
            la, lb = lhs(k)
            for g in groups:
                ps, psb = out[g]
                ra_, rb_ = rhs(k, g)
                ex = extra(g) if extra else []
                last = (k == nk - 1) and not ex
                P.add("pe", lambda e, ps=ps, la=la, ra_=ra_, k=k, last=last: e.matmul(ps[0:rows, :], lhsT=la, rhs=ra_, start=(k == 0), stop=last),
                      reads=[lb, rb_], writes=[psb])
        if extra:
            for g in groups:
                ps, psb = out[g]
                ex = extra(g)
                for i, (la, lb, ra_, rb_) in enumerate(ex):
                    P.add("pe", lambda e, ps=ps, la=la, ra_=ra_, i=i, n=len(ex): e.matmul(ps[0:rows, :], lhsT=la, rhs=ra_, start=False, stop=(i == n - 1)),
                          reads=[lb, rb_], writes=[psb])
        return out

    def hT_rhs(self, k, g):
        kb = self.kb
        return kb.hT[:, k, kb.gcols(g)], kb.B("hT%d_%d" % (k, g))

    def finish_head(self, pss, rows, dst, dstname, norm_g=None, rope=False, state=None):
        kb = self.kb
        P, B = kb.P, kb.B
        for g, (ps, psb) in pss.items():
            src, srcb = ps[0:rows, :], psb
            if norm_g is not None:
                rs, rsb = kb.colsum_rstd([(ps[0:rows, :], psb)], float(rows))
                t, tb = kb.tmp()
                P.add("dve", lambda e, t=t, ps=ps, rs=rs: e.scalar_tensor_tensor(out=t[0:rows, :], in0=ps[0:rows, :], scalar=norm_g, in1=rs[0:rows, :], op0=ALU.mult, op1=ALU.mult),
                      reads=[psb, rsb, B("miscT")], writes=[tb])
                src, srcb = t[0:rows, :], tb
            if state is not None and g == 0:
                if norm_g is None:
                    t, tb = kb.tmp()
                    kb.copy("dve", t[0:rows, :], ps[0:rows, :], [psb], [tb])
                    src, srcb = t[0:rows, :], tb
                self.state_out_fm(src, srcb, rows, state)
            dcols = kb.gcols(g)
            db = B("%s_%d" % (dstname, g))
            if rope and g >= 1:
                i = kb.rot("rope", 2)
                ra, rb = self.ra[i], self.rb[i]
                rab, rbb = B("ra%d" % i), B("rb%d" % i)
                tc = slice((g - 1) * 512, g * 512)
                P.add("dve", lambda e, ra=ra, src=src, tc=tc: e.tensor_tensor(out=ra[0:rows, :], in0=src, in1=kb.rtab[0:rows, 0, tc], op=ALU.mult), reads=[srcb, B("rtab")], writes=[rab])
                P.add("dve", lambda e, rb=rb, src=src, tc=tc: e.tensor_tensor(out=rb[0:rows, :], in0=src, in1=kb.rtab[0:rows, 1, tc], op=ALU.mult), reads=[srcb, B("rtab")], writes=[rbb])
                pr, prb = kb.bank()
                P.add("pe", lambda e, pr=pr, ra=ra: e.matmul(pr[0:rows, :], lhsT=kb.identB[0:rows, 0:rows], rhs=ra[0:rows, :], start=True, stop=False), reads=[rab, B("identB")], writes=[prb])
                P.add("pe", lambda e, pr=pr, rb=rb: e.matmul(pr[0:rows, :], lhsT=kb.rsp[0:rows, 0:rows], rhs=rb[0:rows, :], start=False, stop=True), reads=[rbb, B("rsp")], writes=[prb])
                kb.copy("act", dst[0:rows, dcols], pr[0:rows, :], [prb], [db])
            else:
                kb.copy("act", dst[0:rows, dcols], src, [srcb], [db])

    def state_out_fm(self, src, srcb, rows, dram_view):
        kb = self.kb
        P, B = kb.P, kb.B
        ps, psb = kb.bank()
        for t in range(4):
            P.add("pe", lambda e, t=t: e.transpose(out=ps[:, t * rows:(t + 1) * rows], in_=src[:, t * 128:(t + 1) * 128], identity=kb.identF[0:rows, 0:rows]),
                  reads=[srcb, B("identF")], writes=[psb])
        st, stb = kb.stageF(kb.rot("stage", 2))
        kb.copy("dve", st[:, 0:4 * rows], ps[:, 0:4 * rows], [psb], [stb])
        kb.dma_out([(dram_view.rearrange("(t p) f -> p t f", p=128), st[:, 0:4 * rows].rearrange("p (t f) -> p t f", t=4))], stb, "state")

    def v_tok(self, dst, dstname, tiles, lhs, nk, wslot, wbuf, ncols, state=None):
        kb = self.kb
        P, B = kb.P, kb.B
        wv = wslot[:, 0:nk * ncols].rearrange("p (k n) -> p k n", k=nk)
        for i, a0 in enumerate(tiles):
            nb = (ncols + 511) // 512
            banks = [kb.bank() for _ in range(nb)]
            for k in range(nk):
                la, lb = lhs(k, a0)
                lb = lb if isinstance(lb, list) else [lb]
                for b in range(nb):
                    ps, psb = banks[b]
                    w = min(512, ncols - b * 512)
                    P.add("pe", lambda e, ps=ps, la=la, k=k, b=b, w=w: e.matmul(ps[:, 0:w], lhsT=la, rhs=wv[:, k, b * 512:b * 512 + w], start=(k == 0), stop=(k == nk - 1)),
                          reads=lb + [wbuf], writes=[psb])
            for b in range(nb):
                ps, psb = banks[b]
                w = min(512, ncols - b * 512)
                kb.copy("act", dst[:, i, b * 512:b * 512 + w], ps[:, 0:w], [psb], [B("%s_%d" % (dstname, i))])
                if state is not None and i < 4:
                    dram, cofs = state
                    st, stb = kb.stageF(kb.rot("stage", 2))
                    kb.copy("dve", st[:, 0:w], ps[:, 0:w], [psb], [stb])
                    kb.dma_out([(dram[i * 128:(i + 1) * 128, cofs + b * 512:cofs + b * 512 + w], st[:, 0:w])], stb, "state")

    def load_ctx_T(self, dram, ncol0, rows, dst, dstbuf):
        kb = self.kb
        P, B = kb.P, kb.B
        st, stb = kb.stageF(kb.rot("stage", 2))
        sv = st.rearrange("p (t f) -> p t f", t=4)
        if rows < 128:
            P.add("dve", lambda e, st=st: e.memset(st, 0.0), writes=[stb])
        kb.dma_in(None, stb, [(sv[:, :, 0:rows], dram[:, ncol0:ncol0 + rows].rearrange("(t p) f -> p t f", p=128))], eng="sp")
        ps, psb = kb.bank()
        for t in range(4):
            P.add("pe", lambda e, t=t: e.transpose(out=ps[:, t * 128:(t + 1) * 128], in_=sv[:, t, 0:128], identity=kb.identF[:]),
                  reads=[stb, B("identF")], writes=[psb])
        kb.copy("dve", dst, ps[0:rows, :], [psb], [dstbuf])

    def attn(self, qT, qbuf, ncols, blocks, dv, out_ap, out_bufs, scale, prow=0, sink=None):
        kb = self.kb
        P, B = kb.P, kb.B
        pso, psob = kb.ps[4], B("ps4")
        pss, pssb = kb.ps[5], B("ps5")
        n = len(blocks)
        qk = [None] * n
        r0, r1 = prow, prow + dv

        def emit_qk(i):
            b = blocks[i]
            ps, psb = kb.qkbank()
            lo, hi = b["lo"], b["hi"]
            ka, kbuf = b["k"]
            kbufs = kbuf if isinstance(kbuf, list) else [kbuf]
            hb = b.get("bias") is not None
            P.add("pe", lambda e, ps=ps, ka=ka, lo=lo, hi=hi, hb=hb: e.matmul(ps[:, lo:hi], lhsT=ka, rhs=qT[:, lo:hi], start=True, stop=not hb), reads=kbufs + [qbuf], writes=[psb])
            if hb:
                ba, bb = b["bias"]
                P.add("pe", lambda e, ps=ps, ba=ba, lo=lo, hi=hi: e.matmul(ps[:, lo:hi], lhsT=kb.identB[:], rhs=ba, start=False, stop=True), reads=[bb, B("identB")], writes=[psb])
            qk[i] = (ps, psb)

        for j in range(min(3, n)):
            emit_qk(j)
        for i in range(n):
            b = blocks[i]
            lo, hi = b["lo"], b["hi"]
            ps, psb = qk[i]
            pi = kb.rot("pT", 3)
            pT, pTb = self.pT[pi], B("pT%d" % pi)
            P.add("act", lambda e, pT=pT, ps=ps, lo=lo, hi=hi: e.activation(out=pT[:, lo:hi], in_=ps[:, lo:hi], func=AF.Exp, scale=scale), reads=[psb], writes=[pTb])
            if i + 3 < n:
                emit_qk(i + 3)
            va, vb = b["v"]
            P.add("pe", lambda e, va=va, pT=pT, lo=lo, hi=hi, i=i: e.matmul(pso[r0:r1, lo:hi], lhsT=va, rhs=pT[:, lo:hi], start=(i == 0), stop=(i == n - 1)), reads=[vb, pTb], writes=[psob])
            P.add("pe", lambda e, pT=pT, lo=lo, hi=hi, i=i: e.matmul(pss[r0:r1, lo:hi], lhsT=kb.onesB[:, 0:dv], rhs=pT[:, lo:hi], start=(i == 0), stop=(i == n - 1)), reads=[pTb, B("onesB")], writes=[pssb])
        k = kb.rot("rs", 2)
        rs, rsb = kb.rs[k], B("rs%d" % k)
        if sink is not None:
            P.add("act", lambda e: e.activation(out=rs[r0:r1, 0:ncols], in_=pss[r0:r1, 0:ncols], func=AF.Ln, bias=sink, scale=1.0), reads=[pssb, B("sinkE")], writes=[rsb])
        else:
            P.add("act", lambda e: e.activation(out=rs[r0:r1, 0:ncols], in_=pss[r0:r1, 0:ncols], func=AF.Ln), reads=[pssb], writes=[rsb])
        P.add("act", lambda e: e.activation(out=rs[r0:r1, 0:ncols], in_=rs[r0:r1, 0:ncols], func=AF.Exp, scale=-1.0), reads=[rsb], writes=[rsb])
        P.add("dve", lambda e: e.tensor_tensor(out=out_ap, in0=pso[r0:r1, 0:ncols], in1=rs[r0:r1, 0:ncols], op=ALU.mult), reads=[psob, rsb], writes=out_bufs)

    def gqa(self, wqkv, nq, nkv, hd, scale, kofs, vofs, ci, q_norm=None, k_norm=None, rope=False, sink=False, local=None):
        kb, l = self.kb, self.l
        P, B = kb.P, kb.B
        wv = wqkv.rearrange("(k p) n -> p k n", p=128)
        grp = nq // nkv
        hpc = 128 // hd
        nvch = nkv * hd // 128
        ck, cv = kb.cache[l]
        so_k, so_v = kb.so[l]
        qTs = [self.alloc(1536, "qT") for _ in range(2)]
        kTs = [self.alloc(1536, "kT") for _ in range(2)]
        nck = 2
        ckTs = [self.alloc(512, "ckT") for _ in range(nck)]
        vts = [self.alloc(12 * 128, "vt").rearrange("p (t f) -> p t f", t=12) for _ in range(2)]
        cvs = [self.alloc(4 * 128, "cv").rearrange("p (t f) -> p t f", t=4) for _ in range(2)]
        vss = [self.alloc(7 * 128, "vs").rearrange("p (t f) -> p t f", t=7) for _ in range(2)] if local == "nat" else None
        tiles = [t * 128 for t in range(12)]
        if hd < 128:
            for i in range(2):
                P.add("dve", lambda e, i=i: e.memset(qTs[i], 0.0), writes=[B("qT%d_%d" % (i, g)) for g in range(3)])
                P.add("dve", lambda e, i=i: e.memset(kTs[i], 0.0), writes=[B("kT%d_%d" % (i, g)) for g in range(3)])
            for i in range(nck):
                P.add("dve", lambda e, i=i: e.memset(ckTs[i], 0.0), writes=[B("ckT%d" % i)])
        cur_v = None
        pending = []
        import os
        nh_dbg = int(os.environ.get("NATH", nkv))
        skip = os.environ.get("NATSKIP", "")
        for kvh in range(int(os.environ.get("NATH0", 0)), min(nkv, nh_dbg)):
            vch = kvh * hd // 128
            if cur_v != vch and "v" in skip:
                cur_v = vch
                vi = vch % 2
            if cur_v != vch:
                cur_v = vch
                vi = vch % 2
                slot, sbuf = kb.load_w(lambda s, vch=vch: [(s[:, 0:1024].rearrange("p (k n) -> p k n", k=8), wv[:, :, vofs + vch * 128:vofs + (vch + 1) * 128])])
                hl = lambda k, a: (kb.hT[:, k, a:a + 128], [B("hT%d_%d" % (k, gg)) for gg in sorted({a // 512, (a + 127) // 512})])
                self.v_tok(vts[vi], "vt%d" % vi, tiles, hl, 8, slot, sbuf, 128, state=(so_v, vch * 128))
                if vss is not None:
                    stiles = [576 + i * 128 for i in range(7)]
                    self.v_tok(vss[vi], "vs%d" % vi, stiles, hl, 8, slot, sbuf, 128)
                kb.dma_in(None, B("cv%d" % vi), [(cvs[vi], cv[:, vch * 128:(vch + 1) * 128].rearrange("(t p) f -> p t f", p=128))])
            vt, cvt = vts[vi], cvs[vi]
            vcol = (kvh * hd) % 128
            ki = kvh % 2
            cki = kvh % nck
            kT, ckT = kTs[ki], ckTs[cki]
            slot, sbuf = kb.load_w(lambda s, kvh=kvh: [(s[:, 0:1024].rearrange("p (k n) -> p k n", k=8), wv[:, :, kofs + kvh * hd:kofs + kvh * hd + 128])])
            sv = slot[:, 0:1024].rearrange("p (k n) -> p k n", k=8)
            if "k" not in skip:
                pss = self.proj_fm(lambda k: (sv[:, k, :], sbuf), 8, self.hT_rhs, 128)
                self.finish_head(pss, hd, kT, "kT%d" % ki, norm_g=k_norm, rope=rope, state=(so_k[:, kvh * hd:(kvh + 1) * hd] if "o" not in skip else None))
            if "x" not in skip:
                self.load_ctx_T(ck, kvh * hd, hd, ckT[0:hd, :], B("ckT%d" % cki))
            for qh in range(kvh * grp, (kvh + 1) * grp):
                if "q" in skip:
                    continue
                qi = qh % 2
                qT = qTs[qi]
                slot, sbuf = kb.load_w(lambda s, qh=qh: [(s[:, 0:1024].rearrange("p (k n) -> p k n", k=8), wv[:, :, qh * hd:qh * hd + 128])])
                sv = slot[:, 0:1024].rearrange("p (k n) -> p k n", k=8)
                pss = self.proj_fm(lambda k: (sv[:, k, :], sbuf), 8, self.hT_rhs, 128)
                self.finish_head(pss, hd, qT, "qT%d" % qi, norm_g=q_norm, rope=rope)
                och, prow = (qh * hd) // 128, (qh * hd) % 128
                sk = kb.sinkE[prow:prow + hd, qh:qh + 1] if sink else None
                def do_attn(qT=qT, kT=kT, ckT=ckT, vt=vt, cvt=cvt, vcol=vcol, qi=qi, ki=ki, vi=vi, cki=cki, qh=qh, och=och, prow=prow, sk=sk):
                    for s in range(2 if "p" not in skip else 0):
                        blocks = []
                        for i in range(2):
                            tl = 2 * s + i
                            blocks.append(dict(k=(kT[:, tl * 128:(tl + 1) * 128], B("kT%d_0" % ki)), v=(vt[:, tl, vcol:vcol + hd], B("vt%d_%d" % (vi, tl))), lo=0, hi=256))
                        self.attn(qT[:, s * 256:(s + 1) * 256], B("qT%d_0" % qi), 256, blocks, hd, self.oT[prow:prow + hd, och, s * 256:(s + 1) * 256], [B("oT%d_0" % och)], scale, prow=prow, sink=sk)
                    for g in ((1, 2) if "s" not in skip else ()):
                        blocks = []
                        for i in range(4):
                            blocks.append(dict(k=(ckT[:, i * 128:(i + 1) * 128], B("ckT%d" % cki)), v=(cvt[:, i, vcol:vcol + hd], B("cv%d" % vi)), lo=0, hi=512))
                        if local is None:
                            for i in range(8):
                                blocks.append(dict(k=(kT[:, 512 + i * 128:512 + (i + 1) * 128], B("kT%d_%d" % (ki, 1 + i // 4))), v=(vt[:, 4 + i, vcol:vcol + hd], B("vt%d_%d" % (vi, 4 + i))), lo=0, hi=512))
                        elif local == "swa":
                            for bl in range(4):
                                b = (g - 1) * 4 + bl
                                for kbk in (b - 1, b, b + 1):
                                    if kbk < 0 or kbk > 7:
                                        continue
                                    bias = None
                                    if kbk == b - 1:
                                        bias = (kb.band[:, 0, :], B("band"))
                                    elif kbk == b + 1:
                                        bias = (kb.band[:, 1, :], B("band"))
                                    blocks.append(dict(k=(kT[:, 512 + kbk * 128:512 + (kbk + 1) * 128], B("kT%d_%d" % (ki, 1 + kbk // 4))), v=(vt[:, 4 + kbk, vcol:vcol + hd], B("vt%d_%d" % (vi, 4 + kbk))), lo=bl * 128, hi=(bl + 1) * 128, bias=bias))
                        else:
                            for rl in range(8):
                                r = (g - 1) * 8 + rl
                                r0 = min(max(r - 4, 0), 8)
                                for j in range(4):
                                    kr = r0 + 2 * j
                                    kc0 = 512 + kr * 64
                                    kbn = [B("kT%d_%d" % (ki, gg)) for gg in sorted({kc0 // 512, (kc0 + 127) // 512})]
                                    if r0 % 2 == 0:
                                        va = (vt[:, 4 + kr // 2, vcol:vcol + hd], B("vt%d_%d" % (vi, 4 + kr // 2)))
                                    else:
                                        va = (vss[vi][:, (kr - 1) // 2, vcol:vcol + hd], B("vs%d_%d" % (vi, (kr - 1) // 2)))
                                    tt = kr - r + 7
                                    blocks.append(dict(k=(kT[:, kc0:kc0 + 128], kbn), v=va, lo=rl * 64, hi=(rl + 1) * 64, bias=(self.BB[:, qh, tt, :], B("BB"))))
                        self.attn(qT[:, kb.gcols(g)], B("qT%d_%d" % (qi, g)), 512, blocks, hd, self.oT[prow:prow + hd, och, kb.gcols(g)], [B("oT%d_%d" % (och, g))], scale, prow=prow, sink=sk)


                pending.append(do_attn)
                self.inter()
                if len(pending) > 1:
                    pending.pop(0)()
        while pending:
            pending.pop(0)()
    def mix_a(self):
        kb = self.kb
        self.gqa(kb.attn_w_qkv, 8, 2, 128, 128 ** -0.5, 1024, 1280, 0, q_norm=kb.miscT[:, 16:17], k_norm=kb.miscT[:, 17:18], rope=True)

    def mix_c(self):
        kb = self.kb
        self.gqa(kb.swa_w_qkv, 16, 4, 64, 64 ** -0.5, 1024, 1280, 2, rope=True, sink=True, local="swa")

    def mix_d(self):
        kb = self.kb
        P, B = kb.P, kb.B
        import os
        dbgm = os.environ.get("NATDBG", "")
        if "c" in dbgm:
            self.gqa(kb.nat_w_qkv, 16, 16, 64, 64 ** -0.5, 1024, 2048, 3, local=None)
            return
        self.BB = self.alloc(16 * 14 * 64, "BB").rearrange("p (h t c) -> p h t c", h=16, t=14)
        off_save = self.off
        rext = self.alloc(240, "rext")
        natsh = self.alloc(4096, "natsh").rearrange("p (c k) -> p c k", c=64)
        kb.dma_in(None, B("natsh"), [(natsh, kb.c_natsh[:, :].rearrange("p (c k) -> p c k", c=64))])
        for half2 in range(2):
            st, stb = kb.stageF(kb.rot("stage", 2))
            P.add("dve", lambda e, st=st: e.memset(st[:, 0:128], 0.0), writes=[stb])
            kb.dma_in(None, stb, [(st[0:120, 32:63], kb.nat_rpb[half2 * 120:(half2 + 1) * 120, :])], eng="sp")
            ps, psb = kb.bank()
            P.add("pe", lambda e, ps=ps, st=st: e.transpose(out=ps[:, 0:128], in_=st[:, 0:128], identity=kb.identF[:]), reads=[stb, B("identF")], writes=[psb])
            kb.copy("dve", rext[:, half2 * 120:(half2 + 1) * 120], ps[:, 0:120], [psb], [B("rext")])
        rv = rext[:, :].rearrange("p (h d) -> p h d", h=16)
        import os
        dbgm = os.environ.get("NATDBG", "")
        if "a" in dbgm:
            P.add("dve", lambda e: e.memset(self.BB, 0.0), writes=[B("BB")])
        for c in range(64 if "a" not in dbgm else 0):
            ps, psb = kb.bank()
            pv = ps[:, 0:224].rearrange("p (h t) -> p h t", h=16)
            for half in range(2):
                P.add("pe", lambda e, pv=pv, c=c, half=half: e.matmul(pv[half * 64:(half + 1) * 64, :, :], lhsT=natsh[:, c, :], rhs=rv[:, :, half:half + 14], start=True, stop=True),
                      reads=[B("natsh"), B("rext")], writes=[psb])
            P.add("act", lambda e, pv=pv, c=c: e.activation(out=self.BB[:, :, :, c], in_=pv, func=AF.Identity, scale=8.0, bias=kb.natmask[:, c:c + 1]), reads=[psb, B("natmask")], writes=[B("BB")])
        P.fence()
        self.off = off_save
        self.gqa(kb.nat_w_qkv, 16, 16, 64, 64 ** -0.5, 1024, 2048, 3, local=("nat" if "b" not in dbgm else None))

    def mix_b(self):
        kb, l = self.kb, self.l
        P, B = kb.P, kb.B
        scale = 96 ** -0.5
        win = kb.mla_w_in.rearrange("(k p) n -> p k n", p=128)
        wuq = kb.mla_w_uq.rearrange("(k p) n -> p k n", p=128)
        wukv = kb.mla_w_ukv.rearrange("(k p) (h e) -> p k h e", p=128, e=128)
        ckv_c, kpe_c = kb.cache[l]
        so_ckv, so_kpe = kb.so[l]
        cqn = self.alloc(3 * 1536, "cqn").rearrange("p (c t) -> p c t", c=3)
        ckvn = self.alloc(2 * 1536, "ckvn").rearrange("p (c t) -> p c t", c=2)
        kpe = self.alloc(1536, "kpe")
        cckv = self.alloc(2 * 512, "cckv").rearrange("p (c t) -> p c t", c=2)
        ckpe = self.alloc(512, "ckpe")
        P.add("dve", lambda e: e.memset(kpe[:, :], 0.0), writes=[B("kpe_0"), B("kpe_1"), B("kpe_2")])
        P.add("dve", lambda e: e.memset(ckpe[:, :], 0.0), writes=[B("ckpe")])
        slot, sbuf = kb.load_w(lambda s: [(s[:, 0:8 * 384].rearrange("p (k n) -> p k n", k=8), win[:, :, 0:384])])
        sv = slot[:, 0:8 * 384].rearrange("p (k n) -> p k n", k=8)
        for g in range(3):
            chunks = []
            for c in range(3):
                pss = self.proj_fm(lambda k, c=c: (sv[:, k, c * 128:(c + 1) * 128], sbuf), 8, self.hT_rhs, 128, groups=(g,))
                chunks.append(pss[g])
            rs, rsb = kb.colsum_rstd([(ps[:], psb) for (ps, psb) in chunks], 384.0)
            for c, (ps, psb) in enumerate(chunks):
                P.add("dve", lambda e, ps=ps, c=c, g=g, rs=rs: e.scalar_tensor_tensor(out=cqn[:, c, kb.gcols(g)], in0=ps[:], scalar=kb.miscT[:, 18 + c:19 + c], in1=rs[:], op0=ALU.mult, op1=ALU.mult),
                      reads=[psb, rsb, B("miscT")], writes=[B("cqn%d_%d" % (c, g))])
        slot, sbuf = kb.load_w(lambda s: [(s[:, 0:8 * 256].rearrange("p (k n) -> p k n", k=8), win[:, :, 384:640]), (s[:, 2048:2048 + 8 * 32].rearrange("p (k n) -> p k n", k=8), win[:, :, 640:672])])
        sv = slot[:, 0:8 * 256].rearrange("p (k n) -> p k n", k=8)
        svp = slot[:, 2048:2048 + 8 * 32].rearrange("p (k n) -> p k n", k=8)
        for g in range(3):
            chunks = []
            for c in range(2):
                pss = self.proj_fm(lambda k, c=c: (sv[:, k, c * 128:(c + 1) * 128], sbuf), 8, self.hT_rhs, 128, groups=(g,))
                chunks.append(pss[g])
            rs, rsb = kb.colsum_rstd([(ps[:], psb) for (ps, psb) in chunks], 256.0)
            for c, (ps, psb) in enumerate(chunks):
                t, tb = kb.tmp()
                P.add("dve", lambda e, t=t, ps=ps, c=c, rs=rs: e.scalar_tensor_tensor(out=t[:], in0=ps[:], scalar=kb.miscT[:, 21 + c:22 + c], in1=rs[:], op0=ALU.mult, op1=ALU.mult),
                      reads=[psb, rsb, B("miscT")], writes=[tb])
                kb.copy("act", ckvn[:, c, kb.gcols(g)], t[:], [tb], [B("ckvn%d_%d" % (c, g))])
                if g == 0:
                    self.state_out_fm(t[:], tb, 128, so_ckv[:, c * 128:(c + 1) * 128])
        pss = self.proj_fm(lambda k: (svp[:, k, :], sbuf), 8, self.hT_rhs, 32)
        for g, (ps, psb) in pss.items():
            if g == 0:
                t, tb = kb.tmp()
                kb.copy("dve", t[0:32, :], ps[0:32, :], [psb], [tb])
                self.state_out_fm(t[0:32, :], tb, 32, so_kpe[:, 0:32])
                kb.copy("act", kpe[0:32, kb.gcols(0)], t[0:32, :], [tb], [B("kpe_0")])
            else:
                kb.copy("act", kpe[0:32, kb.gcols(g)], ps[0:32, :], [psb], [B("kpe_%d" % g)])
        for c in range(2):
            self.load_ctx_T(ckv_c, c * 128, 128, cckv[:, c, :], B("cckv%d" % c))
        self.load_ctx_T(kpe_c, 0, 32, ckpe[0:32, :], B("ckpe"))
        Qs = [self.alloc(1536, "Qop") for _ in range(2)]
        Ks = [self.alloc(1536, "Kop") for _ in range(2)]
        cKs = [self.alloc(512, "cKop") for _ in range(2)]
        vts = [self.alloc(12 * 128, "vt").rearrange("p (t f) -> p t f", t=12) for _ in range(2)]
        cvs = [self.alloc(4 * 128, "cv").rearrange("p (t f) -> p t f", t=4) for _ in range(2)]
        wk = [self.alloc(2 * 96, "wk").rearrange("p (k n) -> p k n", k=2) for _ in range(2)]
        for i in range(2):
            P.add("dve", lambda e, i=i: e.memset(wk[i][:, :, :], 0.0), writes=[B("wk%d" % i)])
        tiles = [t * 128 for t in range(12)]
        pending = []
        for h in range(16):
            hi_ = h % 2
            if h % 2 == 0:
                vi = (h // 2) % 2
                slot, sbuf = kb.load_w(lambda s, h=h: [(s[:, 0:256].rearrange("p (k a e) -> p k a e", k=2, a=2)[:, :, a, :], wukv[:, :, h + a, 64:128]) for a in range(2)])
                self.v_tok(vts[vi], "vt%d" % vi, tiles, lambda k, a: (ckvn[:, k, a:a + 128], B("ckvn%d_%d" % (k, a // 512))), 2, slot, sbuf, 128)
                self.v_tok(cvs[vi], "cv%d" % vi, [i * 128 for i in range(4)], lambda k, a: (cckv[:, k, a:a + 128], B("cckv%d" % k)), 2, slot, sbuf, 128)
            vt, cvt = vts[vi], cvs[vi]
            vcol = (h % 2) * 64
            kb.dma_in(None, B("wk%d" % hi_), [(wk[hi_][:, :, 0:64], wukv[:, :, h, 0:64])])
            K, cK, Q = Ks[hi_], cKs[hi_], Qs[hi_]
            pss = self.proj_fm(lambda k: (wk[hi_][:, k, :], B("wk%d" % hi_)), 2, lambda k, g: (ckvn[:, k, kb.gcols(g)], B("ckvn%d_%d" % (k, g))), 96,
                               extra=lambda g: [(kb.mlasel[:, :], B("mlasel"), kpe[:, kb.gcols(g)], B("kpe_%d" % g))])
            self.finish_head(pss, 96, K, "Kop%d" % hi_, rope=True)
            pss = self.proj_fm(lambda k: (wk[hi_][:, k, :], B("wk%d" % hi_)), 2, lambda k, g: (cckv[:, k, :], B("cckv%d" % k)), 96, groups=(0,),
                               extra=lambda g: [(kb.mlasel[:, :], B("mlasel"), ckpe[:, :], B("ckpe"))])
            ps, psb = pss[0]
            kb.copy("act", cK[0:96, :], ps[0:96, :], [psb], [B("cKop%d" % hi_)])
            slot, sbuf = kb.load_w(lambda s, h=h: [(s[:, 0:3 * 96].rearrange("p (k n) -> p k n", k=3), wuq[:, :, h * 96:(h + 1) * 96])])
            sv = slot[:, 0:3 * 96].rearrange("p (k n) -> p k n", k=3)
            pss = self.proj_fm(lambda k: (sv[:, k, :], sbuf), 3, lambda k, g: (cqn[:, k, kb.gcols(g)], B("cqn%d_%d" % (k, g))), 96)
            self.finish_head(pss, 96, Q, "Qop%d" % hi_, rope=True)
            def do_attn(Q=Q, K=K, cK=cK, vt=vt, cvt=cvt, vcol=vcol, hi_=hi_, vi=vi, h=h):
                och, prow = h // 2, (h % 2) * 64
                for s in range(2):
                    blocks = []
                    for i in range(2):
                        tl = 2 * s + i
                        blocks.append(dict(k=(K[0:96, tl * 128:(tl + 1) * 128], B("Kop%d_0" % hi_)), v=(vt[:, tl, vcol:vcol + 64], B("vt%d_%d" % (vi, tl))), lo=0, hi=256))
                    self.attn(Q[0:96, s * 256:(s + 1) * 256], B("Qop%d_0" % hi_), 256, blocks, 64, self.oT[prow:prow + 64, och, s * 256:(s + 1) * 256], [B("oT%d_0" % och)], scale, prow=prow)
                for g in (1, 2):
                    blocks = []
                    for i in range(4):
                        blocks.append(dict(k=(cK[0:96, i * 128:(i + 1) * 128], B("cKop%d" % hi_)), v=(cvt[:, i, vcol:vcol + 64], B("cv%d_%d" % (vi, i))), lo=0, hi=512))
                    for i in range(8):
                        blocks.append(dict(k=(K[0:96, 512 + i * 128:512 + (i + 1) * 128], B("Kop%d_%d" % (hi_, 1 + i // 4))), v=(vt[:, 4 + i, vcol:vcol + 64], B("vt%d_%d" % (vi, 4 + i))), lo=0, hi=512))
                    self.attn(Q[0:96, kb.gcols(g)], B("Qop%d_%d" % (hi_, g)), 512, blocks, 64, self.oT[prow:prow + 64, och, kb.gcols(g)], [B("oT%d_%d" % (och, g))], scale, prow=prow)
            pending.append(do_attn)
            self.inter()
            if len(pending) > 1:
                pending.pop(0)()
        while pending:
            pending.pop(0)()

_CACHE = {}


def _get_nc(nl=NL):
    if nl not in _CACHE:
        kb = KB(nl)
        _CACHE[nl] = (kb.build(), kb)
    return _CACHE[nl]


def kernel(x_prompt, x_sample, cache_l0_k, cache_l0_v, cache_l1_ckv, cache_l1_kpe, cache_l2_k, cache_l2_v, cache_l3_k, cache_l3_v,
           c, c_ctx, ada_w, ada_b, norm_g, mlp_w1, mlp_w2, attn_w_qkv, attn_q_norm, attn_k_norm, attn_w_o,
           mla_w_in, mla_q_norm, mla_kv_norm, mla_w_uq, mla_w_ukv, mla_w_o, swa_w_qkv, swa_sink, swa_w_o,
           nat_w_qkv, nat_rpb, nat_w_o, _nl=NL):
    f = lambda a: np.ascontiguousarray(np.asarray(a, dtype=np.float32))
    nc, kb = _get_nc(_nl)
    consts = _consts()
    shared = dict(ada_w=f(ada_w), ada_b=f(ada_b).reshape(192, 128), norm_g=f(norm_g).reshape(128, 128), mlp_w1=f(mlp_w1), mlp_w2=f(mlp_w2),
                  attn_w_qkv=f(attn_w_qkv), attn_w_o=f(attn_w_o), mla_w_in=f(mla_w_in), mla_w_uq=f(mla_w_uq), mla_w_ukv=f(mla_w_ukv), mla_w_o=f(mla_w_o),
                  swa_w_qkv=f(swa_w_qkv), swa_w_o=f(swa_w_o), swa_sink=f(swa_sink).reshape(1, 16), nat_w_qkv=f(nat_w_qkv), nat_w_o=f(nat_w_o),
                  nat_rpb=f(nat_rpb).reshape(240, 31))
    shared.update(consts)
    xp = f(x_prompt); xs = f(x_sample); cc = f(c)
    in_maps = []
    for i in range(8):
        misc = np.zeros((128, 128), np.float32)
        misc[0:8] = cc[i].reshape(8, 128)
        misc[8:16] = f(c_ctx).reshape(8, 128)
        misc[16] = f(attn_q_norm); misc[17] = f(attn_k_norm)
        misc[18:21] = f(mla_q_norm).reshape(3, 128); misc[21:23] = f(mla_kv_norm).reshape(2, 128)
        m = dict(shared)
        m.update(xp=xp[2 * i:2 * i + 2].reshape(512, 1024), xs=xs[i], misc=misc,
                 c0k=f(cache_l0_k)[i].reshape(512, 256), c0v=f(cache_l0_v)[i].reshape(512, 256),
                 c1ckv=f(cache_l1_ckv)[i], c1kpe=f(cache_l1_kpe)[i],
                 c2k=f(cache_l2_k)[i].reshape(512, 256), c2v=f(cache_l2_v)[i].reshape(512, 256),
                 c3k=f(cache_l3_k)[i].reshape(512, 1024), c3v=f(cache_l3_v)[i].reshape(512, 1024))
        in_maps.append(m)
    res = run_bass_kernel_spmd(nc, in_maps, core_ids=list(range(8)))
    r = res.results
    cat = lambda k: np.concatenate([np.asarray(r[i][k], dtype=np.float32) for i in range(8)], axis=0)
    yp = cat("yp").reshape(16, 256, 1024)
    ys = cat("ys").reshape(8, 1024, 1024)
    return (yp, ys,
            cat("o0k").reshape(16, 256, 2, 128), cat("o0v").reshape(16, 256, 2, 128),
            cat("o1ckv").reshape(16, 256, 256), cat("o1kpe").reshape(16, 256, 32),
            cat("o2k").reshape(16, 256, 4, 64), cat("o2v").reshape(16, 256, 4, 64),
            cat("o3k").reshape(16, 256, 16, 64), cat("o3v").reshape(16, 256, 16, 64))
```

```python
import numpy as np
import concourse.bass as bass
import concourse.mybir as mybir
from concourse.bass_utils import run_bass_kernel_spmd
from contextlib import ExitStack

F32 = mybir.dt.float32
BF16 = mybir.dt.bfloat16
AF = mybir.ActivationFunctionType
ALU = mybir.AluOpType
ENGS = ("pe", "act", "dve", "pool", "sp")
NL = 4
EPS = 1e-6
NEG = -30000.0


class Buf:
    __slots__ = ("name", "w", "r", "psum")

    def __init__(self, name):
        self.name = name
        self.w = None
        self.r = []
        self.psum = len(name) == 3 and name.startswith("ps") and name[2].isdigit()


class Op:
    __slots__ = ("eng", "fn", "waits", "marked", "key", "val", "clock", "isdma", "count", "ndma")


class Prog:
    def __init__(self):
        self.eng_ops = {e: [] for e in ENGS}
        self.eclk = {e: {} for e in ENGS}
        self.dmaval = {}
        self.dmalast = {}
        self.fence_id = 0
        self.fence_deps = []
        self.fence_gen = {e: 0 for e in ENGS}

    def fence(self):
        deps = []
        for e in ENGS:
            for op in reversed(self.eng_ops[e]):
                if not op.isdma:
                    deps.append(op)
                    break
        deps.extend(self.dmalast.values())
        self.fence_id += 1
        self.fence_deps = deps

    def add(self, eng, fn, reads=(), writes=(), dma=None, ndma=1):
        op = Op()
        op.eng = eng
        op.fn = fn
        op.marked = False
        op.isdma = dma is not None
        op.ndma = ndma
        deps = []
        if self.fence_gen[eng] < self.fence_id:
            self.fence_gen[eng] = self.fence_id
            for d in self.fence_deps:
                deps.append((d, 0))
        for b in reads:
            if b.w is not None:
                deps.append((b.w, 0))
            if b.psum:
                for r in b.r:
                    if r.eng != eng:
                        deps.append((r, 3))
        for b in writes:
            if b.w is not None:
                deps.append((b.w, 1))
            for r in b.r:
                deps.append((r, 2))
        clk = self.eclk[eng]
        waits = []
        for d, kind in deps:
            if (not d.isdma) and d.eng == eng:
                if eng == "pe" or kind == 2:
                    continue
            if clk.get(d.key, 0) >= d.val:
                continue
            waits.append(d)
            d.marked = True
            for k, v in d.clock.items():
                if clk.get(k, 0) < v:
                    clk[k] = v
        op.waits = waits
        if dma is not None:
            op.key = ("d", dma.name)
            op.val = self.dmaval.get(op.key, 0) + ndma
            self.dmaval[op.key] = op.val
            self.dmalast[op.key] = op
            op.marked = True
        else:
            op.key = eng
            op.val = len(self.eng_ops[eng]) + 1
        c = dict(clk)
        c[op.key] = op.val
        op.clock = c
        self.eng_ops[eng].append(op)
        for b in reads:
            b.r.append(op)
        for b in writes:
            b.w = op
            b.r = []
        return op

    def emit(self, nc, es, final_keys):
        sems = {}

        def sem_of(key):
            if key not in sems:
                nm = "s%d" % len(sems)
                sems[key] = es.enter_context(nc.semaphore(nm))
            return sems[key]

        for e in ENGS:
            cnt = 0
            for op in self.eng_ops[e]:
                if op.isdma:
                    op.count = 16 * op.val
                elif op.marked:
                    cnt += 1
                    op.count = cnt
        block = es.enter_context(nc.Block())
        handles = {"pe": block.tensor, "act": block.scalar, "dve": block.vector, "pool": block.gpsimd, "sp": block.sync}
        prog = self

        def make(ename):
            def body(e):
                for op in prog.eng_ops[ename]:
                    for d in op.waits:
                        e.wait_ge(sem_of(d.key), d.count)
                    r = op.fn(e)
                    if op.isdma:
                        s = sem_of(op.key)
                        for ins in r:
                            ins.then_inc(s, 16)
                    elif op.marked:
                        r.then_inc(sem_of(op.key), 1)
                if ename == "sp":
                    for key in final_keys:
                        e.wait_ge(sem_of(key), 16 * prog.dmaval[key])
            return body

        for ename in ENGS:
            handles[ename](make(ename))
        return len(sems)


def _rope_tables(hd_rot, rows, row0):
    S, GW = 1024, 64
    q = hd_rot // 4
    t = np.arange(S)
    pos = np.stack([t // GW, t % GW], axis=-1).astype(np.float32)
    inv = (np.float32(10000.0) ** (-np.arange(q, dtype=np.float32) / np.float32(q))).astype(np.float32)
    ang = (pos[:, :, None] * inv).astype(np.float32)
    cos = np.ones((rows, S), np.float32)
    sin = np.zeros((rows, S), np.float32)
    sp = np.zeros((rows, rows), np.float32)
    for a in range(2):
        for j in range(2):
            for i in range(q):
                d = row0 + a * 2 * q + j * q + i
                cos[d] = np.cos(ang[:, a, i])
                sin[d] = np.sin(ang[:, a, i])
                if j == 0:
                    sp[d + q, d] = -1.0
                else:
                    sp[d - q, d] = 1.0
    return cos, sin, sp


def _consts():
    c = {}
    c["ident"] = np.eye(128, dtype=np.float32)
    ca, sa, pa = _rope_tables(128, 128, 0)
    cb, sb_, pb = _rope_tables(32, 96, 64)
    cc, sc, pc = _rope_tables(64, 64, 0)
    tab = np.zeros((3, 2, 128, 1024), np.float32)
    spm = np.zeros((3, 128, 128), np.float32)
    tab[0, 0], tab[0, 1], spm[0] = ca, sa, pa
    tab[1, 0, :96], tab[1, 1, :96], spm[1, :96, :96] = cb, sb_, pb
    tab[2, 0, :64], tab[2, 1, :64], spm[2, :64, :64] = cc, sc, pc
    c["ropetab"] = tab
    c["ropesp"] = spm
    k = np.arange(128)[:, None]
    q = np.arange(128)[None, :]
    bm = np.zeros((2, 128, 128), np.float32)
    bm[0] = np.where(q <= k, 0.0, NEG)
    bm[1] = np.where(k <= q, 0.0, NEG)
    c["bandmask"] = bm
    sh = np.zeros((128, 64, 64), np.float32)
    for cc_ in range(64):
        for kc_ in range(64):
            i_ = kc_ - cc_ + 47
            if 0 <= i_ < 128:
                sh[i_, cc_, kc_] = 1.0
    sh = sh.reshape(128, 4096)
    c["natsh"] = sh
    cq = np.arange(64)
    c0 = np.clip(cq - 8, 0, 48)
    kc = np.arange(64)[:, None]
    m = np.where((kc >= c0[None, :]) & (kc < c0[None, :] + 16), 0.0, NEG).astype(np.float32)
    c["natmask"] = np.concatenate([m, m], 0)
    sel = np.zeros((128, 96), np.float32)
    for i in range(32):
        sel[i, 64 + i] = 1.0
    c["mlasel"] = sel
    return c


class KB:
    def __init__(self, nl=NL, dbg=None):
        self.nl = nl
        self.dbg = dbg
        self.nc = bass.Bass("TRN2", target_bir_lowering=False)
        self.P = Prog()
        self.es = ExitStack()
        self.bufs = {}
        self.outkeys = []
        self._rr = {}

    def B(self, name):
        b = self.bufs.get(name)
        if b is None:
            b = self.bufs[name] = Buf(name)
        return b

    def dram(self, name, shape, out=False):
        return self.nc.dram_tensor(name, list(shape), F32, kind="ExternalOutput" if out else "ExternalInput").ap()

    def sb(self, name, shape, dt=F32):
        return self.es.enter_context(self.nc.sbuf_tensor(name, list(shape), dt))

    def rot(self, name, n):
        i = self._rr.get(name, 0)
        self._rr[name] = (i + 1) % n
        return i

    def bank(self):
        i = self.rot("psum", 6)
        return self.ps[i], self.B("ps%d" % i)

    def qkbank(self):
        i = self.rot("qkb", 4)
        return self.ps[i], self.B("ps%d" % i)

    def tmp(self):
        i = self.rot("tmp", 3)
        return self.tmp32[i], self.B("tmp%d" % i)

    def evac_eng(self):
        return ("act", "dve")[self.rot("evac", 2)]

    def dma_in(self, dst_ap, dst_buf, src_aps, eng="pool", reads=()):
        def fn(e, pairs=src_aps):
            return [e.dma_start(out=o, in_=i) for (o, i) in pairs]
        self.P.add(eng, fn, reads=list(reads), writes=[dst_buf], dma=dst_buf, ndma=len(src_aps))

    def dma_out(self, pairs, src_buf, key):
        def fn(e, pairs=pairs):
            return [e.dma_start(out=o, in_=i) for (o, i) in pairs]
        kb = self.B("out_" + key)
        self.P.add("sp", fn, reads=[src_buf], writes=[], dma=src_buf, ndma=len(pairs))
        k = ("d", src_buf.name)
        if k not in self.outkeys:
            self.outkeys.append(k)

    def wslot(self):
        i = self.rot("wsl", 2)
        return self.wsl[i][:], self.B("wsl%d" % i)

    def copy(self, eng, out, in_, reads, writes):
        if eng == "act":
            self.P.add("act", lambda e: e.activation(out=out, in_=in_, func=AF.Copy), reads=reads, writes=writes)
        else:
            self.P.add("dve", lambda e: e.tensor_copy(out=out, in_=in_), reads=reads, writes=writes)

    def build(self):
        nc, P = self.nc, self.P
        D = self.dram
        self.xp = D("xp", [512, 1024]); self.xs = D("xs", [1024, 1024])
        self.cache = [
            (D("c0k", [512, 256]), D("c0v", [512, 256])),
            (D("c1ckv", [512, 256]), D("c1kpe", [512, 32])),
            (D("c2k", [512, 256]), D("c2v", [512, 256])),
            (D("c3k", [512, 1024]), D("c3v", [512, 1024])),
        ]
        self.misc_in = D("misc", [128, 128])
        self.ada_w = D("ada_w", [4, 1024, 6144]); self.ada_b = D("ada_b", [192, 128]); self.norm_g = D("norm_g", [128, 128])
        self.w1 = D("mlp_w1", [4, 1024, 4096]); self.w2 = D("mlp_w2", [4, 4096, 1024])
        self.attn_w_qkv = D("attn_w_qkv", [1024, 1536]); self.attn_w_o = D("attn_w_o", [1024, 1024])
        self.mla_w_in = D("mla_w_in", [1024, 672]); self.mla_w_uq = D("mla_w_uq", [384, 1536])
        self.mla_w_ukv = D("mla_w_ukv", [256, 2048]); self.mla_w_o = D("mla_w_o", [1024, 1024])
        self.swa_w_qkv = D("swa_w_qkv", [1024, 1536]); self.swa_w_o = D("swa_w_o", [1024, 1024]); self.swa_sink = D("swa_sink", [1, 16])
        self.nat_w_qkv = D("nat_w_qkv", [1024, 3072]); self.nat_w_o = D("nat_w_o", [1024, 1024]); self.nat_rpb = D("nat_rpb", [240, 31])
        self.c_ident = D("ident", [128, 128]); self.c_ropetab = D("ropetab", [3, 2, 128, 1024]); self.c_ropesp = D("ropesp", [3, 128, 128])
        self.c_band = D("bandmask", [2, 128, 128]); self.c_natsh = D("natsh", [128, 4096]); self.c_natmask = D("natmask", [128, 64])
        self.c_mlasel = D("mlasel", [128, 96])
        self.yp = D("yp", [512, 1024], True); self.ys = D("ys", [1024, 1024], True)
        self.so = [
            (D("o0k", [512, 256], True), D("o0v", [512, 256], True)),
            (D("o1ckv", [512, 256], True), D("o1kpe", [512, 32], True)),
            (D("o2k", [512, 256], True), D("o2v", [512, 256], True)),
            (D("o3k", [512, 1024], True), D("o3v", [512, 1024], True)),
        ]
        with self.es:
            sb = self.sb
            self.xT = sb("xT", [128, 8, 1536])
            self.hT = sb("hT", [128, 8, 1536], BF16)
            self.R = sb("R", [128, 49152], BF16)
            self.wsl = [sb("wsl%d" % i, [128, 4096], BF16) for i in range(2)]
            self.sq = [sb("sq%d" % i, [128, 512], BF16) for i in range(4)]
            self.tmp32 = [sb("tmp%d" % i, [128, 512]) for i in range(3)]
            self.rs = [sb("rs%d" % i, [128, 512]) for i in range(2)]
            self.rtab = sb("rtab", [128, 2, 1024], BF16)
            self.rsp = sb("rsp", [128, 128], BF16)
            self.identF = sb("identF", [128, 128]); self.identB = sb("identB", [128, 128], BF16); self.onesB = sb("onesB", [128, 128], BF16)
            self.gT = sb("gT", [128, 128]); self.adabT = sb("adabT", [128, 192]); self.miscT = sb("miscT", [128, 32])
            self.siluT = sb("siluT", [128, 8, 2], BF16)
            self.mod = sb("mod", [128, 48, 2]); self.drv = sb("drv", [128, 4, 8, 2])
            self.band = sb("band", [128, 2, 128], BF16); self.sinkE = sb("sinkE", [128, 16])
            self.natmask = sb("natmaskS", [128, 64]); self.mlasel = sb("mlaselS", [128, 96], BF16)
            self.ps = [self.es.enter_context(nc.psum_tensor("ps%d" % i, [128, 512], F32)) for i in range(8)]
            self.Rf = self.R[:].bitcast(F32)
            self.prologue()
            import os
            lys = os.environ.get("LAYERS")
            lyl = [int(c) for c in lys] if lys else list(range(self.nl))
            for i, l in enumerate(lyl):
                self.layer(l, lyl[i + 1] if i + 1 < len(lyl) else None)
            self.epilogue()
            if self.dbg:
                self.P.fence()
                dbg = self.dram("dbg", [128, 2048], True)
                db = self.B("dbgbuf")
                pairs = [(dbg[:, 0:96], self.mod[:].rearrange("p a b -> p (a b)")), (dbg[:, 96:160], self.drv[:].rearrange("p a b c -> p (a b c)")),
                         (dbg[:, 160:288], self.gT[:]), (dbg[:, 288:480], self.adabT[:]), (dbg[:, 480:512], self.miscT[:]),
                         (dbg[:, 512:1024], self.rs[0][:]), (dbg[:, 1024:1536], self.xT[:, 0, 0:512])]
                self.P.add("sp", lambda e: [e.dma_start(out=o, in_=i) for (o, i) in pairs], reads=[], writes=[db], dma=db, ndma=len(pairs))
                self.P.add("pool", lambda e: [e.dma_start(out=dbg[:, 1536:2048], in_=self.hT[:, 0, 0:512])], reads=[], writes=[db], dma=db, ndma=1)
                self.outkeys.append(("d", "dbgbuf"))
            nsem = P.emit(nc, self.es, self.outkeys)
            self.nsem = nsem
        return nc

    def stageF(self, i):
        off = 22528 + i * 1024
        return self.Rf[:, off:off + 1024], self.B("stage%d" % i)

    def prologue(self):
        P = self.P
        B = self.B
        self.dma_in(None, B("identF"), [(self.identF[:], self.c_ident[:, :])], eng="sp")
        self.dma_in(None, B("identB"), [(self.identB[:], self.c_ident[:, :])])
        self.dma_in(None, B("band"), [(self.band[:, i, :], self.c_band[i, :, :]) for i in range(2)])
        self.dma_in(None, B("natmask"), [(self.natmask[:], self.c_natmask[:, :])], eng="sp")
        self.dma_in(None, B("mlasel"), [(self.mlasel[:], self.c_mlasel[:, :])])
        P.add("dve", lambda e: e.memset(self.onesB[:], 1.0), writes=[B("onesB")])
        self.dma_in(None, B("sinkE"), [(self.sinkE[:], self.swa_sink[0:1, :].partition_broadcast(128))], eng="sp")
        P.add("act", lambda e: e.activation(out=self.sinkE[:], in_=self.sinkE[:], func=AF.Exp), reads=[B("sinkE")], writes=[B("sinkE")])
        st, stb = self.stageF(0)
        for (src, rows, dst, dcol, dname) in ((self.norm_g[:, :], 128, self.gT, 0, "gT"), (self.ada_b[0:128, :], 128, self.adabT, 0, "adabT"),
                                               (self.ada_b[128:192, :], 64, self.adabT, 128, "adabT"), (self.misc_in[0:32, :], 32, self.miscT, 0, "miscT")):
            self.dma_in(None, stb, [(st[0:rows, 0:128], src)], eng="sp")
            ps, psb = self.bank()
            P.add("pe", lambda e, ps=ps, rows=rows, st=st: e.transpose(out=ps[:, 0:rows], in_=st[0:rows, 0:128], identity=self.identF[0:rows, 0:rows]),
                  reads=[stb, B("identF")], writes=[psb])
            self.copy("dve", dst[:, dcol:dcol + rows], ps[:, 0:rows], [psb], [B(dname)])
        P.add("act", lambda e: e.activation(out=self.siluT[:, :, 1], in_=self.miscT[:, 0:8], func=AF.Silu), reads=[B("miscT")], writes=[B("siluT")])
        P.add("act", lambda e: e.activation(out=self.siluT[:, :, 0], in_=self.miscT[:, 8:16], func=AF.Silu), reads=[B("miscT")], writes=[B("siluT")])
        for t in range(12):
            src = self.xp[t * 128:(t + 1) * 128, :] if t < 4 else self.xs[(t - 4) * 128:(t - 3) * 128, :]
            st, stb = self.stageF(t % 2)
            self.dma_in(None, stb, [(st, src)], eng="sp")
            g = t // 4
            for half in range(2):
                ps, psb = self.bank()
                for j in range(4):
                    c = half * 4 + j
                    P.add("pe", lambda e, ps=ps, st=st, c=c, j=j: e.transpose(out=ps[:, j * 128:(j + 1) * 128], in_=st[:, c * 128:(c + 1) * 128], identity=self.identF[:]),
                          reads=[stb, B("identF")], writes=[psb])
                self.copy(self.evac_eng(), self.xT[:, half * 4:half * 4 + 4, t * 128:(t + 1) * 128], ps[:].rearrange("p (c t) -> p c t", c=4),
                          [psb], [B("xT%d_%d" % (c, g)) for c in range(half * 4, half * 4 + 4)])
        P.fence()

    def epilogue(self):
        P = self.P
        B = self.B
        P.fence()
        for t in range(12):
            g = t // 4
            st, stb = self.stageF(t % 2)
            for half in range(2):
                ps, psb = self.bank()
                for j in range(4):
                    c = half * 4 + j
                    P.add("pe", lambda e, ps=ps, c=c, j=j, t=t: e.transpose(out=ps[:, j * 128:(j + 1) * 128], in_=self.xT[:, c, t * 128:(t + 1) * 128], identity=self.identF[:]),
                          reads=[B("xT%d_%d" % (c, g)), B("identF")], writes=[psb])
                self.copy(self.evac_eng(), st[:, half * 512:(half + 1) * 512], ps[:], [psb], [stb])
            dst = self.yp[t * 128:(t + 1) * 128, :] if t < 4 else self.ys[(t - 4) * 128:(t - 3) * 128, :]
            self.dma_out([(dst, st)], stb, "y")

    def gcols(self, g):
        return slice(g * 512, (g + 1) * 512)

    def colsum_rstd(self, srcs, nfeat, eps=EPS):
        P, B = self.P, self.B
        psn, psnb = self.ps[7], B("ps7")
        n = len(srcs)
        for i, (ap, buf) in enumerate(srcs):
            rows = ap.shape[0]
            k = self.rot("sq", 4)
            sq, sqb = self.sq[k], B("sq%d" % k)
            P.add("act", lambda e, sq=sq, ap=ap, rows=rows: e.activation(out=sq[0:rows, :], in_=ap, func=AF.Square), reads=[buf], writes=[sqb])
            P.add("pe", lambda e, sq=sq, rows=rows, i=i: e.matmul(psn[:], lhsT=self.onesB[0:rows, :], rhs=sq[0:rows, :], start=(i == 0), stop=(i == n - 1)),
                  reads=[sqb, B("onesB")], writes=[psnb])
        k = self.rot("rs", 2)
        rs, rsb = self.rs[k], B("rs%d" % k)
        P.add("act", lambda e: e.activation(out=rs[:], in_=psn[:], func=AF.Ln, scale=1.0 / nfeat, bias=eps), reads=[psnb], writes=[rsb])
        P.add("act", lambda e: e.activation(out=rs[:], in_=rs[:], func=AF.Exp, scale=-0.5), reads=[rsb], writes=[rsb])
        return rs, rsb

    def norm_mod(self, l, which):
        P, B = self.P, self.B
        ai = 0 if which == 0 else 2
        sh = 0 if which == 0 else 3
        for g in range(3):
            ci = 0 if g == 0 else 1
            cs = self.gcols(g)
            rs, rsb = self.colsum_rstd([(self.xT[:, c, cs], B("xT%d_%d" % (c, g))) for c in range(8)], 1024.0)
            for c in range(8):
                t, tb = self.tmp()
                P.add("dve", lambda e, t=t, c=c, cs=cs, ci=ci, rs=rs: e.scalar_tensor_tensor(out=t[:], in0=self.xT[:, c, cs], scalar=self.drv[:, ai, c, ci:ci + 1], in1=rs[:], op0=ALU.mult, op1=ALU.mult),
                      reads=[B("xT%d_%d" % (c, g)), rsb, B("drv")], writes=[tb])
                P.add("act", lambda e, t=t, c=c, cs=cs, ci=ci: e.activation(out=self.hT[:, c, cs], in_=t[:], func=AF.Identity, bias=self.mod[:, sh * 8 + c, ci:ci + 1], scale=1.0),
                      reads=[tb, B("mod")], writes=[B("hT%d_%d" % (c, g))])

    def post_norm_res(self, l, which, bo_ap, bo_buf):
        P, B = self.P, self.B
        gi = 1 if which == 0 else 3
        for g in range(3):
            ci = 0 if g == 0 else 1
            cs = self.gcols(g)
            rs, rsb = self.colsum_rstd([(bo_ap(c, g), bo_buf(c, g)) for c in range(8)], 1024.0)
            for c in range(8):
                t, tb = self.tmp()
                P.add("dve", lambda e, t=t, c=c, g=g, rs=rs: e.tensor_tensor(out=t[:], in0=bo_ap(c, g), in1=rs[:], op=ALU.mult), reads=[bo_buf(c, g), rsb], writes=[tb])
                xb = B("xT%d_%d" % (c, g))
                P.add("dve", lambda e, t=t, c=c, cs=cs, ci=ci: e.scalar_tensor_tensor(out=self.xT[:, c, cs], in0=t[:], scalar=self.drv[:, gi, c, ci:ci + 1], in1=self.xT[:, c, cs], op0=ALU.mult, op1=ALU.add),
                      reads=[tb, B("drv"), xb], writes=[xb])

    def load_w(self, pairs):
        slot, sbuf = self.wslot()
        self.dma_in(None, sbuf, pairs(slot))
        return slot, sbuf

    def mod_units(self, l, ring=None):
        P, B = self.P, self.B
        psm, psmb = self.ps[6], B("ps6")
        wv = self.ada_w[l].rearrange("(k p) n -> p k n", p=128)
        if ring is None:
            ring = [self.wsl[i][:, j * 1024:(j + 1) * 1024] for i in range(2) for j in range(4)]
            names = ["mring%d" % i for i in range(8)]
        else:
            names = ["ring%d" % i for i in range(len(ring))]
        D = len(ring)
        bufs = [r.rearrange("p (k n) -> p k n", k=8) for r in ring]

        def load(i):
            self.dma_in(None, B(names[i % D]), [(bufs[i % D], wv[:, :, i * 128:(i + 1) * 128])])

        def consume(i):
            bv = bufs[i % D]
            for k in range(8):
                P.add("pe", lambda e, bv=bv, k=k, i=i: e.matmul(psm[:, 2 * i:2 * i + 2], lhsT=bv[:, k, :], rhs=self.siluT[:, k, :], start=(k == 0), stop=(k == 7)),
                      reads=[B(names[i % D]), B("siluT")], writes=[psmb])

        units = []
        for i in range(48 + D):
            def unit(i=i):
                if 0 <= i - D < 48:
                    consume(i - D)
                if i < 48:
                    load(i)
            units.append(unit)
        return units

    def modulation_finish(self, l):
        P, B = self.P, self.B
        psm, psmb = self.ps[6], B("ps6")
        pv = psm[:, 0:96].rearrange("p (j c) -> p j c", c=2)
        for ci in range(2):
            P.add("dve", lambda e, ci=ci: e.tensor_tensor(out=self.mod[:, :, ci], in0=pv[:, :, ci], in1=self.adabT[:, l * 48:(l + 1) * 48], op=ALU.add),
                  reads=[psmb, B("adabT")], writes=[B("mod")])
        for ci in range(2):
            for (di, mi, gi, plus1) in ((0, 1, 0, True), (1, 2, 1, False), (2, 4, 2, True), (3, 5, 3, False)):
                gcol = self.gT[:, (l * 4 + gi) * 8:(l * 4 + gi) * 8 + 8]
                if plus1:
                    P.add("dve", lambda e, di=di, mi=mi, ci=ci, gcol=gcol: e.scalar_tensor_tensor(out=self.drv[:, di, :, ci], in0=self.mod[:, mi * 8:mi * 8 + 8, ci], scalar=1.0, in1=gcol, op0=ALU.add, op1=ALU.mult),
                          reads=[B("mod"), B("gT")], writes=[B("drv")])
                else:
                    P.add("dve", lambda e, di=di, mi=mi, ci=ci, gcol=gcol: e.tensor_tensor(out=self.drv[:, di, :, ci], in0=self.mod[:, mi * 8:mi * 8 + 8, ci], in1=gcol, op=ALU.mult),
                          reads=[B("mod"), B("gT")], writes=[B("drv")])

    def mlp(self, l, units=()):
        P, B = self.P, self.B
        units = list(units)
        per = (len(units) + 13) // 14 if units else 0

        def interleave():
            for _ in range(per):
                if units:
                    units.pop(0)()
        h1 = self.R[:].rearrange("p (c t) -> p c t", c=32)
        w1v = self.w1[l].rearrange("(k p) n -> p k n", p=128)
        for pc in range(8):
            interleave()
            slot, sbuf = self.load_w(lambda s, pc=pc: [(s.rearrange("p (k n) -> p k n", k=8), w1v[:, :, pc * 512:(pc + 1) * 512])])
            sv = slot.rearrange("p (k n) -> p k n", k=8)
            for j in range(4):
                oc = pc * 4 + j
                banks = [self.bank() for _ in range(3)]
                for k in range(8):
                    for g in range(3):
                        ps, psb = banks[g]
                        P.add("pe", lambda e, ps=ps, sv=sv, j=j, k=k, g=g: e.matmul(ps[:], lhsT=sv[:, k, j * 128:(j + 1) * 128], rhs=self.hT[:, k, self.gcols(g)], start=(k == 0), stop=(k == 7)),
                              reads=[sbuf, B("hT%d_%d" % (k, g))], writes=[psb])
                for g in range(3):
                    ps, psb = banks[g]
                    t, tb = self.tmp()
                    P.add("act", lambda e, ps=ps, t=t: e.activation(out=t[:], in_=ps[:], func=AF.Relu), reads=[psb], writes=[tb])
                    P.add("dve", lambda e, t=t, oc=oc, g=g: e.tensor_tensor(out=h1[:, oc, self.gcols(g)], in0=t[:], in1=t[:], op=ALU.mult), reads=[tb], writes=[B("h1_%d_%d" % (oc, g))])
        w2v = self.w2[l].rearrange("(k p) n -> p k n", p=128)
        for dc in range(8):
            interleave()
            slot, sbuf = self.load_w(lambda s, dc=dc: [(s.rearrange("p (k n) -> p k n", k=32), w2v[:, :, dc * 128:(dc + 1) * 128])])
            sv = slot.rearrange("p (k n) -> p k n", k=32)
            banks = [self.bank() for _ in range(3)]
            for k in range(32):
                for g in range(3):
                    ps, psb = banks[g]
                    P.add("pe", lambda e, ps=ps, sv=sv, k=k, g=g: e.matmul(ps[:], lhsT=sv[:, k, :], rhs=h1[:, k, self.gcols(g)], start=(k == 0), stop=(k == 31)),
                          reads=[sbuf, B("h1_%d_%d" % (k, g))], writes=[psb])
            for g in range(3):
                ps, psb = banks[g]
                self.copy(self.evac_eng(), self.hT[:, dc, self.gcols(g)], ps[:], [psb], [B("hT%d_%d" % (dc, g))])
        while units:
            units.pop(0)()
        self.post_norm_res(l, 1, lambda c, g: self.hT[:, c, self.gcols(g)], lambda c, g: B("hT%d_%d" % (c, g)))

    def layer(self, l, nxt=None):
        P = self.P
        if getattr(self, "mod_ready", None) != l:
            for u in self.mod_units(l):
                u()
            P.fence()
        self.modulation_finish(l)
        self.norm_mod(l, 0)
        if self.dbg == "norm0":
            return
        self.mod_ready = None
        self.prefetch_next = nxt if (nxt is not None and l % 4 != 3) else None
        self.mixer(l)
        P.fence()
        self.norm_mod(l, 1)
        self.mlp(l)
        P.fence()

    def mixer(self, l):
        from_kernel_mixers(self, l)


def from_kernel_mixers(kb, l):
    Mixer(kb, l).run()


class Mixer:
    def __init__(self, kb, l):
        self.kb = kb
        self.l = l
        self.off = 12288

    def inter(self):
        for _ in range(self.per):
            if self.units:
                self.units.pop(0)()

    def alloc(self, n, name):
        o = self.off
        self.off += n
        assert self.off <= 45056, (name, self.off)
        return self.kb.R[:, o:o + n]

    def run(self):
        kb, l = self.kb, self.l
        P, B = kb.P, kb.B
        kind = l % 4
        self.oT = kb.R[:, 0:12288].rearrange("p (c t) -> p c t", c=8)
        self.pT = [self.alloc(512, "pT") for _ in range(3)]
        if kind != 3:
            self.ra = [self.alloc(512, "ra") for _ in range(2)]
            self.rb = [self.alloc(512, "rb") for _ in range(2)]
        if kind in (0, 1, 2):
            ti = kind
            kb.dma_in(None, B("rtab"), [(kb.rtab[:, i, :], kb.c_ropetab[ti, i, :, :]) for i in range(2)])
            kb.dma_in(None, B("rsp"), [(kb.rsp[:], kb.c_ropesp[ti, :, :])])
        self.units = []
        self.per = 0
        if kb.prefetch_next is not None:
            ring = [self.alloc(1024, "ring") for _ in range(6)]
            self.units = kb.mod_units(kb.prefetch_next, ring)
            nheads = 8 if kind == 0 else 16
            self.per = (len(self.units) + nheads - 1) // nheads
            kb.mod_ready = kb.prefetch_next
        [self.mix_a, self.mix_b, self.mix_c, self.mix_d][kind]()
        while self.units:
            self.units.pop(0)()
        P.fence()
        wo = [kb.attn_w_o, kb.mla_w_o, kb.swa_w_o, kb.nat_w_o][kind]
        wov = wo.rearrange("(k p) n -> p k n", p=128)
        bo = kb.R[:, 12288:24576].rearrange("p (c t) -> p c t", c=8)
        for pc in range(2):
            slot, sbuf = kb.load_w(lambda s, pc=pc: [(s.rearrange("p (k n) -> p k n", k=8), wov[:, :, pc * 512:(pc + 1) * 512])])
            sv = slot.rearrange("p (k n) -> p k n", k=8)
            for j in range(4):
                oc = pc * 4 + j
                banks = [kb.bank() for _ in range(3)]
                for k in range(8):
                    for g in range(3):
                        ps, psb = banks[g]
                        P.add("pe", lambda e, ps=ps, sv=sv, j=j, k=k, g=g: e.matmul(ps[:], lhsT=sv[:, k, j * 128:(j + 1) * 128], rhs=self.oT[:, k, kb.gcols(g)], start=(k == 0), stop=(k == 7)),
                              reads=[sbuf, B("oT%d_%d" % (k, g))], writes=[psb])
                for g in range(3):
                    ps, psb = banks[g]
                    kb.copy(kb.evac_eng(), bo[:, oc, kb.gcols(g)], ps[:], [psb], [B("bo%d_%d" % (oc, g))])
        kb.post_norm_res(l, 0, lambda c, g: bo[:, c, kb.gcols(g)], lambda c, g: B("bo%d_%d" % (c, g)))

    def proj_fm(self, lhs, nk, rhs, rows, groups=(0, 1, 2), extra=None):
        kb = self.kb
        P = kb.P
        out = {}
        for g in groups:
            out[g] = kb.bank()
        for k in range(nk):
            la, lb = lhs(k)
            for g in groups:
                ps, psb = out[g]
                ra_, rb_ = rhs(k, g)
                ex = extra(g) if extra else []
                last = (k == nk - 1) and not ex
                P.add("pe", lambda e, ps=ps, la=la, ra_=ra_, k=k, last=last: e.matmul(ps[0:rows, :], lhsT=la, rhs=ra_, start=(k == 0), stop=last),
                      reads=[lb, rb_], writes=[psb])
        if extra:
            for g in groups:
                ps, psb = out[g]
                ex = extra(g)
                for i, (la, lb, ra_, rb_) in enumerate(ex):
                    P.add("pe", lambda e, ps=ps, la=la, ra_=ra_, i=i, n=len(ex): e.matmul(ps[0:rows, :], lhsT=la, rhs=ra_, start=False, stop=(i == n - 1)),
                          reads=[lb, rb_], writes=[psb])
        return out

    def hT_rhs(self, k, g):
        kb = self.kb
        return kb.hT[:, k, kb.gcols(g)], kb.B("hT%d_%d" % (k, g))

    def finish_head(self, pss, rows, dst, dstname, norm_g=None, rope=False, state=None):
        kb = self.kb
        P, B = kb.P, kb.B
        for g, (ps, psb) in pss.items():
            src, srcb = ps[0:rows, :], psb
            if norm_g is not None:
                rs, rsb = kb.colsum_rstd([(ps[0:rows, :], psb)], float(rows))
                t, tb = kb.tmp()
                P.add("dve", lambda e, t=t, ps=ps, rs=rs: e.scalar_tensor_tensor(out=t[0:rows, :], in0=ps[0:rows, :], scalar=norm_g, in1=rs[0:rows, :], op0=ALU.mult, op1=ALU.mult),
                      reads=[psb, rsb, B("miscT")], writes=[tb])
                src, srcb = t[0:rows, :], tb
            if state is not None and g == 0:
                if norm_g is None:
                    t, tb = kb.tmp()
                    kb.copy("dve", t[0:rows, :], ps[0:rows, :], [psb], [tb])
                    src, srcb = t[0:rows, :], tb
                self.state_out_fm(src, srcb, rows, state)
            dcols = kb.gcols(g)
            db = B("%s_%d" % (dstname, g))
            if rope and g >= 1:
                i = kb.rot("rope", 2)
                ra, rb = self.ra[i], self.rb[i]
                rab, rbb = B("ra%d" % i), B("rb%d" % i)
                tc = slice((g - 1) * 512, g * 512)
                P.add("dve", lambda e, ra=ra, src=src, tc=tc: e.tensor_tensor(out=ra[0:rows, :], in0=src, in1=kb.rtab[0:rows, 0, tc], op=ALU.mult), reads=[srcb, B("rtab")], writes=[rab])
                P.add("dve", lambda e, rb=rb, src=src, tc=tc: e.tensor_tensor(out=rb[0:rows, :], in0=src, in1=kb.rtab[0:rows, 1, tc], op=ALU.mult), reads=[srcb, B("rtab")], writes=[rbb])
                pr, prb = kb.bank()
                P.add("pe", lambda e, pr=pr, ra=ra: e.matmul(pr[0:rows, :], lhsT=kb.identB[0:rows, 0:rows], rhs=ra[0:rows, :], start=True, stop=False), reads=[rab, B("identB")], writes=[prb])
                P.add("pe", lambda e, pr=pr, rb=rb: e.matmul(pr[0:rows, :], lhsT=kb.rsp[0:rows, 0:rows], rhs=rb[0:rows, :], start=False, stop=True), reads=[rbb, B("rsp")], writes=[prb])
                kb.copy("act", dst[0:rows, dcols], pr[0:rows, :], [prb], [db])
            else:
                kb.copy("act", dst[0:rows, dcols], src, [srcb], [db])

    def state_out_fm(self, src, srcb, rows, dram_view):
        kb = self.kb
        P, B = kb.P, kb.B
        ps, psb = kb.bank()
        for t in range(4):
            P.add("pe", lambda e, t=t: e.transpose(out=ps[:, t * rows:(t + 1) * rows], in_=src[:, t * 128:(t + 1) * 128], identity=kb.identF[0:rows, 0:rows]),
                  reads=[srcb, B("identF")], writes=[psb])
        st, stb = kb.stageF(kb.rot("stage", 2))
        kb.copy("dve", st[:, 0:4 * rows], ps[:, 0:4 * rows], [psb], [stb])
        kb.dma_out([(dram_view.rearrange("(t p) f -> p t f", p=128), st[:, 0:4 * rows].rearrange("p (t f) -> p t f", t=4))], stb, "state")

    def v_tok(self, dst, dstname, tiles, lhs, nk, wslot, wbuf, ncols, state=None):
        kb = self.kb
        P, B = kb.P, kb.B
        wv = wslot[:, 0:nk * ncols].rearrange("p (k n) -> p k n", k=nk)
        for i, a0 in enumerate(tiles):
            nb = (ncols + 511) // 512
            banks = [kb.bank() for _ in range(nb)]
            for k in range(nk):
                la, lb = lhs(k, a0)
                lb = lb if isinstance(lb, list) else [lb]
                for b in range(nb):
                    ps, psb = banks[b]
                    w = min(512, ncols - b * 512)
                    P.add("pe", lambda e, ps=ps, la=la, k=k, b=b, w=w: e.matmul(ps[:, 0:w], lhsT=la, rhs=wv[:, k, b * 512:b * 512 + w], start=(k == 0), stop=(k == nk - 1)),
                          reads=lb + [wbuf], writes=[psb])
            for b in range(nb):
                ps, psb = banks[b]
                w = min(512, ncols - b * 512)
                kb.copy("act", dst[:, i, b * 512:b * 512 + w], ps[:, 0:w], [psb], [B("%s_%d" % (dstname, i))])
                if state is not None and i < 4:
                    dram, cofs = state
                    st, stb = kb.stageF(kb.rot("stage", 2))
                    kb.copy("dve", st[:, 0:w], ps[:, 0:w], [psb], [stb])
                    kb.dma_out([(dram[i * 128:(i + 1) * 128, cofs + b * 512:cofs + b * 512 + w], st[:, 0:w])], stb, "state")

    def load_ctx_T(self, dram, ncol0, rows, dst, dstbuf):
        kb = self.kb
        P, B = kb.P, kb.B
        st, stb = kb.stageF(kb.rot("stage", 2))
        sv = st.rearrange("p (t f) -> p t f", t=4)
        if rows < 128:
            P.add("dve", lambda e, st=st: e.memset(st, 0.0), writes=[stb])
        kb.dma_in(None, stb, [(sv[:, :, 0:rows], dram[:, ncol0:ncol0 + rows].rearrange("(t p) f -> p t f", p=128))], eng="sp")
        ps, psb = kb.bank()
        for t in range(4):
            P.add("pe", lambda e, t=t: e.transpose(out=ps[:, t * 128:(t + 1) * 128], in_=sv[:, t, 0:128], identity=kb.identF[:]),
                  reads=[stb, B("identF")], writes=[psb])
        kb.copy("dve", dst, ps[0:rows, :], [psb], [dstbuf])

    def attn(self, qT, qbuf, ncols, blocks, dv, out_ap, out_bufs, scale, prow=0, sink=None):
        kb = self.kb
        P, B = kb.P, kb.B
        pso, psob = kb.ps[4], B("ps4")
        pss, pssb = kb.ps[5], B("ps5")
        n = len(blocks)
        qk = [None] * n
        r0, r1 = prow, prow + dv

        def emit_qk(i):
            b = blocks[i]
            ps, psb = kb.qkbank()
            lo, hi = b["lo"], b["hi"]
            ka, kbuf = b["k"]
            kbufs = kbuf if isinstance(kbuf, list) else [kbuf]
            bl = list(b.get("biases") or [])
            if b.get("bias") is not None:
                bl.append((b["bias"][0], b["bias"][1], lo, hi))
            hb = len(bl) > 0
            P.add("pe", lambda e, ps=ps, ka=ka, lo=lo, hi=hi, hb=hb: e.matmul(ps[:, lo:hi], lhsT=ka, rhs=qT[:, lo:hi], start=True, stop=not hb), reads=kbufs + [qbuf], writes=[psb])
            for bi, (ba, bb, blo, bhi) in enumerate(bl):
                P.add("pe", lambda e, ps=ps, ba=ba, blo=blo, bhi=bhi, last=(bi == len(bl) - 1): e.matmul(ps[:, blo:bhi], lhsT=kb.identB[:], rhs=ba, start=False, stop=last), reads=[bb, B("identB")], writes=[psb])
            qk[i] = (ps, psb)

        for j in range(min(3, n)):
            emit_qk(j)
        for i in range(n):
            b = blocks[i]
            lo, hi = b["lo"], b["hi"]
            ps, psb = qk[i]
            pi = kb.rot("pT", 3)
            pT, pTb = self.pT[pi], B("pT%d" % pi)
            P.add("act", lambda e, pT=pT, ps=ps, lo=lo, hi=hi: e.activation(out=pT[:, lo:hi], in_=ps[:, lo:hi], func=AF.Exp, scale=scale), reads=[psb], writes=[pTb])
            if i + 3 < n:
                emit_qk(i + 3)
            va, vb = b["v"]
            P.add("pe", lambda e, va=va, pT=pT, lo=lo, hi=hi, i=i: e.matmul(pso[r0:r1, lo:hi], lhsT=va, rhs=pT[:, lo:hi], start=(i == 0), stop=(i == n - 1)), reads=[vb, pTb], writes=[psob])
            P.add("pe", lambda e, pT=pT, lo=lo, hi=hi, i=i: e.matmul(pss[r0:r1, lo:hi], lhsT=kb.onesB[:, 0:dv], rhs=pT[:, lo:hi], start=(i == 0), stop=(i == n - 1)), reads=[pTb, B("onesB")], writes=[pssb])
        k = kb.rot("rs", 2)
        rs, rsb = kb.rs[k], B("rs%d" % k)
        if sink is not None:
            P.add("act", lambda e: e.activation(out=rs[r0:r1, 0:ncols], in_=pss[r0:r1, 0:ncols], func=AF.Ln, bias=sink, scale=1.0), reads=[pssb, B("sinkE")], writes=[rsb])
        else:
            P.add("act", lambda e: e.activation(out=rs[r0:r1, 0:ncols], in_=pss[r0:r1, 0:ncols], func=AF.Ln), reads=[pssb], writes=[rsb])
        P.add("act", lambda e: e.activation(out=rs[r0:r1, 0:ncols], in_=rs[r0:r1, 0:ncols], func=AF.Exp, scale=-1.0), reads=[rsb], writes=[rsb])
        P.add("dve", lambda e: e.tensor_tensor(out=out_ap, in0=pso[r0:r1, 0:ncols], in1=rs[r0:r1, 0:ncols], op=ALU.mult), reads=[psob, rsb], writes=out_bufs)

    def gqa(self, wqkv, nq, nkv, hd, scale, kofs, vofs, ci, q_norm=None, k_norm=None, rope=False, sink=False, local=None):
        kb, l = self.kb, self.l
        P, B = kb.P, kb.B
        wv = wqkv.rearrange("(k p) n -> p k n", p=128)
        grp = nq // nkv
        hpc = 128 // hd
        nvch = nkv * hd // 128
        ck, cv = kb.cache[l]
        so_k, so_v = kb.so[l]
        qTs = [self.alloc(1536, "qT") for _ in range(2)]
        kTs = [self.alloc(1536, "kT") for _ in range(2)]
        nck = 2
        ckTs = [self.alloc(512, "ckT") for _ in range(nck)]
        vts = [self.alloc(12 * 128, "vt").rearrange("p (t f) -> p t f", t=12) for _ in range(2)]
        cvs = [self.alloc(4 * 128, "cv").rearrange("p (t f) -> p t f", t=4) for _ in range(2)]
        vss = [self.alloc(7 * 128, "vs").rearrange("p (t f) -> p t f", t=7) for _ in range(2)] if local == "nat" else None
        tiles = [t * 128 for t in range(12)]
        if hd < 128:
            for i in range(2):
                P.add("dve", lambda e, i=i: e.memset(qTs[i], 0.0), writes=[B("qT%d_%d" % (i, g)) for g in range(3)])
                P.add("dve", lambda e, i=i: e.memset(kTs[i], 0.0), writes=[B("kT%d_%d" % (i, g)) for g in range(3)])
            for i in range(nck):
                P.add("dve", lambda e, i=i: e.memset(ckTs[i], 0.0), writes=[B("ckT%d" % i)])
        cur_v = None
        pending = []
        import os
        nh_dbg = int(os.environ.get("NATH", nkv))
        skip = os.environ.get("NATSKIP", "")
        for kvh in range(int(os.environ.get("NATH0", 0)), min(nkv, nh_dbg)):
            vch = kvh * hd // 128
            if cur_v != vch and "v" in skip:
                cur_v = vch
                vi = vch % 2
            if cur_v != vch:
                cur_v = vch
                vi = vch % 2
                slot, sbuf = kb.load_w(lambda s, vch=vch: [(s[:, 0:1024].rearrange("p (k n) -> p k n", k=8), wv[:, :, vofs + vch * 128:vofs + (vch + 1) * 128])])
                hl = lambda k, a: (kb.hT[:, k, a:a + 128], [B("hT%d_%d" % (k, gg)) for gg in sorted({a // 512, (a + 127) // 512})])
                self.v_tok(vts[vi], "vt%d" % vi, tiles, hl, 8, slot, sbuf, 128, state=(so_v, vch * 128))
                if vss is not None:
                    stiles = [576 + i * 128 for i in range(7)]
                    self.v_tok(vss[vi], "vs%d" % vi, stiles, hl, 8, slot, sbuf, 128)
                kb.dma_in(None, B("cv%d" % vi), [(cvs[vi], cv[:, vch * 128:(vch + 1) * 128].rearrange("(t p) f -> p t f", p=128))])
            vt, cvt = vts[vi], cvs[vi]
            vcol = (kvh * hd) % 128
            ki = kvh % 2
            cki = kvh % nck
            kT, ckT = kTs[ki], ckTs[cki]
            slot, sbuf = kb.load_w(lambda s, kvh=kvh: [(s[:, 0:1024].rearrange("p (k n) -> p k n", k=8), wv[:, :, kofs + kvh * hd:kofs + kvh * hd + 128])])
            sv = slot[:, 0:1024].rearrange("p (k n) -> p k n", k=8)
            if "k" not in skip:
                pss = self.proj_fm(lambda k: (sv[:, k, :], sbuf), 8, self.hT_rhs, 128)
                self.finish_head(pss, hd, kT, "kT%d" % ki, norm_g=k_norm, rope=rope, state=(so_k[:, kvh * hd:(kvh + 1) * hd] if "o" not in skip else None))
            if "x" not in skip:
                self.load_ctx_T(ck, kvh * hd, hd, ckT[0:hd, :], B("ckT%d" % cki))
            for qh in range(kvh * grp, (kvh + 1) * grp):
                if "q" in skip:
                    continue
                qi = qh % 2
                qT = qTs[qi]
                slot, sbuf = kb.load_w(lambda s, qh=qh: [(s[:, 0:1024].rearrange("p (k n) -> p k n", k=8), wv[:, :, qh * hd:qh * hd + 128])])
                sv = slot[:, 0:1024].rearrange("p (k n) -> p k n", k=8)
                pss = self.proj_fm(lambda k: (sv[:, k, :], sbuf), 8, self.hT_rhs, 128)
                self.finish_head(pss, hd, qT, "qT%d" % qi, norm_g=q_norm, rope=rope)
                och, prow = (qh * hd) // 128, (qh * hd) % 128
                sk = kb.sinkE[prow:prow + hd, qh:qh + 1] if sink else None
                def do_attn(qT=qT, kT=kT, ckT=ckT, vt=vt, cvt=cvt, vcol=vcol, qi=qi, ki=ki, vi=vi, cki=cki, qh=qh, och=och, prow=prow, sk=sk):
                    for s in range(2 if "p" not in skip else 0):
                        blocks = []
                        for i in range(2):
                            tl = 2 * s + i
                            blocks.append(dict(k=(kT[:, tl * 128:(tl + 1) * 128], B("kT%d_0" % ki)), v=(vt[:, tl, vcol:vcol + hd], B("vt%d_%d" % (vi, tl))), lo=0, hi=256))
                        self.attn(qT[:, s * 256:(s + 1) * 256], B("qT%d_0" % qi), 256, blocks, hd, self.oT[prow:prow + hd, och, s * 256:(s + 1) * 256], [B("oT%d_0" % och)], scale, prow=prow, sink=sk)
                    for g in ((1, 2) if "s" not in skip else ()):
                        blocks = []
                        for i in range(4):
                            blocks.append(dict(k=(ckT[:, i * 128:(i + 1) * 128], B("ckT%d" % cki)), v=(cvt[:, i, vcol:vcol + hd], B("cv%d" % vi)), lo=0, hi=512))
                        if local is None:
                            for i in range(8):
                                blocks.append(dict(k=(kT[:, 512 + i * 128:512 + (i + 1) * 128], B("kT%d_%d" % (ki, 1 + i // 4))), v=(vt[:, 4 + i, vcol:vcol + hd], B("vt%d_%d" % (vi, 4 + i))), lo=0, hi=512))
                        elif local == "swa":
                            b0 = (g - 1) * 4
                            for kbk in range(max(b0 - 1, 0), min(b0 + 4, 7) + 1):
                                qb_lo, qb_hi = max(kbk - 1, b0), min(kbk + 1, b0 + 3)
                                biases = []
                                for qb in range(qb_lo, qb_hi + 1):
                                    if kbk == qb - 1:
                                        biases.append((kb.band[:, 0, :], B("band"), (qb - b0) * 128, (qb - b0 + 1) * 128))
                                    elif kbk == qb + 1:
                                        biases.append((kb.band[:, 1, :], B("band"), (qb - b0) * 128, (qb - b0 + 1) * 128))
                                blocks.append(dict(k=(kT[:, 512 + kbk * 128:512 + (kbk + 1) * 128], B("kT%d_%d" % (ki, 1 + kbk // 4))), v=(vt[:, 4 + kbk, vcol:vcol + hd], B("vt%d_%d" % (vi, 4 + kbk))), lo=(qb_lo - b0) * 128, hi=(qb_hi - b0 + 1) * 128, biases=biases))
                        else:
                            rows = [(g - 1) * 8 + rl for rl in range(8)]
                            buckets = []
                            for r in rows:
                                r0 = min(max(r - 4, 0), 8)
                                if buckets and buckets[-1][0] == r0:
                                    buckets[-1][1].append(r)
                                else:
                                    buckets.append((r0, [r]))
                            for r0, rs_ in buckets:
                                lo_ = (rs_[0] - (g - 1) * 8) * 64
                                hi_q = (rs_[-1] - (g - 1) * 8 + 1) * 64
                                for j in range(4):
                                    kr = r0 + 2 * j
                                    kc0 = 512 + kr * 64
                                    kbn = [B("kT%d_%d" % (ki, gg)) for gg in sorted({kc0 // 512, (kc0 + 127) // 512})]
                                    if r0 % 2 == 0:
                                        va = (vt[:, 4 + kr // 2, vcol:vcol + hd], B("vt%d_%d" % (vi, 4 + kr // 2)))
                                    else:
                                        va = (vss[vi][:, (kr - 1) // 2, vcol:vcol + hd], B("vs%d_%d" % (vi, (kr - 1) // 2)))
                                    biases = [(self.BB[:, qh, kr - r + 7, :], B("BB"), (r - (g - 1) * 8) * 64, (r - (g - 1) * 8 + 1) * 64) for r in rs_]
                                    blocks.append(dict(k=(kT[:, kc0:kc0 + 128], kbn), v=va, lo=lo_, hi=hi_q, biases=biases))
                        self.attn(qT[:, kb.gcols(g)], B("qT%d_%d" % (qi, g)), 512, blocks, hd, self.oT[prow:prow + hd, och, kb.gcols(g)], [B("oT%d_%d" % (och, g))], scale, prow=prow, sink=sk)


                pending.append(do_attn)
                self.inter()
                if len(pending) > 1:
                    pending.pop(0)()
        while pending:
            pending.pop(0)()
    def mix_a(self):
        kb = self.kb
        self.gqa(kb.attn_w_qkv, 8, 2, 128, 128 ** -0.5, 1024, 1280, 0, q_norm=kb.miscT[:, 16:17], k_norm=kb.miscT[:, 17:18], rope=True)

    def mix_c(self):
        kb = self.kb
        self.gqa(kb.swa_w_qkv, 16, 4, 64, 64 ** -0.5, 1024, 1280, 2, rope=True, sink=True, local="swa")

    def mix_d(self):
        kb = self.kb
        P, B = kb.P, kb.B
        import os
        dbgm = os.environ.get("NATDBG", "")
        if "c" in dbgm:
            self.gqa(kb.nat_w_qkv, 16, 16, 64, 64 ** -0.5, 1024, 2048, 3, local=None)
            return
        self.BB = self.alloc(16 * 14 * 64, "BB").rearrange("p (h t c) -> p h t c", h=16, t=14)
        off_save = self.off
        rext = self.alloc(240, "rext")
        natsh = self.alloc(4096, "natsh").rearrange("p (c k) -> p c k", c=64)
        kb.dma_in(None, B("natsh"), [(natsh, kb.c_natsh[:, :].rearrange("p (c k) -> p c k", c=64))])
        for half2 in range(2):
            st, stb = kb.stageF(kb.rot("stage", 2))
            P.add("dve", lambda e, st=st: e.memset(st[:, 0:128], 0.0), writes=[stb])
            kb.dma_in(None, stb, [(st[0:120, 32:63], kb.nat_rpb[half2 * 120:(half2 + 1) * 120, :])], eng="sp")
            ps, psb = kb.bank()
            P.add("pe", lambda e, ps=ps, st=st: e.transpose(out=ps[:, 0:128], in_=st[:, 0:128], identity=kb.identF[:]), reads=[stb, B("identF")], writes=[psb])
            kb.copy("dve", rext[:, half2 * 120:(half2 + 1) * 120], ps[:, 0:120], [psb], [B("rext")])
        rv = rext[:, :].rearrange("p (h d) -> p h d", h=16)
        import os
        dbgm = os.environ.get("NATDBG", "")
        if "a" in dbgm:
            P.add("dve", lambda e: e.memset(self.BB, 0.0), writes=[B("BB")])
        for c in range(64 if "a" not in dbgm else 0):
            ps, psb = kb.bank()
            pv = ps[:, 0:224].rearrange("p (h t) -> p h t", h=16)
            for half in range(2):
                P.add("pe", lambda e, pv=pv, c=c, half=half: e.matmul(pv[half * 64:(half + 1) * 64, :, :], lhsT=natsh[:, c, :], rhs=rv[:, :, half:half + 14], start=True, stop=True),
                      reads=[B("natsh"), B("rext")], writes=[psb])
            P.add("act", lambda e, pv=pv, c=c: e.activation(out=self.BB[:, :, :, c], in_=pv, func=AF.Identity, scale=8.0, bias=kb.natmask[:, c:c + 1]), reads=[psb, B("natmask")], writes=[B("BB")])
        P.fence()
        self.off = off_save
        self.gqa(kb.nat_w_qkv, 16, 16, 64, 64 ** -0.5, 1024, 2048, 3, local=("nat" if "b" not in dbgm else None))

    def mix_b(self):
        kb, l = self.kb, self.l
        P, B = kb.P, kb.B
        scale = 96 ** -0.5
        win = kb.mla_w_in.rearrange("(k p) n -> p k n", p=128)
        wuq = kb.mla_w_uq.rearrange("(k p) n -> p k n", p=128)
        wukv = kb.mla_w_ukv.rearrange("(k p) (h e) -> p k h e", p=128, e=128)
        ckv_c, kpe_c = kb.cache[l]
        so_ckv, so_kpe = kb.so[l]
        cqn = self.alloc(3 * 1536, "cqn").rearrange("p (c t) -> p c t", c=3)
        ckvn = self.alloc(2 * 1536, "ckvn").rearrange("p (c t) -> p c t", c=2)
        kpe = self.alloc(1536, "kpe")
        cckv = self.alloc(2 * 512, "cckv").rearrange("p (c t) -> p c t", c=2)
        ckpe = self.alloc(512, "ckpe")
        P.add("dve", lambda e: e.memset(kpe[:, :], 0.0), writes=[B("kpe_0"), B("kpe_1"), B("kpe_2")])
        P.add("dve", lambda e: e.memset(ckpe[:, :], 0.0), writes=[B("ckpe")])
        slot, sbuf = kb.load_w(lambda s: [(s[:, 0:8 * 384].rearrange("p (k n) -> p k n", k=8), win[:, :, 0:384])])
        sv = slot[:, 0:8 * 384].rearrange("p (k n) -> p k n", k=8)
        for g in range(3):
            chunks = []
            for c in range(3):
                pss = self.proj_fm(lambda k, c=c: (sv[:, k, c * 128:(c + 1) * 128], sbuf), 8, self.hT_rhs, 128, groups=(g,))
                chunks.append(pss[g])
            rs, rsb = kb.colsum_rstd([(ps[:], psb) for (ps, psb) in chunks], 384.0)
            for c, (ps, psb) in enumerate(chunks):
                P.add("dve", lambda e, ps=ps, c=c, g=g, rs=rs: e.scalar_tensor_tensor(out=cqn[:, c, kb.gcols(g)], in0=ps[:], scalar=kb.miscT[:, 18 + c:19 + c], in1=rs[:], op0=ALU.mult, op1=ALU.mult),
                      reads=[psb, rsb, B("miscT")], writes=[B("cqn%d_%d" % (c, g))])
        slot, sbuf = kb.load_w(lambda s: [(s[:, 0:8 * 256].rearrange("p (k n) -> p k n", k=8), win[:, :, 384:640]), (s[:, 2048:2048 + 8 * 32].rearrange("p (k n) -> p k n", k=8), win[:, :, 640:672])])
        sv = slot[:, 0:8 * 256].rearrange("p (k n) -> p k n", k=8)
        svp = slot[:, 2048:2048 + 8 * 32].rearrange("p (k n) -> p k n", k=8)
        for g in range(3):
            chunks = []
            for c in range(2):
                pss = self.proj_fm(lambda k, c=c: (sv[:, k, c * 128:(c + 1) * 128], sbuf), 8, self.hT_rhs, 128, groups=(g,))
                chunks.append(pss[g])
            rs, rsb = kb.colsum_rstd([(ps[:], psb) for (ps, psb) in chunks], 256.0)
            for c, (ps, psb) in enumerate(chunks):
                t, tb = kb.tmp()
                P.add("dve", lambda e, t=t, ps=ps, c=c, rs=rs: e.scalar_tensor_tensor(out=t[:], in0=ps[:], scalar=kb.miscT[:, 21 + c:22 + c], in1=rs[:], op0=ALU.mult, op1=ALU.mult),
                      reads=[psb, rsb, B("miscT")], writes=[tb])
                kb.copy("act", ckvn[:, c, kb.gcols(g)], t[:], [tb], [B("ckvn%d_%d" % (c, g))])
                if g == 0:
                    self.state_out_fm(t[:], tb, 128, so_ckv[:, c * 128:(c + 1) * 128])
        pss = self.proj_fm(lambda k: (svp[:, k, :], sbuf), 8, self.hT_rhs, 32)
        for g, (ps, psb) in pss.items():
            if g == 0:
                t, tb = kb.tmp()
                kb.copy("dve", t[0:32, :], ps[0:32, :], [psb], [tb])
                self.state_out_fm(t[0:32, :], tb, 32, so_kpe[:, 0:32])
                kb.copy("act", kpe[0:32, kb.gcols(0)], t[0:32, :], [tb], [B("kpe_0")])
            else:
                kb.copy("act", kpe[0:32, kb.gcols(g)], ps[0:32, :], [psb], [B("kpe_%d" % g)])
        for c in range(2):
            self.load_ctx_T(ckv_c, c * 128, 128, cckv[:, c, :], B("cckv%d" % c))
        self.load_ctx_T(kpe_c, 0, 32, ckpe[0:32, :], B("ckpe"))
        Qs = [self.alloc(1536, "Qop") for _ in range(2)]
        Ks = [self.alloc(1536, "Kop") for _ in range(2)]
        cKs = [self.alloc(512, "cKop") for _ in range(2)]
        vts = [self.alloc(12 * 128, "vt").rearrange("p (t f) -> p t f", t=12) for _ in range(2)]
        cvs = [self.alloc(4 * 128, "cv").rearrange("p (t f) -> p t f", t=4) for _ in range(2)]
        wk = [self.alloc(2 * 96, "wk").rearrange("p (k n) -> p k n", k=2) for _ in range(2)]
        for i in range(2):
            P.add("dve", lambda e, i=i: e.memset(wk[i][:, :, :], 0.0), writes=[B("wk%d" % i)])
        tiles = [t * 128 for t in range(12)]
        pending = []
        for h in range(16):
            hi_ = h % 2
            if h % 2 == 0:
                vi = (h // 2) % 2
                slot, sbuf = kb.load_w(lambda s, h=h: [(s[:, 0:256].rearrange("p (k a e) -> p k a e", k=2, a=2)[:, :, a, :], wukv[:, :, h + a, 64:128]) for a in range(2)])
                self.v_tok(vts[vi], "vt%d" % vi, tiles, lambda k, a: (ckvn[:, k, a:a + 128], B("ckvn%d_%d" % (k, a // 512))), 2, slot, sbuf, 128)
                self.v_tok(cvs[vi], "cv%d" % vi, [i * 128 for i in range(4)], lambda k, a: (cckv[:, k, a:a + 128], B("cckv%d" % k)), 2, slot, sbuf, 128)
            vt, cvt = vts[vi], cvs[vi]
            vcol = (h % 2) * 64
            kb.dma_in(None, B("wk%d" % hi_), [(wk[hi_][:, :, 0:64], wukv[:, :, h, 0:64])])
            K, cK, Q = Ks[hi_], cKs[hi_], Qs[hi_]
            pss = self.proj_fm(lambda k: (wk[hi_][:, k, :], B("wk%d" % hi_)), 2, lambda k, g: (ckvn[:, k, kb.gcols(g)], B("ckvn%d_%d" % (k, g))), 96,
                               extra=lambda g: [(kb.mlasel[:, :], B("mlasel"), kpe[:, kb.gcols(g)], B("kpe_%d" % g))])
            self.finish_head(pss, 96, K, "Kop%d" % hi_, rope=True)
            pss = self.proj_fm(lambda k: (wk[hi_][:, k, :], B("wk%d" % hi_)), 2, lambda k, g: (cckv[:, k, :], B("cckv%d" % k)), 96, groups=(0,),
                               extra=lambda g: [(kb.mlasel[:, :], B("mlasel"), ckpe[:, :], B("ckpe"))])
            ps, psb = pss[0]
            kb.copy("act", cK[0:96, :], ps[0:96, :], [psb], [B("cKop%d" % hi_)])
            slot, sbuf = kb.load_w(lambda s, h=h: [(s[:, 0:3 * 96].rearrange("p (k n) -> p k n", k=3), wuq[:, :, h * 96:(h + 1) * 96])])
            sv = slot[:, 0:3 * 96].rearrange("p (k n) -> p k n", k=3)
            pss = self.proj_fm(lambda k: (sv[:, k, :], sbuf), 3, lambda k, g: (cqn[:, k, kb.gcols(g)], B("cqn%d_%d" % (k, g))), 96)
            self.finish_head(pss, 96, Q, "Qop%d" % hi_, rope=True)
            def do_attn(Q=Q, K=K, cK=cK, vt=vt, cvt=cvt, vcol=vcol, hi_=hi_, vi=vi, h=h):
                och, prow = h // 2, (h % 2) * 64
                for s in range(2):
                    blocks = []
                    for i in range(2):
                        tl = 2 * s + i
                        blocks.append(dict(k=(K[0:96, tl * 128:(tl + 1) * 128], B("Kop%d_0" % hi_)), v=(vt[:, tl, vcol:vcol + 64], B("vt%d_%d" % (vi, tl))), lo=0, hi=256))
                    self.attn(Q[0:96, s * 256:(s + 1) * 256], B("Qop%d_0" % hi_), 256, blocks, 64, self.oT[prow:prow + 64, och, s * 256:(s + 1) * 256], [B("oT%d_0" % och)], scale, prow=prow)
                for g in (1, 2):
                    blocks = []
                    for i in range(4):
                        blocks.append(dict(k=(cK[0:96, i * 128:(i + 1) * 128], B("cKop%d" % hi_)), v=(cvt[:, i, vcol:vcol + 64], B("cv%d_%d" % (vi, i))), lo=0, hi=512))
                    for i in range(8):
                        blocks.append(dict(k=(K[0:96, 512 + i * 128:512 + (i + 1) * 128], B("Kop%d_%d" % (hi_, 1 + i // 4))), v=(vt[:, 4 + i, vcol:vcol + 64], B("vt%d_%d" % (vi, 4 + i))), lo=0, hi=512))
                    self.attn(Q[0:96, kb.gcols(g)], B("Qop%d_%d" % (hi_, g)), 512, blocks, 64, self.oT[prow:prow + 64, och, kb.gcols(g)], [B("oT%d_%d" % (och, g))], scale, prow=prow)
            pending.append(do_attn)
            self.inter()
            if len(pending) > 1:
                pending.pop(0)()
        while pending:
            pending.pop(0)()

_CACHE = {}


def _get_nc(nl=NL):
    if nl not in _CACHE:
        kb = KB(nl)
        _CACHE[nl] = (kb.build(), kb)
    return _CACHE[nl]


def kernel(x_prompt, x_sample, cache_l0_k, cache_l0_v, cache_l1_ckv, cache_l1_kpe, cache_l2_k, cache_l2_v, cache_l3_k, cache_l3_v,
           c, c_ctx, ada_w, ada_b, norm_g, mlp_w1, mlp_w2, attn_w_qkv, attn_q_norm, attn_k_norm, attn_w_o,
           mla_w_in, mla_q_norm, mla_kv_norm, mla_w_uq, mla_w_ukv, mla_w_o, swa_w_qkv, swa_sink, swa_w_o,
           nat_w_qkv, nat_rpb, nat_w_o, _nl=NL):
    f = lambda a: np.ascontiguousarray(np.asarray(a, dtype=np.float32))
    nc, kb = _get_nc(_nl)
    consts = _consts()
    shared = dict(ada_w=f(ada_w), ada_b=f(ada_b).reshape(192, 128), norm_g=f(norm_g).reshape(128, 128), mlp_w1=f(mlp_w1), mlp_w2=f(mlp_w2),
                  attn_w_qkv=f(attn_w_qkv), attn_w_o=f(attn_w_o), mla_w_in=f(mla_w_in), mla_w_uq=f(mla_w_uq), mla_w_ukv=f(mla_w_ukv), mla_w_o=f(mla_w_o),
                  swa_w_qkv=f(swa_w_qkv), swa_w_o=f(swa_w_o), swa_sink=f(swa_sink).reshape(1, 16), nat_w_qkv=f(nat_w_qkv), nat_w_o=f(nat_w_o),
                  nat_rpb=f(nat_rpb).reshape(240, 31))
    shared.update(consts)
    xp = f(x_prompt); xs = f(x_sample); cc = f(c)
    in_maps = []
    for i in range(8):
        misc = np.zeros((128, 128), np.float32)
        misc[0:8] = cc[i].reshape(8, 128)
        misc[8:16] = f(c_ctx).reshape(8, 128)
        misc[16] = f(attn_q_norm); misc[17] = f(attn_k_norm)
        misc[18:21] = f(mla_q_norm).reshape(3, 128); misc[21:23] = f(mla_kv_norm).reshape(2, 128)
        m = dict(shared)
        m.update(xp=xp[2 * i:2 * i + 2].reshape(512, 1024), xs=xs[i], misc=misc,
                 c0k=f(cache_l0_k)[i].reshape(512, 256), c0v=f(cache_l0_v)[i].reshape(512, 256),
                 c1ckv=f(cache_l1_ckv)[i], c1kpe=f(cache_l1_kpe)[i],
                 c2k=f(cache_l2_k)[i].reshape(512, 256), c2v=f(cache_l2_v)[i].reshape(512, 256),
                 c3k=f(cache_l3_k)[i].reshape(512, 1024), c3v=f(cache_l3_v)[i].reshape(512, 1024))
        in_maps.append(m)
    res = run_bass_kernel_spmd(nc, in_maps, core_ids=list(range(8)))
    r = res.results
    cat = lambda k: np.concatenate([np.asarray(r[i][k], dtype=np.float32) for i in range(8)], axis=0)
    yp = cat("yp").reshape(16, 256, 1024)
    ys = cat("ys").reshape(8, 1024, 1024)
    return (yp, ys,
            cat("o0k").reshape(16, 256, 2, 128), cat("o0v").reshape(16, 256, 2, 128),
            cat("o1ckv").reshape(16, 256, 256), cat("o1kpe").reshape(16, 256, 32),
            cat("o2k").reshape(16, 256, 4, 64), cat("o2v").reshape(16, 256, 4, 64),
            cat("o3k").reshape(16, 256, 16, 64), cat("o3v").reshape(16, 256, 16, 64))
```
